# Optimizing a Trainium2 kernel written in Bass

```python
import jax, jax.numpy as jnp
from jax import lax
import numpy as np

D_MODEL = 1024
BATCH = 2
SEQ = 8192
DEPTH = 1

HEAD_DIM = 64
N_FOX_HEADS = 8
N_SB_HEADS = 8
FOX_WIDTH = N_FOX_HEADS * HEAD_DIM
SB_WIDTH = N_SB_HEADS * HEAD_DIM
MIX_WIDTH = FOX_WIDTH + SB_WIDTH
IN_COLS = 3 * FOX_WIDTH + 3 * SB_WIDTH + N_FOX_HEADS
Q_BLOCK = 128
N_MEM = 256
N_MEM_HEADS = 4
MEM_HEAD_DIM = D_MODEL // N_MEM_HEADS
D_FF = ((8 * D_MODEL // 3 + 255) // 256) * 256
CONV_WIDTH = 3
EPS = 1e-6

kernel_name = "hybrid_fox_stickbreak_memxattn_convffn"


def _rmsnorm(x, g):
    xf = x.astype(jnp.float32)
    y = xf * lax.rsqrt(jnp.mean(xf * xf, axis=-1, keepdims=True) + EPS)
    return (y * g.astype(jnp.float32)).astype(x.dtype)


def _heads(t, n):
    b, s, w = t.shape
    return t.reshape(b, s, n, w // n).transpose(0, 2, 1, 3)


def _merge(t):
    b, n, s, d = t.shape
    return t.transpose(0, 2, 1, 3).reshape(b, s, n * d)


def _sweep(fn, q, *per_query):
    b, h, s, d = q.shape
    n = s // Q_BLOCK
    qb = q.reshape(b, h, n, Q_BLOCK, d).transpose(2, 0, 1, 3, 4)
    extras = [e.reshape(b, h, n, Q_BLOCK).transpose(2, 0, 1, 3) for e in per_query]
    out = lax.map(lambda a: fn(*a), (jnp.arange(n), qb, *extras))
    return out.transpose(1, 2, 0, 3, 4).reshape(b, h, s, d)


def _fox_attention(q, k, v, log_f):
    s_len = q.shape[2]
    scale = HEAD_DIM ** -0.5
    c = jnp.cumsum(log_f.astype(jnp.float32), axis=-1)
    kpos = jnp.arange(s_len)

    def block(i, qi, ci):
        qpos = i * Q_BLOCK + jnp.arange(Q_BLOCK)
        logits = jnp.einsum('bhqd,bhkd->bhqk', qi, k, preferred_element_type=jnp.float32) * scale
        logits = logits + ci[..., None] - c[:, :, None, :]
        logits = jnp.where(kpos[None, :] <= qpos[:, None], logits, -jnp.inf)
        p = jax.nn.softmax(logits, axis=-1)
        return jnp.einsum('bhqk,bhkd->bhqd', p.astype(v.dtype), v)

    return _sweep(block, q, c)


def _stick_breaking_attention(q, k, v):
    s_len = q.shape[2]
    scale = HEAD_DIM ** -0.5
    kpos = jnp.arange(s_len)

    def block(i, qi):
        qpos = i * Q_BLOCK + jnp.arange(Q_BLOCK)
        z = jnp.einsum('bhqd,bhkd->bhqk', qi, k, preferred_element_type=jnp.float32) * scale
        strict = kpos[None, :] < qpos[:, None]
        log_keep = jnp.where(strict, jax.nn.log_sigmoid(-z), 0.0)
        after = lax.cumsum(log_keep, axis=3, reverse=True) - log_keep
        a = jnp.where(strict, jnp.exp(jax.nn.log_sigmoid(z) + after), 0.0)
        return jnp.einsum('bhqk,bhkd->bhqd', a.astype(v.dtype), v)

    return _sweep(block, q)


def _memory_cross_attention(h, m, w_q, w_kv, w_o):
    q = _heads(jnp.einsum('bsd,dc->bsc', h, w_q), N_MEM_HEADS)
    kv = jnp.einsum('bmd,dc->bmc', m, w_kv)
    k, v = jnp.split(kv, 2, axis=-1)
    k, v = _heads(k, N_MEM_HEADS), _heads(v, N_MEM_HEADS)
    logits = jnp.einsum('bhqd,bhkd->bhqk', q, k, preferred_element_type=jnp.float32) * (MEM_HEAD_DIM ** -0.5)
    p = jax.nn.softmax(logits, axis=-1)
    o = jnp.einsum('bhqk,bhkd->bhqd', p.astype(v.dtype), v)
    return jnp.einsum('bsc,cd->bsd', _merge(o), w_o)


def _causal_dwconv(u, w, b):
    s_len = u.shape[1]
    up = jnp.pad(u, ((0, 0), (CONV_WIDTH - 1, 0), (0, 0)))
    y = b
    for i in range(CONV_WIDTH):
        y = y + up[:, i:i + s_len] * w[i]
    return y


def setup_inputs(seed: int = 0) -> dict:
    key = jax.random.key(seed)
    ks = jax.random.split(key, 24)
    f32 = jnp.float32

    def nrm(k, shape, fan_in):
        return jax.random.normal(k, shape, f32) * (fan_in ** -0.5)

    def gain(k, shape):
        return 1.0 + 0.02 * jax.random.normal(k, shape, f32)

    return {
        "x": jax.random.normal(ks[0], (BATCH, SEQ, D_MODEL), f32),
        "mem": jax.random.normal(ks[1], (BATCH, N_MEM, D_MODEL), f32),
        "attn_norm_g": gain(ks[2], (DEPTH, D_MODEL)),
        "w_in": nrm(ks[3], (DEPTH, D_MODEL, IN_COLS), D_MODEL),
        "b_forget": jnp.linspace(1.0, 6.0, N_FOX_HEADS, dtype=f32)[None, :]
                    + 0.1 * jax.random.normal(ks[4], (DEPTH, N_FOX_HEADS), f32),
        "fox_out_g": gain(ks[5], (DEPTH, FOX_WIDTH)),
        "sb_out_g": gain(ks[6], (DEPTH, SB_WIDTH)),
        "w_out": nrm(ks[7], (DEPTH, MIX_WIDTH, D_MODEL), MIX_WIDTH),
        "xattn_norm_g": gain(ks[8], (DEPTH, D_MODEL)),
        "mem_norm_g": gain(ks[9], (DEPTH, D_MODEL)),
        "w_mq": nrm(ks[10], (DEPTH, D_MODEL, D_MODEL), D_MODEL),
        "w_mkv": nrm(ks[11], (DEPTH, D_MODEL, 2 * D_MODEL), D_MODEL),
        "w_mo": nrm(ks[12], (DEPTH, D_MODEL, D_MODEL), D_MODEL),
        "ffn_norm_g": gain(ks[13], (DEPTH, D_MODEL)),
        "w_up": nrm(ks[14], (DEPTH, D_MODEL, 2 * D_FF), D_MODEL),
        "conv_w": nrm(ks[15], (DEPTH, CONV_WIDTH, 2 * D_FF), CONV_WIDTH),
        "conv_b": 0.02 * jax.random.normal(ks[16], (DEPTH, 2 * D_FF), f32),
        "w_down": nrm(ks[17], (DEPTH, D_FF, D_MODEL), D_FF),
        "final_norm_g": gain(ks[18], (D_MODEL,)),
    }


def reference(x, mem, attn_norm_g, w_in, b_forget, fox_out_g, sb_out_g, w_out,
              xattn_norm_g, mem_norm_g, w_mq, w_mkv, w_mo,
              ffn_norm_g, w_up, conv_w, conv_b, w_down, final_norm_g):
    split_at = np.cumsum([FOX_WIDTH, FOX_WIDTH, FOX_WIDTH, SB_WIDTH, SB_WIDTH, SB_WIDTH]).tolist()
    for l in range(DEPTH):
        h = _rmsnorm(x, attn_norm_g[l])
        proj = jnp.einsum('bsd,dc->bsc', h, w_in[l])
        fq, fk, fv, sq, sk, sv, f_logit = jnp.split(proj, split_at, axis=-1)
        log_f = jax.nn.log_sigmoid(f_logit.astype(jnp.float32) + b_forget[l].astype(jnp.float32))
        log_f = log_f.transpose(0, 2, 1)
        fox_o = _merge(_fox_attention(_heads(fq, N_FOX_HEADS), _heads(fk, N_FOX_HEADS),
                                      _heads(fv, N_FOX_HEADS), log_f))
        sb_o = _merge(_stick_breaking_attention(_heads(sq, N_SB_HEADS), _heads(sk, N_SB_HEADS),
                                                _heads(sv, N_SB_HEADS)))
        mixed = jnp.concatenate([_rmsnorm(fox_o, fox_out_g[l]), _rmsnorm(sb_o, sb_out_g[l])], axis=-1)
        x = x + jnp.einsum('bsc,cd->bsd', mixed, w_out[l])

        h = _rmsnorm(x, xattn_norm_g[l])
        m = _rmsnorm(mem, mem_norm_g[l])
        x = x + _memory_cross_attention(h, m, w_mq[l], w_mkv[l], w_mo[l])

        h = _rmsnorm(x, ffn_norm_g[l])
        u = jnp.einsum('bsd,df->bsf', h, w_up[l])
        u = _causal_dwconv(u, conv_w[l], conv_b[l])
        gate, val = jnp.split(u, 2, axis=-1)
        x = x + jnp.einsum('bsf,fd->bsd', jax.nn.silu(gate) * val, w_down[l])
    return _rmsnorm(x, final_norm_g)
```

```python
import numpy as np
from contextlib import ExitStack
import concourse.bass as bass
import concourse.mybir as mybir
from concourse.bass_utils import run_bass_kernel_spmd

F32 = mybir.dt.float32
BF16 = mybir.dt.bfloat16
AF = mybir.ActivationFunctionType
ALU = mybir.AluOpType

S = 8192
NOWN = 2080
EPS = 1e-6
NEG = -30000.0


class Buf:
    __slots__ = ("w", "r")

    def __init__(self):
        self.w = None
        self.r = {}


class Trk:
    EPOCH = 30000

    def __init__(self, nc, es):
        self.nc = nc
        self.es = es
        self.eng = {"pe": nc.tensor, "act": nc.scalar, "dve": nc.vector, "pool": nc.gpsimd, "sp": nc.sync}
        self.sems = {e: [] for e in ("pe", "act", "dve", "pool")}
        self.cnt = {e: 0 for e in ("pe", "act", "dve", "pool")}
        self.pending = {e: False for e in ("pe", "act", "dve", "pool")}
        self.dsem = {}
        self.dcnt = {}
        self.know = {e: {} for e in self.eng}
        self.snap = {}
        self.bufs = {}
        self.nwait = 0
        self.nins = 0

    def _newsem(self, name):
        return self.es.enter_context(self.nc.semaphore(name))

    def _next_tok(self, e):
        c = self.cnt[e]
        ep, v = c // self.EPOCH, c % self.EPOCH + 1
        while len(self.sems[e]) <= ep:
            self.sems[e].append(self._newsem(f"s_{e}{len(self.sems[e])}"))
        return (e, ep, v)

    def _sem_of(self, tok):
        p, ep, v = tok
        if p.startswith("d:"):
            return self.dsem[p]
        return self.sems[p][ep]

    def _need(self, e, tok):
        return self.know[e].get(tok[0], (-1, 0)) < (tok[1], tok[2])

    def _learn(self, e, tok):
        k = self.know[e]
        sn = self.snap.get(tok)
        if sn:
            for p, val in sn.items():
                if k.get(p, (-1, 0)) < val:
                    k[p] = val
        if k.get(tok[0], (-1, 0)) < (tok[1], tok[2]):
            k[tok[0]] = (tok[1], tok[2])

    def _collect(self, e, reads, writes):
        cand = []
        for k in reads:
            b = self.bufs.get(k)
            if b is not None and b.w is not None:
                cand.append(b.w)
        for k in writes:
            b = self.bufs.get(k)
            if b is None:
                continue
            if b.w is not None and (b.w[0] != e or e != "pe"):
                cand.append(b.w)
            for p, t in b.r.items():
                if p != e or e != "pe":
                    cand.append(t)
        cand.sort(key=lambda t: (t[1], t[2]), reverse=True)
        needed = []
        for tok in cand:
            if self._need(e, tok):
                needed.append(tok)
                self._learn(e, tok)
        return needed

    def _emit_waits(self, e, needed, ins_fn):
        for tok in needed[:-1]:
            self.eng[e].wait_ge(self._sem_of(tok), tok[2])
            self.nwait += 1
        ins = ins_fn()
        if needed:
            tok = needed[-1]
            ins._wait_ge(self._sem_of(tok), tok[2])
        return ins

    def _record(self, tok, reads, writes):
        for k in reads:
            self.bufs.setdefault(k, Buf()).r[tok[0]] = tok
        for k in writes:
            b = self.bufs.setdefault(k, Buf())
            b.w = tok
            b.r = {}

    def op(self, e, fn, reads=(), writes=(), inc=True):
        needed = self._collect(e, reads, writes)
        ins = self._emit_waits(e, needed, fn)
        tok = self._next_tok(e)
        if inc:
            ins.then_inc(self.sems[e][tok[1]], 1)
            self.cnt[e] += 1
            self.pending[e] = False
            sn = dict(self.know[e])
            sn[e] = (tok[1], tok[2])
            self.snap[tok] = sn
        else:
            self.pending[e] = True
        self._record(tok, reads, writes)
        self.nins += 1
        return ins

    def dma(self, lane, out, in_, reads=(), writes=()):
        lane = "d:" + lane
        if lane not in self.dsem:
            self.dsem[lane] = self._newsem("s_" + lane.replace(":", "_"))
            self.dcnt[lane] = 0
        needed = self._collect("sp", reads, writes)
        ins = self._emit_waits("sp", needed, lambda: self.nc.sync.dma_start(out=out, in_=in_))
        self.dcnt[lane] += 1
        tok = (lane, 0, 16 * self.dcnt[lane])
        ins.then_inc(self.dsem[lane], 16)
        sn = dict(self.know["sp"])
        sn[lane] = (0, tok[2])
        self.snap[tok] = sn
        self._record(tok, reads, writes)
        self.nins += 1

    def barrier(self):
        for e in self.eng:
            toks = []
            for p in self.cnt:
                assert not self.pending[p]
                if p != e and self.cnt[p] > 0:
                    c = self.cnt[p] - 1
                    toks.append((p, c // self.EPOCH, c % self.EPOCH + 1))
            for lane, n in self.dcnt.items():
                if n > 0:
                    toks.append((lane, 0, 16 * n))
            for tok in toks:
                if self._need(e, tok):
                    self.eng[e].wait_ge(self._sem_of(tok), tok[2])
                    self.nwait += 1
                    self._learn(e, tok)
        self.bufs = {}
        self.snap = {}


def build_program():
    nc = bass.Bass("TRN2", target_bir_lowering=False)

    def din(name, shape):
        return nc.dram_tensor(name, list(shape), F32, kind="ExternalInput").ap()

    xfT = din("xfT", [128, 8, S])
    xoT = din("xoT", [128, 8, NOWN])
    memT = din("memT", [128, 8, 256])
    wqkv = din("wqkv", [4, 128, 8, 768])
    wf = din("wf", [128, 8, 8])
    gall = din("gall", [128, 5, 8])
    bfg = din("bfg", [128, 8])
    gout = din("gout", [128, 8])
    wout = din("wout", [128, 8, 1024])
    wmq = din("wmq", [128, 8, 1024])
    wmkv = din("wmkv", [128, 8, 2048])
    wmo = din("wmo", [128, 8, 1024])
    wup = din("wup", [128, 8, 5632])
    convw = din("convw", [128, 44, 3])
    convb = din("convb", [128, 44])
    wdown = din("wdown", [128, 22, 1024])
    masks = din("masks", [128, 2, 4, 132])
    esel = din("esel", [128, 4, 128])
    selc = din("selc", [12, 4, 128])
    cmat = din("cmat", [128, 3, 128])
    hvalid = din("hvalid", [128, 32])
    outT = nc.dram_tensor("outT", [128, 8, 2048], F32, kind="ExternalOutput").ap()

    with ExitStack() as es:
        trk = Trk(nc, es)

        uid = [0]

        def sbt(st, name, shape, dt):
            uid[0] += 1
            return st.enter_context(nc.sbuf_tensor(f"{name}_{uid[0]}", list(shape), dt))

        PALL = es.enter_context(nc.psum_tensor("pall", [128, 8, 512], F32))
        PB = [PALL[:, i, :] for i in range(8)]

        def P(i):
            return ("ps", i)

        def mm(out, lhsT, rhs, start, stop, reads, writes, inc=True, skip=False, tpos=None):
            if tpos is None:
                trk.op("pe", lambda: nc.tensor.matmul(out, lhsT, rhs, start=start, stop=stop,
                                                      skip_group_check=skip), reads, writes, inc=inc)
            else:
                trk.op("pe", lambda: nc.tensor.matmul(out, lhsT, rhs, start=start, stop=stop,
                                                      skip_group_check=skip, tile_position=tpos),
                       reads, writes, inc=inc)

        def act(out, in_, func, reads, writes, bias=None, scale=None):
            kw = {}
            if bias is not None:
                kw["bias"] = bias
            if scale is not None:
                kw["scale"] = scale
            trk.op("act", lambda: nc.scalar.activation(out=out, in_=in_, func=func, **kw), reads, writes)

        def veng(e):
            return nc.vector if e == "dve" else nc.gpsimd

        def tt(e, out, in0, in1, op, reads, writes):
            trk.op(e, lambda: veng(e).tensor_tensor(out=out, in0=in0, in1=in1, op=op), reads, writes)

        def ts(e, out, in0, s1, s2, op0, op1, reads, writes):
            if s2 is None:
                trk.op(e, lambda: veng(e).tensor_scalar(out=out, in0=in0, scalar1=s1, scalar2=None, op0=op0),
                       reads, writes)
            else:
                trk.op(e, lambda: veng(e).tensor_scalar(out=out, in0=in0, scalar1=s1, scalar2=s2, op0=op0,
                                                        op1=op1), reads, writes)

        def stt(e, out, in0, scalar, in1, op0, op1, reads, writes):
            trk.op(e, lambda: veng(e).scalar_tensor_tensor(out=out, in0=in0, scalar=scalar, in1=in1,
                                                           op0=op0, op1=op1), reads, writes)

        def cp(e, out, in_, reads, writes):
            if e == "act":
                act(out, in_, AF.Copy, reads, writes)
            else:
                trk.op(e, lambda: veng(e).tensor_copy(out=out, in_=in_), reads, writes)

        def mset(e, ap, val, writes):
            trk.op(e, lambda: veng(e).memset(ap, val), (), writes)

        OT = sbt(es, "OT", [128, 8, NOWN], BF16)
        IDB = sbt(es, "IDB", [128, 128], BF16)
        NEGU = sbt(es, "NEGU", [128, 128], BF16)
        UF = sbt(es, "UF", [128, 128], F32)
        ONEB = sbt(es, "ONEB", [128, 128], BF16)
        NONEB = sbt(es, "NONEB", [128, 128], BF16)
        ONEF = sbt(es, "ONEF", [128, 128], F32)
        MSK = sbt(es, "MSK", [128, 2, 4, 132], BF16)
        ESEL = sbt(es, "ESEL", [128, 4, 128], BF16)
        SELB = sbt(es, "SELB", [12, 4, 128], BF16)
        GALL = sbt(es, "GALL", [128, 5, 8], F32)
        GOUT = sbt(es, "GOUT", [128, 8], F32)
        BFG = sbt(es, "BFG", [128, 8], F32)
        HVAL = sbt(es, "HVAL", [128, 32], F32)
        EPST = sbt(es, "EPST", [128, 1], F32)
        G8 = sbt(es, "G8", [128, 8], F32)
        G16 = sbt(es, "G16", [128, 8], F32)
        CW = sbt(es, "CW", [128, 44, 3], F32)
        CB = sbt(es, "CB", [128, 44], F32)

        with ExitStack() as ph:
            STG = sbt(ph, "STG0", [128, 2, 4, 132], F32)
            CM = sbt(ph, "CM0", [128, 3, 128], F32)
            ES0 = sbt(ph, "ES0", [128, 4, 128], F32)
            SL0 = sbt(ph, "SL0", [12, 4, 128], F32)
            trk.dma("c0", STG[:], masks[:, :, :, :], (), ["STG"])
            trk.dma("c1", CM[:], cmat[:, :, :], (), ["CM"])
            trk.dma("c2", ES0[:], esel[:, :, :], (), ["ES0"])
            trk.dma("c3", SL0[:], selc[:, :, :], (), ["SL0"])
            trk.dma("c4", GALL[:], gall[:, :, :], (), ["GALL"])
            trk.dma("c5", GOUT[:], gout[:, :], (), ["GOUT"])
            trk.dma("c6", BFG[:], bfg[:, :], (), ["BFG"])
            trk.dma("c7", HVAL[:], hvalid[:, :], (), ["HVAL"])
            trk.dma("c8", CW[:], convw[:, :, :], (), ["CW"])
            trk.dma("c9", CB[:], convb[:, :], (), ["CB"])
            cp("dve", MSK[:], STG[:], ["STG"], ["MSK"])
            cp("dve", IDB[:], CM[:, 0, :], ["CM"], ["IDB"])
            cp("dve", NEGU[:], CM[:, 1, :], ["CM"], ["NEGU"])
            cp("dve", UF[:], CM[:, 2, :], ["CM"], ["UF"])
            cp("dve", ESEL[:], ES0[:], ["ES0"], ["ESEL"])
            cp("dve", SELB[:], SL0[:], ["SL0"], ["SELB"])
            mset("pool", ONEB[:], 1.0, ["ONEB"])
            mset("pool", NONEB[:], -1.0, ["NONEB"])
            mset("pool", ONEF[:], 1.0, ["ONEF"])
            mset("pool", EPST[:], EPS, ["EPST"])
            ts("dve", G8[:], GALL[:, 0, :], 0.125, None, ALU.mult, None, ["GALL"], ["G8"])
            ts("dve", G16[:], GALL[:, 1, :], 1.0 / 16, None, ALU.mult, None, ["GALL"], ["G16"])

            trk.barrier()

        def mem_prep(ph, KMT, VM):
            MS = sbt(ph, "MS", [128, 8, 256], F32)
            MBm = sbt(ph, "MBm", [128, 8, 256], BF16)
            MSQ = sbt(ph, "MSQ", [128, 8, 256], BF16)
            RMB = sbt(ph, "RMB", [128, 256], F32)
            RMT = sbt(ph, "RMT", [128, 2], F32)
            TMPm = sbt(ph, "TMPm", [128, 256], F32)
            WKVs = [sbt(ph, f"WKVs{i}", [128, 2048], F32) for i in range(2)]
            WKV = sbt(ph, "WKV", [128, 8, 2048], BF16)
            trk.dma("ms", MS[:], memT[:, :, :], (), ["MS"])
            cp("act", MBm[:], MS[:], ["MS"], ["MBm"])
            tt("dve", MSQ[:], MS[:], MS[:], ALU.mult, ["MS"], ["MSQ"])
            for fc in range(8):
                s = fc % 2
                trk.dma(f"wkvs{s}", WKVs[s][:], wmkv[:, fc, :], (), [f"WKVs{s}"])
                act(WKV[:, fc, :], WKVs[s][:], AF.Copy, [f"WKVs{s}", "GALL"], ["WKV"], scale=GALL[:, 2, fc:fc + 1])
            for fc in range(8):
                mm(PB[6][:, 0:256], ONEB[:], MSQ[:, fc, :], fc == 0, fc == 7, ["ONEB", "MSQ"], [P(6)], inc=(fc == 7))
            act(TMPm[:], PB[6][:, 0:256], AF.Ln, [P(6), "EPST"], ["TMPm"], bias=EPST[:], scale=1.0 / 1024)
            act(RMB[:], TMPm[:], AF.Exp, ["TMPm"], ["RMB"], scale=-0.5)
            for mc in range(2):
                for fc in range(8):
                    mm(PB[7][:, mc:mc + 1], MSQ[:, fc, 128 * mc:128 * mc + 128], ONEB[:, 0:1], fc == 0, fc == 7,
                       ["MSQ", "ONEB"], [P(7)], inc=(fc == 7))
            act(TMPm[:, 0:2], PB[7][:, 0:2], AF.Ln, [P(7), "EPST"], ["TMPm"], bias=EPST[:], scale=1.0 / 1024)
            act(RMT[:], TMPm[:, 0:2], AF.Exp, ["TMPm"], ["RMT"], scale=-0.5)
            for cc in range(8):
                h, dc = cc // 2, cc % 2
                bk = cc % 2
                for fc in range(8):
                    mm(PB[bk][:, 0:256], WKV[:, fc, 128 * cc:128 * cc + 128], MBm[:, fc, :], fc == 0, fc == 7,
                       ["WKV", "MBm"], [P(bk)], inc=(fc == 7))
                tt("dve", KMT[:, h, dc, :], PB[bk][:, 0:256], RMB[:], ALU.mult, [P(bk), "RMB"], ["KMT"])
            for mc in range(2):
                for hf in range(2):
                    bk = 2 + (2 * mc + hf) % 2
                    for fc in range(8):
                        mm(PB[bk][:, :], MBm[:, fc, 128 * mc:128 * mc + 128],
                           WKV[:, fc, 1024 + 512 * hf:1024 + 512 * hf + 512], fc == 0, fc == 7,
                           ["MBm", "WKV"], [P(bk)], inc=(fc == 7))
                    ts("dve", VM[:, mc, 512 * hf:512 * hf + 512], PB[bk][:, :], RMT[:, mc:mc + 1], None, ALU.mult,
                       None, [P(bk), "RMT"], ["VM"])

        def attention_pass(p):
            fox = p < 2
            with ExitStack() as ph:
                KT = [sbt(ph, f"KT{j}", [128, S], BF16) for j in range(2)]
                VP = sbt(ph, "VP", [128, 64, 4, 65], BF16)
                QT = [sbt(ph, f"QT{j}", [128, NOWN], BF16) for j in range(2)]
                WQ = sbt(ph, "WQ", [128, 8, 256], BF16)
                WK = sbt(ph, "WK", [128, 8, 256], BF16)
                WV = sbt(ph, "WV", [128, 8, 256], BF16)
                WF = sbt(ph, "WF", [128, 8, 8], BF16)
                WFs = sbt(ph, "WFs", [128, 8, 8], F32)
                QAUG = sbt(ph, "QAUG", [12, NOWN], BF16) if fox else None
                XS = [sbt(ph, f"XS{i}", [128, 8, 260], F32) for i in range(2)]
                XB = [sbt(ph, f"XB{i}", [128, 8, 260], BF16) for i in range(2)]
                SQ = [sbt(ph, f"SQ{i}", [128, 8, 260], BF16) for i in range(2)]
                RBC = [sbt(ph, f"RBC{i}", [128, 260], F32) for i in range(2)]
                RTK = [sbt(ph, f"RTK{i}", [128, 2], F32) for i in range(2)]
                TMPRs = [sbt(ph, f"TMPR{i}", [128, 260], F32) for i in range(2)]
                TMPTs = [sbt(ph, f"TMPT{i}", [128, 2], F32) for i in range(2)]
                WST = [sbt(ph, f"WST{i}", [128, 768], F32) for i in range(2)]
                ZF = sbt(ph, "ZF", [128, 8], F32)
                EF = sbt(ph, "EF", [128, 8], F32)
                SPF = [sbt(ph, f"SPF{i}", [128, 8], F32) for i in range(4)]
                ACCF = sbt(ph, "ACCF", [128, 8], F32)
                if fox:
                    ET = [sbt(ph, f"ET{i}", [128, 512], F32) for i in range(2)]
                    PT = [sbt(ph, f"PT{i}", [128, 512], BF16) for i in range(3)]
                    ACB = [sbt(ph, f"ACB{i}", [128, 512], BF16) for i in range(1)]
                    LR = sbt(ph, "LR", [128, 512], F32)
                    BC = sbt(ph, "BC", [128, 512], F32)
                    AUGT = [sbt(ph, f"AUGT{i}", [128, 512], F32) for i in range(2)]
                else:
                    ET = SPB = PT = ACB = None
                    ACC = LR = BC = AUGT = None
                if not fox:
                    ETp = [sbt(ph, f"ETp{i}", [128, 2, 512], F32) for i in range(2)]
                    SPBp = [sbt(ph, f"SPBp{i}", [128, 2, 512], BF16) for i in range(2)]
                    PTp = [sbt(ph, f"PTp{i}", [128, 2, 512], BF16) for i in range(3)]
                    ACCp = sbt(ph, "ACCp", [128, 2, 512], F32)
                    ACBp = [sbt(ph, f"ACBp{i}", [128, 2, 512], BF16) for i in range(2)]

                for fc in range(8):
                    s = fc % 2
                    trk.dma(f"wst{s}", WST[s][:], wqkv[p, :, fc, :], (), [f"WST{s}"])
                    act(WQ[:, fc, :], WST[s][:, 0:256], AF.Copy, [f"WST{s}", "G8"], ["WQ"], scale=G8[:, fc:fc + 1])
                    act(WK[:, fc, :], WST[s][:, 256:512], AF.Copy, [f"WST{s}", "GALL"], ["WK"],
                        scale=GALL[:, 0, fc:fc + 1])
                    act(WV[:, fc, :], WST[s][:, 512:768], AF.Copy, [f"WST{s}", "GALL"], ["WV"],
                        scale=GALL[:, 0, fc:fc + 1])
                if p == 0:
                    trk.dma("wfs", WFs[:], wf[:, :, :], (), ["WFs"])
                    for fc in range(8):
                        ts("pool", WF[:, fc, :], WFs[:, fc, :], GALL[:, 0, fc:fc + 1], None, ALU.mult, None,
                           ["WFs", "GALL"], ["WF"])
                    mset("pool", ACCF[:], 0.0, ["ACCF"])
                mset("pool", VP[:, :, :, 64:65], 1.0, ["VP"])

                def prefetch(src, c0, n, s):
                    trk.dma(f"xs{s}", XS[s][:, :, 0:n], src[:, :, c0:c0 + n], (), [f"XS{s}"])
                    cp("act", XB[s][:, :, 0:n], XS[s][:, :, 0:n], [f"XS{s}"], [f"XB{s}"])
                    tt("dve", SQ[s][:, :, 0:n], XS[s][:, :, 0:n], XS[s][:, :, 0:n], ALU.mult, [f"XS{s}"], [f"SQ{s}"])

                def load_chunk(src, c0, n, s):
                    for fc in range(8):
                        mm(PB[6][:, 0:n], ONEB[:], SQ[s][:, fc, 0:n], fc == 0, fc == 7, ["ONEB", f"SQ{s}"], [P(6)],
                           inc=(fc == 7))
                    TMPR = TMPRs[s]
                    act(TMPR[:, 0:n], PB[6][:, 0:n], AF.Ln, [P(6), "EPST"], [f"TMPR{s}"], bias=EPST[:], scale=1.0 / 1024)
                    act(RBC[s][:, 0:n], TMPR[:, 0:n], AF.Exp, [f"TMPR{s}"], [f"RBC{s}"], scale=-0.5)

                stream = [(xoT, 260 * c_, 260) for c_ in range(8)] + [(xfT, 256 * c_, 256) for c_ in range(32)]
                prefetch(stream[0][0], stream[0][1], stream[0][2], 0)

                def prefetch_next(i_):
                    if i_ + 1 < len(stream):
                        prefetch(stream[i_ + 1][0], stream[i_ + 1][1], stream[i_ + 1][2], (i_ + 1) % 2)

                for c in range(8):
                    s = c % 2
                    prefetch_next(c)
                    load_chunk(xoT, 260 * c, 260, s)
                    for j in range(2):
                        bk = j
                        for fc in range(8):
                            mm(PB[bk][:, 0:260], WQ[:, fc, 128 * j:128 * j + 128], XB[s][:, fc, 0:260], fc == 0,
                               fc == 7, ["WQ", f"XB{s}"], [P(bk)], inc=(fc == 7))
                        tt("dve", QT[j][:, 260 * c:260 * c + 260], PB[bk][:, 0:260], RBC[s][:, 0:260], ALU.mult,
                           [P(bk), f"RBC{s}"], [f"QT{j}"])

                def fl_part(ci, s):
                    for blk in range(2):
                        kb = 2 * ci + blk
                        sp_ = kb % 4
                        for fc in range(8):
                            mm(PB[7][:, 0:8], XB[s][:, fc, 128 * blk:128 * blk + 128], WF[:, fc, :], fc == 0,
                               fc == 7, [f"XB{s}", "WF"], [P(7)], inc=(fc == 7))
                        stt("dve", ZF[:], PB[7][:, 0:8], RTK[s][:, blk:blk + 1], BFG[:], ALU.mult, ALU.add,
                            [P(7), f"RTK{s}", "BFG"], ["ZF"])
                        act(EF[:], ZF[:], AF.Exp, ["ZF"], ["EF"], scale=-1.0)
                        act(SPF[sp_][:], EF[:], AF.Ln, ["EF"], [f"SPF{sp_}"], bias=1.0)

                def cs_part(ci):
                    for blk in range(2):
                        kb = 2 * ci + blk
                        sp_ = kb % 4
                        mm(PB[7][:, 8:16], UF[:], SPF[sp_][:], True, False, ["UF", f"SPF{sp_}"], [P(7)], inc=False)
                        mm(PB[7][:, 8:16], ONEF[:], ACCF[:], False, True, ["ONEF", "ACCF"], [P(7)])
                        cp("dve", CNEG[:, kb, :], PB[7][:, 8:16], [P(7)], ["CNEG"])
                        tt("dve", ACCF[:], ACCF[:], SPF[sp_][:], ALU.add, ["ACCF", f"SPF{sp_}"], ["ACCF"])

                for ci in range(32):
                    s = ci % 2
                    prefetch_next(8 + ci)
                    load_chunk(xfT, 256 * ci, 256, s)
                    for blk in range(2):
                        for fc in range(8):
                            mm(PB[7][:, 16 + blk:17 + blk], SQ[s][:, fc, 128 * blk:128 * blk + 128], ONEB[:, 0:1],
                               fc == 0, fc == 7, [f"SQ{s}", "ONEB"], [P(7)], inc=(fc == 7))
                    TMPT = TMPTs[s]
                    act(TMPT[:], PB[7][:, 16:18], AF.Ln, [P(7), "EPST"], [f"TMPT{s}"], bias=EPST[:], scale=1.0 / 1024)
                    act(RTK[s][:], TMPT[:], AF.Exp, [f"TMPT{s}"], [f"RTK{s}"], scale=-0.5)
                    for j in range(2):
                        bk = j
                        for fc in range(8):
                            mm(PB[bk][:, 0:256], WK[:, fc, 128 * j:128 * j + 128], XB[s][:, fc, 0:256], fc == 0,
                               fc == 7, ["WK", f"XB{s}"], [P(bk)], inc=(fc == 7))
                        tt("dve", KT[j][:, 256 * ci:256 * ci + 256], PB[bk][:, 0:256], RBC[s][:, 0:256], ALU.mult,
                           [P(bk), f"RBC{s}"], [f"KT{j}"])
                    for blk in range(2):
                        kb = 2 * ci + blk
                        bk = 2 + blk
                        for fc in range(8):
                            mm(PB[bk][:, 0:256], XB[s][:, fc, 128 * blk:128 * blk + 128], WV[:, fc, :], fc == 0,
                               fc == 7, [f"XB{s}", "WV"], [P(bk)], inc=(fc == 7))
                        ts("dve", VP[:, kb, :, 0:64], PB[bk][:, 0:256].rearrange("p (h d) -> p h d", h=4),
                           RTK[s][:, blk:blk + 1], None, ALU.mult, None, [P(bk), f"RTK{s}"], ["VP"])
                    if p == 0:
                        fl_part(ci, s)
                        if ci >= 1:
                            cs_part(ci - 1)

                if p == 0:
                    cs_part(31)

                if p == 0:
                    for hf in range(2):
                        src = CNEG[:, 32 * hf:32 * hf + 32, :]
                        d0 = SPL3[:, 32 * hf:32 * hf + 32, :, 0]
                        d1 = SPL3[:, 32 * hf:32 * hf + 32, :, 1]
                        d2 = SPL3[:, 32 * hf:32 * hf + 32, :, 2]
                        ta = ET[0][:, 0:256].rearrange("p (k h) -> p k h", h=8)
                        tb = ET[1][:, 0:256].rearrange("p (k h) -> p k h", h=8)
                        cp("dve", d0, src, ["CNEG"], ["SPL3"])
                        tt("dve", ta, src, d0, ALU.subtract, ["CNEG", "SPL3"], ["ET0"])
                        cp("dve", d1, ta, ["ET0"], ["SPL3"])
                        tt("dve", tb, ta, d1, ALU.subtract, ["ET0", "SPL3"], ["ET1"])
                        cp("dve", d2, tb, ["ET1"], ["SPL3"])
                if fox:
                    for g4 in range(4):
                        for ib in range(4):
                            i = 4 * g4 + ib
                            srcs = [(a, 4 * i + a - 1) for a in range(4) if 4 * i + a - 1 >= 0]
                            for n_, (a, kbs) in enumerate(srcs):
                                mm(PB[6][0:12, 128 * ib:128 * ib + 128],
                                   SPL3[:, kbs, 4 * p:4 * p + 4, :].rearrange("p h j -> p (h j)"), ESEL[:, a, :],
                                   n_ == 0, n_ == len(srcs) - 1, ["SPL3", "ESEL"], [P(6)],
                                   inc=(n_ == len(srcs) - 1), skip=True)
                        cp("dve", QAUG[:, 512 * g4:512 * g4 + 512], PB[6][0:12, :], [P(6)], ["QAUG"])
                        cp("dve", QAUG[:, 2048 + 8 * g4:2048 + 8 * g4 + 8].rearrange("p (b c) -> p b c", c=2),
                           PB[6][0:12, :].rearrange("p (b c) -> p b c", c=128)[:, :, 0:2], [P(6)], ["QAUG"])

                mk = 0 if fox else 1
                tiles = []
                for hl in range(4):
                    for g in range(5):
                        halo = g == 4
                        kbs = list(range(64)) if halo else list(range(16 * g + 16))
                        if not fox:
                            kbs = kbs[::-1]
                        for n_, kb in enumerate(kbs):
                            j = kb // 4
                            if halo:
                                c0, n = 2048 + 2 * j, 32 - 2 * j
                                diag = True
                            else:
                                a = max(0, j - 4 * g)
                                c0, n = 512 * g + 128 * a, 512 - 128 * a
                                diag = j >= 4 * g
                            tiles.append(dict(hl=hl, g=g, kb=kb, c0=c0, n=n, diag=diag, first=(n_ == 0),
                                              last=(n_ == len(kbs) - 1), halo=halo,
                                              cbase=(2048 if halo else 512 * g)))
                for t_, tl in enumerate(tiles):
                    tl["idx"] = t_
                chain_no = -1
                for tl in tiles:
                    if tl["first"]:
                        chain_no += 1
                    tl["chain"] = chain_no

                def kq_mm(tl, bank, extra_last):
                    hl, kb, c0, n = tl["hl"], tl["kb"], tl["c0"], tl["n"]
                    j2, r0 = hl // 2, 64 * (hl % 2)
                    o = c0 - tl["cbase"]
                    steps = [(KT[j2][r0:r0 + 64, 128 * kb:128 * kb + 128], QT[j2][r0:r0 + 64, c0:c0 + n],
                              PB[bank][:, o:o + n], [f"KT{j2}", f"QT{j2}"])]
                    if tl["diag"]:
                        if tl["halo"]:
                            w = min(4, n)
                            steps.append((IDB[:], MSK[:, mk, kb % 4, 128:128 + w], PB[bank][:, o:o + w], ["IDB", "MSK"]))
                        else:
                            steps.append((IDB[:], MSK[:, mk, kb % 4, 0:128], PB[bank][:, o:o + 128], ["IDB", "MSK"]))
                    for n_, (l_, r_, o_, rd) in enumerate(steps):
                        lastst = (n_ == len(steps) - 1)
                        mm(o_, l_, r_, (n_ == 0) and not fox, lastst and extra_last, rd, [P(bank)],
                           inc=(lastst and extra_last), skip=True)

                def abank(tl):
                    return tl["idx"] % (4 if fox else 3)

                def st_A(tl):
                    if fox:
                        c2 = tl["chain"] % 2
                        b = abank(tl)
                        if tl["first"]:
                            nfull = 32 if tl["halo"] else 512
                            cb_ = tl["cbase"]
                            mm(PB[6][:, 0:nfull], SELB[:, tl["hl"], :], QAUG[:, cb_:cb_ + nfull], True, True,
                               ["SELB", "QAUG"], [P(6)])
                            cp("dve", AUGT[c2][:, 0:nfull], PB[6][:, 0:nfull], [P(6)], [f"AUGT{c2}"])
                        o = tl["c0"] - tl["cbase"]
                        n = tl["n"]
                        cp("dve", PB[b][:, o:o + n], AUGT[c2][:, o:o + n], [f"AUGT{c2}"], [P(b)])
                    kq_mm(tl, abank(tl), True)

                def st_fox_exp(tl):
                    hl, kb, n = tl["hl"], tl["kb"], tl["n"]
                    o = tl["c0"] - tl["cbase"]
                    b, s3 = abank(tl), tl["idx"] % 3
                    act(PT[s3][:, 0:n], PB[b][:, o:o + n], AF.Exp, [P(b), "CNEG"], [f"PT{s3}"],
                        bias=CNEG[:, kb, 4 * p + hl:4 * p + hl + 1])

                def st_O(tl):
                    hl, kb, n = tl["hl"], tl["kb"], tl["n"]
                    o = tl["c0"] - tl["cbase"]
                    s3 = tl["idx"] % 3
                    ob = 4 + tl["chain"] % 2
                    mm(PB[ob][0:65, o:o + n], VP[:, kb, hl, :], PT[s3][:, 0:n], tl["first"], tl["last"],
                       ["VP", f"PT{s3}"], [P(ob)], skip=True)
                    if tl["last"]:
                        finalize(tl)

                def finalize(tl):
                    hl = tl["hl"]
                    ob = 4 + tl["chain"] % 2
                    n = 32 if tl["halo"] else 512
                    cb = tl["cbase"]
                    gh = 4 * p + hl
                    pair, r0 = gh // 2, 64 * (gh % 2)
                    dst = OT[r0:r0 + 64, pair, cb:cb + n]
                    if fox:
                        ts("dve", LR[64:65, 0:n], PB[ob][64:65, 0:n], 1e-30, None, ALU.max, None, [P(ob)], ["LR"])
                        trk.op("dve", lambda: nc.vector.reciprocal(out=LR[64:65, 0:n], in_=LR[64:65, 0:n]),
                               ["LR"], ["LR"])
                        mm(PB[6][0:64, 0:n], ONEF[64:65, 0:64], LR[64:65, 0:n], True, True, ["ONEF", "LR"], [P(6)])
                        cp("dve", BC[0:64, 0:n], PB[6][0:64, 0:n], [P(6)], ["BC"])
                        tt("dve", dst, PB[ob][0:64, 0:n], BC[0:64, 0:n], ALU.mult, [P(ob), "BC"], ["OT"])
                    else:
                        cp("dve", dst, PB[ob][0:64, 0:n], [P(ob)], ["OT"])

                def st_sb_esp(tl):
                    n = tl["n"]
                    o = tl["c0"] - tl["cbase"]
                    b, s2 = abank(tl), tl["idx"] % 2
                    act(ET[s2][:, 0:n], PB[b][:, o:o + n], AF.Exp, [P(b)], [f"ET{s2}"])
                    act(SPB[s2][:, 0:n], ET[s2][:, 0:n], AF.Ln, [f"ET{s2}"], [f"SPB{s2}"], bias=1.0)

                def st_sb_B(tl):
                    n = tl["n"]
                    o = tl["c0"] - tl["cbase"]
                    s2 = tl["idx"] % 2
                    bank = abank(tl)
                    use_acc = not tl["first"]
                    mm(PB[bank][:, o:o + n], NEGU[:], SPB[s2][:, 0:n], False, not use_acc, ["NEGU", f"SPB{s2}"],
                       [P(bank)], inc=(not use_acc), skip=True)
                    if use_acc:
                        ab = tl["idx"] % 2
                        mm(PB[bank][:, o:o + n], NONEB[:], ACB[ab][:, o:o + n], False, True, ["NONEB", f"ACB{ab}"],
                           [P(bank)], skip=True)

                def st_sb_acc(tl):
                    n = tl["n"]
                    o = tl["c0"] - tl["cbase"]
                    s2 = tl["idx"] % 2
                    if tl["last"]:
                        return
                    nb = (tl["idx"] + 1) % 2
                    if tl["first"]:
                        mset("dve", ACC[:], 0.0, ["ACC"])
                    tt("dve", ACC[:, o:o + n], ACC[:, o:o + n], SPB[s2][:, 0:n], ALU.add, ["ACC", f"SPB{s2}"], ["ACC"])
                    nx = tiles[tl["idx"] + 1]
                    o2 = nx["c0"] - nx["cbase"]
                    cp("dve", ACB[nb][:, o2:o2 + nx["n"]], ACC[:, o2:o2 + nx["n"]], ["ACC"], [f"ACB{nb}"])

                def st_sb_exp2(tl):
                    n = tl["n"]
                    o = tl["c0"] - tl["cbase"]
                    bank = abank(tl)
                    s3 = tl["idx"] % 3
                    act(PT[s3][:, 0:n], PB[bank][:, o:o + n], AF.Exp, [P(bank)], [f"PT{s3}"])

                if fox:
                    mset("dve", ACB[0][:], 0.0, ["ACB0"])
                    for b_ in range(4):
                        mm(PB[b_][:, :], IDB[:], ACB[0][:, :], True, True, ["IDB", "ACB0"], [P(b_)])
                    sched = [(st_A, 0), (st_fox_exp, 2), (st_O, 2)]
                else:
                    sched = [(st_A, 0), (st_sb_esp, 1), (st_sb_B, 1), (st_sb_acc, 1), (st_sb_exp2, 2), (st_O, 2)]
                if not fox:
                    ptiles = []
                    for j2 in range(2):
                        for g in range(5):
                            halo = g == 4
                            kbs = (list(range(64)) if halo else list(range(16 * g + 16)))[::-1]
                            for n_, kb in enumerate(kbs):
                                j = kb // 4
                                if halo:
                                    c0, n = 2048 + 2 * j, 32 - 2 * j
                                    diag = True
                                else:
                                    a = max(0, j - 4 * g)
                                    c0, n = 512 * g + 128 * a, 512 - 128 * a
                                    diag = j >= 4 * g
                                ptiles.append(dict(j2=j2, g=g, kb=kb, c0=c0, n=n, diag=diag, first=(n_ == 0),
                                                   last=(n_ == len(kbs) - 1), halo=halo,
                                                   cbase=(2048 if halo else 512 * g)))
                    ch = -1
                    for t_, tl in enumerate(ptiles):
                        tl["idx"] = t_
                        if tl["first"]:
                            ch += 1
                        tl["chain"] = ch

                    def pA(tl):
                        j2, kb, c0, n = tl["j2"], tl["kb"], tl["c0"], tl["n"]
                        o = c0 - tl["cbase"]
                        bA = 2 * (tl["idx"] % 3)
                        steps = []
                        for h in range(2):
                            steps.append((KT[j2][64 * h:64 * h + 64, 128 * kb:128 * kb + 128],
                                          QT[j2][64 * h:64 * h + 64, c0:c0 + n], PB[bA + h][:, o:o + n],
                                          [f"KT{j2}", f"QT{j2}"], bA + h, True))
                        if tl["diag"]:
                            for h in range(2):
                                if tl["halo"]:
                                    w = min(4, n)
                                    steps.append((IDB[:], MSK[:, 1, kb % 4, 128:128 + w], PB[bA + h][:, o:o + w],
                                                  ["IDB", "MSK"], bA + h, False))
                                else:
                                    steps.append((IDB[:], MSK[:, 1, kb % 4, 0:128], PB[bA + h][:, o:o + 128],
                                                  ["IDB", "MSK"], bA + h, False))
                        for n_, (l_, r_, o_, rd, bk_, st_) in enumerate(steps):
                            lastst = n_ == len(steps) - 1
                            mm(o_, l_, r_, st_, lastst, rd, [P(bk_)], inc=lastst, skip=True)

                    def pESP(tl):
                        n = tl["n"]
                        o = tl["c0"] - tl["cbase"]
                        bA = 2 * (tl["idx"] % 3)
                        s2 = tl["idx"] % 2
                        act(ETp[s2][:, :, 0:n], PALL[:, bA:bA + 2, o:o + n], AF.Exp, [P(bA), P(bA + 1)], [f"ETp{s2}"])
                        act(SPBp[s2][:, :, 0:n], ETp[s2][:, :, 0:n], AF.Ln, [f"ETp{s2}"], [f"SPBp{s2}"], bias=1.0)

                    def pB(tl):
                        n = tl["n"]
                        o = tl["c0"] - tl["cbase"]
                        bA = 2 * (tl["idx"] % 3)
                        s2 = tl["idx"] % 2
                        use_acc = not tl["first"]
                        for h in range(2):
                            lastm = (h == 1) and not use_acc
                            mm(PB[bA + h][:, o:o + n], NEGU[:], SPBp[s2][:, h, 0:n], False, lastm,
                               ["NEGU", f"SPBp{s2}"], [P(bA + h)], inc=lastm, skip=True)
                        if use_acc:
                            ab = tl["idx"] % 2
                            for h in range(2):
                                mm(PB[bA + h][:, o:o + n], NONEB[:], ACBp[ab][:, h, o:o + n], False, h == 1,
                                   ["NONEB", f"ACBp{ab}"], [P(bA + h)], inc=(h == 1), skip=True)

                    def pACC(tl):
                        n = tl["n"]
                        o = tl["c0"] - tl["cbase"]
                        s2 = tl["idx"] % 2
                        if tl["last"]:
                            return
                        nb = (tl["idx"] + 1) % 2
                        if tl["first"]:
                            mset("dve", ACCp[:], 0.0, ["ACCp"])
                        tt("dve", ACCp[:, :, o:o + n], ACCp[:, :, o:o + n], SPBp[s2][:, :, 0:n], ALU.add,
                           ["ACCp", f"SPBp{s2}"], ["ACCp"])
                        nx = ptiles[tl["idx"] + 1]
                        o2 = nx["c0"] - nx["cbase"]
                        cp("dve", ACBp[nb][:, :, o2:o2 + nx["n"]], ACCp[:, :, o2:o2 + nx["n"]], ["ACCp"],
                           [f"ACBp{nb}"])

                    def pEXP2(tl):
                        n = tl["n"]
                        o = tl["c0"] - tl["cbase"]
                        bA = 2 * (tl["idx"] % 3)
                        s3 = tl["idx"] % 3
                        act(PTp[s3][:, :, 0:n], PALL[:, bA:bA + 2, o:o + n], AF.Exp, [P(bA), P(bA + 1)], [f"PTp{s3}"])

                    def pO(tl):
                        j2, kb, n = tl["j2"], tl["kb"], tl["n"]
                        o = tl["c0"] - tl["cbase"]
                        s3 = tl["idx"] % 3
                        ob = 6 + tl["chain"] % 2
                        mm(PALL[0:64, ob, o:o + n], VP[:, kb, 2 * j2, 0:64], PTp[s3][:, 0, 0:n], tl["first"], tl["last"],
                           ["VP", f"PTp{s3}"], [P(ob)], inc=False, skip=True)
                        mm(PALL[64:128, ob, o:o + n], VP[:, kb, 2 * j2 + 1, 0:64], PTp[s3][:, 1, 0:n], tl["first"],
                           tl["last"], ["VP", f"PTp{s3}"], [P(ob)], inc=True, skip=True, tpos=(0, 64))
                        if tl["last"]:
                            nf = 32 if tl["halo"] else 512
                            cb_ = tl["cbase"]
                            cp("dve", OT[:, 2 * p + j2, cb_:cb_ + nf], PB[ob][:, 0:nf], [P(ob)], ["OT"])

                    psched = [(pA, 0), (pESP, 1), (pB, 1), (pACC, 1), (pEXP2, 2), (pO, 2)]
                    npt = len(ptiles)
                    for T in range(npt + 3):
                        for fn, skew in psched:
                            k = T - skew
                            if 0 <= k < npt:
                                fn(ptiles[k])
                else:
                    nt = len(tiles)
                    for T in range(nt + 3):
                        for fn, skew in sched:
                            k = T - skew
                            if 0 <= k < nt:
                                fn(tiles[k])
                trk.barrier()

        with ExitStack() as satt:
            CNEG = sbt(satt, "CNEG", [128, 64, 8], F32)
            SPL3 = sbt(satt, "SPL3", [128, 64, 8, 3], BF16)
            for p in range(4):
                attention_pass(p)

        chunks = [(256 * c, 256) for c in range(8)] + [(2048, 32)]
        with ExitStack() as pha:
            XT = sbt(pha, "XT", [128, 8, NOWN], F32)
            trk.dma("xt", XT[:, 0:4, :], xoT[:, 0:4, :], (), ["XT"])
            trk.dma("xt2", XT[:, 4:8, :], xoT[:, 4:8, :], (), ["XT"])
            with ExitStack() as ph:
                KMT = sbt(ph, "KMT", [128, 4, 2, 256], BF16)
                VM = sbt(ph, "VM", [128, 2, 1024], BF16)
                with ExitStack() as phm:
                    mem_prep(phm, KMT, VM)
                    trk.barrier()
                WO = sbt(ph, "WO", [128, 8, 1024], BF16)
                WMQ = sbt(ph, "WMQ", [128, 8, 1024], BF16)
                WMO = sbt(ph, "WMO", [128, 8, 1024], BF16)
                WS = [sbt(ph, f"WS{i}", [128, 512], F32) for i in range(2)]
                SQO = sbt(ph, "SQO", [128, 8, 256], BF16)
                OTS = sbt(ph, "OTS", [128, 8, 256], BF16)
                RF = sbt(ph, "RF", [128, 256], F32)
                RS = sbt(ph, "RS", [128, 256], F32)
                R2 = sbt(ph, "R2", [128, 256], F32)
                TMP = sbt(ph, "TMP", [128, 256], F32)
                T1 = [sbt(ph, f"T1{i}", [128, 256], F32) for i in range(2)]
                SQX = sbt(ph, "SQX", [128, 8, 256], BF16)
                H2 = sbt(ph, "H2", [128, 8, 256], BF16)
                QM = sbt(ph, "QM", [128, 8, 256], BF16)
                PM = sbt(ph, "PM", [128, 8, 256], BF16)
                RL = sbt(ph, "RL", [128, 256], F32)
                OM = sbt(ph, "OM", [128, 8, 256], BF16)
                k = 0
                for (wsrc, wdst, nm) in ((wout, WO, "WO"), (wmq, WMQ, "WMQ"), (wmo, WMO, "WMO")):
                    for fc in range(8):
                        for hf in range(2):
                            s = k % 2
                            k += 1
                            trk.dma(f"ws{s}", WS[s][:], wsrc[:, fc, 512 * hf:512 * hf + 512], (), [f"WS{s}"])
                            dst_ = wdst[:, fc, 512 * hf:512 * hf + 512]
                            if nm == "WO":
                                act(dst_, WS[s][:], AF.Copy, [f"WS{s}", "GOUT"], [nm], scale=GOUT[:, fc:fc + 1])
                            elif nm == "WMQ":
                                act(dst_, WS[s][:], AF.Copy, [f"WS{s}", "G16"], [nm], scale=G16[:, fc:fc + 1])
                            else:
                                cp("act", dst_, WS[s][:], [f"WS{s}"], [nm])

                for (c0, n) in chunks:
                    act(SQO[:, :, 0:n], OT[:, :, c0:c0 + n], AF.Square, ["OT"], ["SQO"])
                    for grp, dst, nm in ((0, RF, "RF"), (1, RS, "RS")):
                        for q_ in range(4):
                            mm(PB[6][:, 0:n], ONEB[:], SQO[:, 4 * grp + q_, 0:n], q_ == 0, q_ == 3, ["ONEB", "SQO"],
                               [P(6)], inc=(q_ == 3))
                        act(TMP[:, 0:n], PB[6][:, 0:n], AF.Ln, [P(6), "EPST"], ["TMP"], bias=EPST[:], scale=1.0 / 512)
                        act(dst[:, 0:n], TMP[:, 0:n], AF.Exp, ["TMP"], [nm], scale=-0.5)
                    for q_ in range(8):
                        rr, nm = (RF, "RF") if q_ < 4 else (RS, "RS")
                        tt("dve", OTS[:, q_, 0:n], OT[:, q_, c0:c0 + n], rr[:, 0:n], ALU.mult, ["OT", nm], ["OTS"])
                    for fc in range(8):
                        bk = fc % 2
                        for q_ in range(8):
                            mm(PB[bk][:, 0:n], WO[:, q_, 128 * fc:128 * fc + 128], OTS[:, q_, 0:n], q_ == 0, q_ == 7,
                               ["WO", "OTS"], [P(bk)], inc=(q_ == 7))
                        tt("dve", XT[:, fc, c0:c0 + n], PB[bk][:, 0:n], XT[:, fc, c0:c0 + n], ALU.add,
                           [P(bk), "XT"], ["XT"])
                    act(SQX[:, :, 0:n], XT[:, :, c0:c0 + n], AF.Square, ["XT"], ["SQX"])
                    for fc in range(8):
                        mm(PB[6][:, 0:n], ONEB[:], SQX[:, fc, 0:n], fc == 0, fc == 7, ["ONEB", "SQX"], [P(6)],
                           inc=(fc == 7))
                    act(TMP[:, 0:n], PB[6][:, 0:n], AF.Ln, [P(6), "EPST"], ["TMP"], bias=EPST[:], scale=1.0 / 1024)
                    act(R2[:, 0:n], TMP[:, 0:n], AF.Exp, ["TMP"], ["R2"], scale=-0.5)
                    for fc in range(8):
                        tt("dve", H2[:, fc, 0:n], XT[:, fc, c0:c0 + n], R2[:, 0:n], ALU.mult, ["XT", "R2"], ["H2"])
                    for cc in range(8):
                        bk = cc % 2
                        for fc in range(8):
                            mm(PB[bk][:, 0:n], WMQ[:, fc, 128 * cc:128 * cc + 128], H2[:, fc, 0:n], fc == 0, fc == 7,
                               ["WMQ", "H2"], [P(bk)], inc=(fc == 7))
                        cp("act", QM[:, cc, 0:n], PB[bk][:, 0:n], [P(bk)], ["QM"])
                    for h in range(4):
                        for mc in range(2):
                            bk = 2 + mc
                            for dc in range(2):
                                mm(PB[bk][:, 0:n], KMT[:, h, dc, 128 * mc:128 * mc + 128], QM[:, 2 * h + dc, 0:n],
                                   dc == 0, dc == 1, ["KMT", "QM"], [P(bk)], inc=(dc == 1))
                            act(PM[:, 2 * h + mc, 0:n], PB[bk][:, 0:n], AF.Exp, [P(bk)], ["PM"])
                        for mc in range(2):
                            mm(PB[6][:, 0:n], ONEB[:], PM[:, 2 * h + mc, 0:n], mc == 0, mc == 1, ["ONEB", "PM"],
                               [P(6)], inc=(mc == 1))
                        trk.op("dve", lambda: nc.vector.reciprocal(out=RL[:, 0:n], in_=PB[6][:, 0:n]), [P(6)], ["RL"])
                        for dc in range(2):
                            bk = dc
                            for mc in range(2):
                                mm(PB[bk][:, 0:n], VM[:, mc, 256 * h + 128 * dc:256 * h + 128 * dc + 128],
                                   PM[:, 2 * h + mc, 0:n], mc == 0, mc == 1, ["VM", "PM"], [P(bk)], inc=(mc == 1))
                            tt("dve", OM[:, 2 * h + dc, 0:n], PB[bk][:, 0:n], RL[:, 0:n], ALU.mult, [P(bk), "RL"],
                               ["OM"])
                    for fc in range(8):
                        bk = 4 + fc % 2
                        for cc in range(8):
                            mm(PB[bk][:, 0:n], WMO[:, cc, 128 * fc:128 * fc + 128], OM[:, cc, 0:n], cc == 0, cc == 7,
                               ["WMO", "OM"], [P(bk)], inc=(cc == 7))
                        tt("dve", XT[:, fc, c0:c0 + n], PB[bk][:, 0:n], XT[:, fc, c0:c0 + n], ALU.add,
                           [P(bk), "XT"], ["XT"])
                trk.barrier()

            with ExitStack() as ph:
                H3 = sbt(ph, "H3", [128, 8, NOWN], BF16)
                WU = sbt(ph, "WU", [128, 8, 2, 1408], BF16)
                WD = OT[:, :, :].rearrange("p a b -> p (a b)")[:, 0:11 * 1024].rearrange("p (c n) -> p c n", n=1024)
                WUs = [sbt(ph, f"WUs{i}", [128, 704], F32) for i in range(2)]
                WDs = [sbt(ph, f"WDs{i}", [128, 512], F32) for i in range(2)]
                UH = sbt(ph, "UH", [128, 22, 32], F32)
                FIX = sbt(ph, "FIX", [128, 22, 16, 2], F32)
                YG = [sbt(ph, f"YG{i}", [128, 256], F32) for i in range(2)]
                YV = [sbt(ph, f"YV{i}", [128, 256], F32) for i in range(2)]
                AT_ = [sbt(ph, f"ATt{i}", [128, 256], BF16) for i in range(3)]
                TMP4 = sbt(ph, "TMP4", [128, 256], F32)
                R4 = sbt(ph, "R4", [128, 256], F32)
                SQ4 = OT[:, :, :].rearrange("p a b -> p (a b)")[:, 11264:11264 + 2048].rearrange("p (c n) -> p c n", n=256)
                OS = [sbt(ph, f"OS{i}", [128, 256], F32) for i in range(2)]
                for (c0, n) in chunks:
                    act(SQ4[:, :, 0:n], XT[:, :, c0:c0 + n], AF.Square, ["XT"], ["SQ4"])
                    for fc in range(8):
                        mm(PB[6][:, 0:n], ONEB[:], SQ4[:, fc, 0:n], fc == 0, fc == 7, ["ONEB", "SQ4"], [P(6)],
                           inc=(fc == 7))
                    act(TMP4[:, 0:n], PB[6][:, 0:n], AF.Ln, [P(6), "EPST"], ["TMP4"], bias=EPST[:], scale=1.0 / 1024)
                    act(R4[:, 0:n], TMP4[:, 0:n], AF.Exp, ["TMP4"], ["R4"], scale=-0.5)
                    for fc in range(8):
                        tt("dve", H3[:, fc, c0:c0 + n], XT[:, fc, c0:c0 + n], R4[:, 0:n], ALU.mult, ["XT", "R4"],
                           ["H3"])
                for sw in range(2):
                    k = 0
                    for fc in range(8):
                        for gv in range(2):
                            for hf in range(2):
                                s = k % 2
                                k += 1
                                col = 2816 * gv + 1408 * sw + 704 * hf
                                trk.dma(f"wus{s}", WUs[s][:], wup[:, fc, col:col + 704], (), [f"WUs{s}"])
                                act(WU[:, fc, gv, 704 * hf:704 * hf + 704], WUs[s][:], AF.Copy, [f"WUs{s}", "GALL"],
                                    ["WU"], scale=GALL[:, 3, fc:fc + 1])
                    k = 0
                    for cl in range(11):
                        for hf in range(2):
                            s = k % 2
                            k += 1
                            trk.dma(f"wds{s}", WDs[s][:], wdown[:, 11 * sw + cl, 512 * hf:512 * hf + 512], (),
                                    [f"WDs{s}"])
                            cp("act", WD[:, cl, 512 * hf:512 * hf + 512], WDs[s][:], [f"WDs{s}"], ["WD"])
                    for q_ in range(22):
                        gv, cl = q_ // 11, q_ % 11
                        bk = q_ % 2
                        for fc in range(8):
                            mm(PB[bk][:, 0:32], WU[:, fc, gv, 128 * cl:128 * cl + 128], H3[:, fc, 2048:2080], fc == 0,
                               fc == 7, ["WU", "H3"], [P(bk)], inc=(fc == 7))
                        tt("dve", UH[:, q_, :], PB[bk][:, 0:32], HVAL[:], ALU.mult, [P(bk), "HVAL"], ["UH"])
                    for q_ in range(22):
                        gv, cl = q_ // 11, q_ % 11
                        cc = 22 * gv + 11 * sw + cl
                        uh = UH[:, q_, :].rearrange("p (b c) -> p b c", c=2)
                        ts("pool", FIX[:, q_, :, 1], uh[:, :, 1], CW[:, cc, 0:1], None, ALU.mult, None, ["UH", "CW"],
                           ["FIX"])
                        ts("pool", FIX[:, q_, :, 0], uh[:, :, 0], CW[:, cc, 0:1], None, ALU.mult, None, ["UH", "CW"],
                           ["FIX"])
                        stt("dve", FIX[:, q_, :, 0], uh[:, :, 1], CW[:, cc, 1:2], FIX[:, q_, :, 0], ALU.mult, ALU.add,
                            ["UH", "CW", "FIX"], ["FIX"])
                    for tc in range(8):
                        c0 = 256 * tc
                        units = list(range(11))

                        def u_stage(cl, tc=tc, c0=c0, sw=sw):
                            for gv in range(2):
                                bk = 2 * (cl % 2) + gv
                                for fc in range(8):
                                    mm(PB[bk][:, 0:256], WU[:, fc, gv, 128 * cl:128 * cl + 128], H3[:, fc, c0:c0 + 256],
                                       fc == 0, fc == 7, ["WU", "H3"], [P(bk)], inc=(fc == 7))

                        def conv_stage(cl, tc=tc, c0=c0, sw=sw):
                            s = cl % 2
                            for gv, Y in ((0, YG[s]), (1, YV[s])):
                                bk = 2 * (cl % 2) + gv
                                cc = 22 * gv + 11 * sw + cl
                                q_ = 11 * gv + cl
                                nm = ("YG" if gv == 0 else "YV") + str(s)
                                act(Y[:, :], PB[bk][:, 0:256], AF.Identity, [P(bk), "CW", "CB"], [nm],
                                    bias=CB[:, cc:cc + 1], scale=CW[:, cc, 2:3])
                                y3 = Y[:, :].rearrange("p (b c) -> p b c", c=128)
                                u3 = PB[bk][:, 0:256].rearrange("p (b c) -> p b c", c=128)
                                stt("dve", y3[:, :, 1:128], u3[:, :, 0:127], CW[:, cc, 1:2], y3[:, :, 1:128], ALU.mult,
                                    ALU.add, [P(bk), "CW", nm], [nm])
                                stt("dve", y3[:, :, 2:128], u3[:, :, 0:126], CW[:, cc, 0:1], y3[:, :, 2:128], ALU.mult,
                                    ALU.add, [P(bk), "CW", nm], [nm])
                                tt("pool", y3[:, :, 0:2], y3[:, :, 0:2], FIX[:, q_, 2 * tc:2 * tc + 2, :], ALU.add,
                                   [nm, "FIX"], [nm])
                            act(YG[s][:, :], YG[s][:, :], AF.Silu, [f"YG{s}"], [f"YG{s}"])
                            a3 = cl % 3
                            tt("dve", AT_[a3][:, :], YG[s][:, :], YV[s][:, :], ALU.mult, [f"YG{s}", f"YV{s}"],
                               [f"AT{a3}"])

                        def down_stage(cl, tc=tc, c0=c0, sw=sw):
                            a3 = cl % 3
                            for fc in range(8):
                                bk = 4 + fc // 2
                                o = 256 * (fc % 2)
                                mm(PB[bk][:, o:o + 256], WD[:, cl, 128 * fc:128 * fc + 128], AT_[a3][:, :],
                                   (cl == 0 and fc % 2 == 0), cl == 10, ["WD", f"AT{a3}"], [P(bk)],
                                   inc=(fc == 7 or cl == 10), skip=True)

                        for T in range(11 + 2):
                            if T < 11:
                                u_stage(T)
                            if 0 <= T - 1 < 11:
                                conv_stage(T - 1)
                            if 0 <= T - 2 < 11:
                                down_stage(T - 2)
                        for fc in range(8):
                            bk = 4 + fc // 2
                            o = 256 * (fc % 2)
                            tt("dve", XT[:, fc, c0:c0 + 256], PB[bk][:, o:o + 256], XT[:, fc, c0:c0 + 256], ALU.add,
                               [P(bk), "XT"], ["XT"])
                for tc in range(8):
                    c0 = 256 * tc
                    act(SQ4[:, :, :], XT[:, :, c0:c0 + 256], AF.Square, ["XT"], ["SQ4"])
                    for fc in range(8):
                        mm(PB[0][:, 0:256], ONEB[:], SQ4[:, fc, :], fc == 0, fc == 7, ["ONEB", "SQ4"], [P(0)],
                           inc=(fc == 7))
                    act(TMP4[:, :], PB[0][:, 0:256], AF.Ln, [P(0), "EPST"], ["TMP4"], bias=EPST[:], scale=1.0 / 1024)
                    act(R4[:, :], TMP4[:, :], AF.Exp, ["TMP4"], ["R4"], scale=-0.5)
                    for fc in range(8):
                        s = fc % 2
                        stt("dve", OS[s][:, :], XT[:, fc, c0:c0 + 256], GALL[:, 4, fc:fc + 1], R4[:, :], ALU.mult,
                            ALU.mult, ["XT", "GALL", "R4"], [f"OS{s}"])
                        trk.dma(f"os{s}", outT[:, fc, c0:c0 + 256], OS[s][:, :], [f"OS{s}"], ())
                trk.barrier()
    return nc


_CACHE = {}


def _consts():
    ident = np.eye(128, dtype=np.float32)
    j = np.arange(128)[:, None]
    s = np.arange(128)[None, :]
    negu = np.where(j >= s, -1.0, 0.0).astype(np.float32)
    u = np.where(j <= s, 1.0, 0.0).astype(np.float32)
    cmat = np.stack([ident, negu, u], axis=1)
    selc = np.zeros((12, 4, 128), np.float32)
    for h in range(4):
        selc[3 * h:3 * h + 3, h, :] = 1.0
    return np.ascontiguousarray(cmat), selc


def _masks(r):
    m = np.zeros((128, 2, 4, 132), np.float32)
    s = np.arange(128)[:, None]
    t = np.arange(128)[None, :]
    for kind in range(2):
        for a in range(4):
            if r > a:
                mm_ = np.zeros((128, 128), np.float32)
            elif r == a:
                ok = (s <= t) if kind == 0 else (s < t)
                mm_ = np.where(ok, 0.0, NEG).astype(np.float32)
            else:
                mm_ = np.full((128, 128), NEG, np.float32)
            m[:, kind, a, 0:128] = mm_
            for ii in range(2):
                hb = 4 * ii + r - 1
                for c in range(2):
                    pos = 126 + c
                    if hb > a:
                        col = np.zeros(128, np.float32)
                    elif hb == a:
                        ok = (np.arange(128) <= pos) if kind == 0 else (np.arange(128) < pos)
                        col = np.where(ok, 0.0, NEG).astype(np.float32)
                    else:
                        col = np.full(128, NEG, np.float32)
                    m[:, kind, a, 128 + 2 * ii + c] = col
    return m


def kernel(x, mem, attn_norm_g, w_in, b_forget, fox_out_g, sb_out_g, w_out, xattn_norm_g, mem_norm_g,
           w_mq, w_mkv, w_mo, ffn_norm_g, w_up, conv_w, conv_b, w_down, final_norm_g):
    f32 = np.float32
    x = np.asarray(x, f32)
    mem = np.asarray(mem, f32)

    def fm(w):
        w = np.asarray(w, f32)
        k, n = w.shape
        return np.ascontiguousarray(w.reshape(k // 128, 128, n).transpose(1, 0, 2))

    w_in = np.asarray(w_in, f32)[0]
    wq = []
    for p in range(4):
        base = 0 if p < 2 else 1536
        h0 = 4 * (p % 2)
        cols = np.concatenate([np.arange(base + 512 * t + 64 * h0, base + 512 * t + 64 * h0 + 256) for t in range(3)])
        wq.append(fm(w_in[:, cols]))
    wqkv = np.ascontiguousarray(np.stack(wq, 0))
    wf = fm(w_in[:, 3072:3080])

    def gv(g):
        return np.asarray(g, f32).reshape(8, 128).T

    gall = np.ascontiguousarray(np.stack([gv(attn_norm_g[0]), gv(xattn_norm_g[0]), gv(mem_norm_g[0]),
                                          gv(ffn_norm_g[0]), gv(final_norm_g)], axis=1))
    bfg = np.ascontiguousarray(np.broadcast_to(np.asarray(b_forget, f32)[0][None, :], (128, 8)))
    gout = np.ascontiguousarray(gv(np.concatenate([np.asarray(fox_out_g, f32)[0], np.asarray(sb_out_g, f32)[0]])))
    common = dict(
        wqkv=wqkv, wf=wf, gall=gall, bfg=bfg, gout=gout,
        wout=fm(w_out[0]), wmq=fm(w_mq[0]), wmkv=fm(w_mkv[0]), wmo=fm(w_mo[0]), wup=fm(w_up[0]),
        convw=np.ascontiguousarray(np.asarray(conv_w, f32)[0].reshape(3, 44, 128).transpose(2, 1, 0)),
        convb=np.ascontiguousarray(np.asarray(conv_b, f32)[0].reshape(44, 128).T),
        wdown=fm(w_down[0]),
    )
    cmat, selc = _consts()
    common["cmat"] = cmat
    common["selc"] = selc

    xT = [fm(x[b]) for b in range(2)]
    xT = [fm(np.ascontiguousarray(x[b].T)) for b in range(2)]
    mT = [fm(np.ascontiguousarray(mem[b].T)) for b in range(2)]
    in_maps = []
    for c in range(8):
        b, r = c // 4, c % 4
        xo = np.zeros((128, 8, NOWN), f32)
        for i in range(16):
            t0 = 128 * (4 * i + r)
            xo[:, :, 128 * i:128 * i + 128] = xT[b][:, :, t0:t0 + 128]
            if t0 >= 2:
                xo[:, :, 2048 + 2 * i:2048 + 2 * i + 2] = xT[b][:, :, t0 - 2:t0]
        es_ = np.zeros((128, 4, 128), f32)
        es_[127, r, :] = -1.0
        hv = np.ones((128, 32), f32)
        if r == 0:
            hv[:, 0:2] = 0.0
        d = dict(common)
        d.update(xfT=xT[b], xoT=xo, memT=mT[b], masks=_masks(r), esel=es_, hvalid=hv)
        in_maps.append(d)

    if "nc" not in _CACHE:
        _CACHE["nc"] = build_program()
    res = run_bass_kernel_spmd(_CACHE["nc"], in_maps, core_ids=list(range(8)))
    out = np.zeros((2, S, 1024), f32)
    for c in range(8):
        b, r = c // 4, c % 4
        o = res.results[c]["outT"]
        o = o.transpose(2, 1, 0).reshape(2048, 1024)
        for i in range(16):
            t0 = 128 * (4 * i + r)
            out[b, t0:t0 + 128, :] = o[128 * i:128 * i + 128]
    return out
```

```python
import numpy as np
from contextlib import ExitStack
import concourse.bass as bass
import concourse.mybir as mybir
from concourse.bass_utils import run_bass_kernel_spmd

F32 = mybir.dt.float32
BF16 = mybir.dt.bfloat16
AF = mybir.ActivationFunctionType
ALU = mybir.AluOpType

S = 8192
NOWN = 2080
EPS = 1e-6
NEG = -30000.0


class Buf:
    __slots__ = ("w", "r")

    def __init__(self):
        self.w = None
        self.r = {}


class Trk:
    EPOCH = 30000

    def __init__(self, nc, es):
        self.nc = nc
        self.es = es
        self.eng = {"pe": nc.tensor, "act": nc.scalar, "dve": nc.vector, "pool": nc.gpsimd, "sp": nc.sync}
        self.sems = {e: [] for e in ("pe", "act", "dve", "pool")}
        self.cnt = {e: 0 for e in ("pe", "act", "dve", "pool")}
        self.pending = {e: False for e in ("pe", "act", "dve", "pool")}
        self.dsem = {}
        self.dcnt = {}
        self.know = {e: {} for e in self.eng}
        self.snap = {}
        self.bufs = {}
        self.nwait = 0
        self.nins = 0

    def _newsem(self, name):
        return self.es.enter_context(self.nc.semaphore(name))

    def _next_tok(self, e):
        c = self.cnt[e]
        ep, v = c // self.EPOCH, c % self.EPOCH + 1
        while len(self.sems[e]) <= ep:
            self.sems[e].append(self._newsem(f"s_{e}{len(self.sems[e])}"))
        return (e, ep, v)

    def _sem_of(self, tok):
        p, ep, v = tok
        if p.startswith("d:"):
            return self.dsem[p]
        return self.sems[p][ep]

    def _need(self, e, tok):
        return self.know[e].get(tok[0], (-1, 0)) < (tok[1], tok[2])

    def _learn(self, e, tok):
        k = self.know[e]
        sn = self.snap.get(tok)
        if sn:
            for p, val in sn.items():
                if k.get(p, (-1, 0)) < val:
                    k[p] = val
        if k.get(tok[0], (-1, 0)) < (tok[1], tok[2]):
            k[tok[0]] = (tok[1], tok[2])

    def _collect(self, e, reads, writes):
        cand = []
        for k in reads:
            b = self.bufs.get(k)
            if b is not None and b.w is not None:
                cand.append(b.w)
        for k in writes:
            b = self.bufs.get(k)
            if b is None:
                continue
            if b.w is not None and (b.w[0] != e or e != "pe"):
                cand.append(b.w)
            for p, t in b.r.items():
                if p != e or e != "pe":
                    cand.append(t)
        cand.sort(key=lambda t: (t[1], t[2]), reverse=True)
        needed = []
        for tok in cand:
            if self._need(e, tok):
                needed.append(tok)
                self._learn(e, tok)
        return needed

    def _emit_waits(self, e, needed, ins_fn):
        for tok in needed[:-1]:
            self.eng[e].wait_ge(self._sem_of(tok), tok[2])
            self.nwait += 1
        ins = ins_fn()
        if needed:
            tok = needed[-1]
            ins._wait_ge(self._sem_of(tok), tok[2])
        return ins

    def _record(self, tok, reads, writes):
        for k in reads:
            self.bufs.setdefault(k, Buf()).r[tok[0]] = tok
        for k in writes:
            b = self.bufs.setdefault(k, Buf())
            b.w = tok
            b.r = {}

    def op(self, e, fn, reads=(), writes=(), inc=True):
        needed = self._collect(e, reads, writes)
        ins = self._emit_waits(e, needed, fn)
        tok = self._next_tok(e)
        if inc:
            ins.then_inc(self.sems[e][tok[1]], 1)
            self.cnt[e] += 1
            self.pending[e] = False
            sn = dict(self.know[e])
            sn[e] = (tok[1], tok[2])
            self.snap[tok] = sn
        else:
            self.pending[e] = True
        self._record(tok, reads, writes)
        self.nins += 1
        return ins

    def dma(self, lane, out, in_, reads=(), writes=()):
        lane = "d:" + lane
        if lane not in self.dsem:
            self.dsem[lane] = self._newsem("s_" + lane.replace(":", "_"))
            self.dcnt[lane] = 0
        needed = self._collect("sp", reads, writes)
        ins = self._emit_waits("sp", needed, lambda: self.nc.sync.dma_start(out=out, in_=in_))
        self.dcnt[lane] += 1
        tok = (lane, 0, 16 * self.dcnt[lane])
        ins.then_inc(self.dsem[lane], 16)
        sn = dict(self.know["sp"])
        sn[lane] = (0, tok[2])
        self.snap[tok] = sn
        self._record(tok, reads, writes)
        self.nins += 1

    def barrier(self):
        for e in self.eng:
            toks = []
            for p in self.cnt:
                assert not self.pending[p]
                if p != e and self.cnt[p] > 0:
                    c = self.cnt[p] - 1
                    toks.append((p, c // self.EPOCH, c % self.EPOCH + 1))
            for lane, n in self.dcnt.items():
                if n > 0:
                    toks.append((lane, 0, 16 * n))
            for tok in toks:
                if self._need(e, tok):
                    self.eng[e].wait_ge(self._sem_of(tok), tok[2])
                    self.nwait += 1
                    self._learn(e, tok)
        self.bufs = {}
        self.snap = {}


def build_program():
    nc = bass.Bass("TRN2", target_bir_lowering=False)

    def din(name, shape):
        return nc.dram_tensor(name, list(shape), F32, kind="ExternalInput").ap()

    xfT = din("xfT", [128, 8, S])
    xoT = din("xoT", [128, 8, NOWN])
    memT = din("memT", [128, 8, 256])
    wqkv = din("wqkv", [4, 128, 8, 768])
    wf = din("wf", [128, 8, 8])
    gall = din("gall", [128, 5, 8])
    bfg = din("bfg", [128, 8])
    gout = din("gout", [128, 8])
    wout = din("wout", [128, 8, 1024])
    wmq = din("wmq", [128, 8, 1024])
    wmkv = din("wmkv", [128, 8, 2048])
    wmo = din("wmo", [128, 8, 1024])
    wup = din("wup", [128, 8, 5632])
    convw = din("convw", [128, 44, 3])
    convb = din("convb", [128, 44])
    wdown = din("wdown", [128, 22, 1024])
    masks = din("masks", [128, 2, 4, 132])
    esel = din("esel", [128, 4, 128])
    selc = din("selc", [12, 4, 128])
    cmat = din("cmat", [128, 3, 128])
    hvalid = din("hvalid", [128, 32])
    outT = nc.dram_tensor("outT", [128, 8, 2048], F32, kind="ExternalOutput").ap()

    with ExitStack() as es:
        trk = Trk(nc, es)

        uid = [0]

        def sbt(st, name, shape, dt):
            uid[0] += 1
            return st.enter_context(nc.sbuf_tensor(f"{name}_{uid[0]}", list(shape), dt))

        PALL = es.enter_context(nc.psum_tensor("pall", [128, 8, 512], F32))
        PB = [PALL[:, i, :] for i in range(8)]

        def P(i):
            return ("ps", i)

        def mm(out, lhsT, rhs, start, stop, reads, writes, inc=True, skip=False, tpos=None):
            if tpos is None:
                trk.op("pe", lambda: nc.tensor.matmul(out, lhsT, rhs, start=start, stop=stop,
                                                      skip_group_check=skip), reads, writes, inc=inc)
            else:
                trk.op("pe", lambda: nc.tensor.matmul(out, lhsT, rhs, start=start, stop=stop,
                                                      skip_group_check=skip, tile_position=tpos),
                       reads, writes, inc=inc)

        def act(out, in_, func, reads, writes, bias=None, scale=None):
            kw = {}
            if bias is not None:
                kw["bias"] = bias
            if scale is not None:
                kw["scale"] = scale
            trk.op("act", lambda: nc.scalar.activation(out=out, in_=in_, func=func, **kw), reads, writes)

        def veng(e):
            return nc.vector if e == "dve" else nc.gpsimd

        def tt(e, out, in0, in1, op, reads, writes):
            trk.op(e, lambda: veng(e).tensor_tensor(out=out, in0=in0, in1=in1, op=op), reads, writes)

        def ts(e, out, in0, s1, s2, op0, op1, reads, writes):
            if s2 is None:
                trk.op(e, lambda: veng(e).tensor_scalar(out=out, in0=in0, scalar1=s1, scalar2=None, op0=op0),
                       reads, writes)
            else:
                trk.op(e, lambda: veng(e).tensor_scalar(out=out, in0=in0, scalar1=s1, scalar2=s2, op0=op0,
                                                        op1=op1), reads, writes)

        def stt(e, out, in0, scalar, in1, op0, op1, reads, writes):
            trk.op(e, lambda: veng(e).scalar_tensor_tensor(out=out, in0=in0, scalar=scalar, in1=in1,
                                                           op0=op0, op1=op1), reads, writes)

        def cp(e, out, in_, reads, writes):
            if e == "act":
                act(out, in_, AF.Copy, reads, writes)
            else:
                trk.op(e, lambda: veng(e).tensor_copy(out=out, in_=in_), reads, writes)

        def mset(e, ap, val, writes):
            trk.op(e, lambda: veng(e).memset(ap, val), (), writes)

        OT = sbt(es, "OT", [128, 8, NOWN], BF16)
        IDB = sbt(es, "IDB", [128, 128], BF16)
        NEGU = sbt(es, "NEGU", [128, 128], BF16)
        UF = sbt(es, "UF", [128, 128], F32)
        ONEB = sbt(es, "ONEB", [128, 128], BF16)
        NONEB = sbt(es, "NONEB", [128, 128], BF16)
        ONEF = sbt(es, "ONEF", [128, 128], F32)
        MSK = sbt(es, "MSK", [128, 2, 4, 132], BF16)
        ESEL = sbt(es, "ESEL", [128, 4, 128], BF16)
        SELB = sbt(es, "SELB", [12, 4, 128], BF16)
        GALL = sbt(es, "GALL", [128, 5, 8], F32)
        GOUT = sbt(es, "GOUT", [128, 8], F32)
        BFG = sbt(es, "BFG", [128, 8], F32)
        HVAL = sbt(es, "HVAL", [128, 32], F32)
        EPST = sbt(es, "EPST", [128, 1], F32)
        G8 = sbt(es, "G8", [128, 8], F32)
        G16 = sbt(es, "G16", [128, 8], F32)
        CW = sbt(es, "CW", [128, 44, 3], F32)
        CB = sbt(es, "CB", [128, 44], F32)

        with ExitStack() as ph:
            STG = sbt(ph, "STG0", [128, 2, 4, 132], F32)
            CM = sbt(ph, "CM0", [128, 3, 128], F32)
            ES0 = sbt(ph, "ES0", [128, 4, 128], F32)
            SL0 = sbt(ph, "SL0", [12, 4, 128], F32)
            trk.dma("c0", STG[:], masks[:, :, :, :], (), ["STG"])
            trk.dma("c1", CM[:], cmat[:, :, :], (), ["CM"])
            trk.dma("c2", ES0[:], esel[:, :, :], (), ["ES0"])
            trk.dma("c3", SL0[:], selc[:, :, :], (), ["SL0"])
            trk.dma("c4", GALL[:], gall[:, :, :], (), ["GALL"])
            trk.dma("c5", GOUT[:], gout[:, :], (), ["GOUT"])
            trk.dma("c6", BFG[:], bfg[:, :], (), ["BFG"])
            trk.dma("c7", HVAL[:], hvalid[:, :], (), ["HVAL"])
            trk.dma("c8", CW[:], convw[:, :, :], (), ["CW"])
            trk.dma("c9", CB[:], convb[:, :], (), ["CB"])
            cp("dve", MSK[:], STG[:], ["STG"], ["MSK"])
            cp("dve", IDB[:], CM[:, 0, :], ["CM"], ["IDB"])
            cp("dve", NEGU[:], CM[:, 1, :], ["CM"], ["NEGU"])
            cp("dve", UF[:], CM[:, 2, :], ["CM"], ["UF"])
            cp("dve", ESEL[:], ES0[:], ["ES0"], ["ESEL"])
            cp("dve", SELB[:], SL0[:], ["SL0"], ["SELB"])
            mset("pool", ONEB[:], 1.0, ["ONEB"])
            mset("pool", NONEB[:], -1.0, ["NONEB"])
            mset("pool", ONEF[:], 1.0, ["ONEF"])
            mset("pool", EPST[:], EPS, ["EPST"])
            ts("dve", G8[:], GALL[:, 0, :], 0.125, None, ALU.mult, None, ["GALL"], ["G8"])
            ts("dve", G16[:], GALL[:, 1, :], 1.0 / 16, None, ALU.mult, None, ["GALL"], ["G16"])

            trk.barrier()

        def mem_prep(ph, KMT, VM):
            MS = sbt(ph, "MS", [128, 8, 256], F32)
            MBm = sbt(ph, "MBm", [128, 8, 256], BF16)
            MSQ = sbt(ph, "MSQ", [128, 8, 256], BF16)
            RMB = sbt(ph, "RMB", [128, 256], F32)
            RMT = sbt(ph, "RMT", [128, 2], F32)
            TMPm = sbt(ph, "TMPm", [128, 256], F32)
            WKVs = [sbt(ph, f"WKVs{i}", [128, 2048], F32) for i in range(2)]
            WKV = sbt(ph, "WKV", [128, 8, 2048], BF16)
            trk.dma("ms", MS[:], memT[:, :, :], (), ["MS"])
            cp("act", MBm[:], MS[:], ["MS"], ["MBm"])
            tt("dve", MSQ[:], MS[:], MS[:], ALU.mult, ["MS"], ["MSQ"])
            for fc in range(8):
                s = fc % 2
                trk.dma(f"wkvs{s}", WKVs[s][:], wmkv[:, fc, :], (), [f"WKVs{s}"])
                act(WKV[:, fc, :], WKVs[s][:], AF.Copy, [f"WKVs{s}", "GALL"], ["WKV"], scale=GALL[:, 2, fc:fc + 1])
            for fc in range(8):
                mm(PB[6][:, 0:256], ONEB[:], MSQ[:, fc, :], fc == 0, fc == 7, ["ONEB", "MSQ"], [P(6)], inc=(fc == 7))
            act(TMPm[:], PB[6][:, 0:256], AF.Ln, [P(6), "EPST"], ["TMPm"], bias=EPST[:], scale=1.0 / 1024)
            act(RMB[:], TMPm[:], AF.Exp, ["TMPm"], ["RMB"], scale=-0.5)
            for mc in range(2):
                for fc in range(8):
                    mm(PB[7][:, mc:mc + 1], MSQ[:, fc, 128 * mc:128 * mc + 128], ONEB[:, 0:1], fc == 0, fc == 7,
                       ["MSQ", "ONEB"], [P(7)], inc=(fc == 7))
            act(TMPm[:, 0:2], PB[7][:, 0:2], AF.Ln, [P(7), "EPST"], ["TMPm"], bias=EPST[:], scale=1.0 / 1024)
            act(RMT[:], TMPm[:, 0:2], AF.Exp, ["TMPm"], ["RMT"], scale=-0.5)
            for cc in range(8):
                h, dc = cc // 2, cc % 2
                bk = cc % 2
                for fc in range(8):
                    mm(PB[bk][:, 0:256], WKV[:, fc, 128 * cc:128 * cc + 128], MBm[:, fc, :], fc == 0, fc == 7,
                       ["WKV", "MBm"], [P(bk)], inc=(fc == 7))
                tt("dve", KMT[:, h, dc, :], PB[bk][:, 0:256], RMB[:], ALU.mult, [P(bk), "RMB"], ["KMT"])
            for mc in range(2):
                for hf in range(2):
                    bk = 2 + (2 * mc + hf) % 2
                    for fc in range(8):
                        mm(PB[bk][:, :], MBm[:, fc, 128 * mc:128 * mc + 128],
                           WKV[:, fc, 1024 + 512 * hf:1024 + 512 * hf + 512], fc == 0, fc == 7,
                           ["MBm", "WKV"], [P(bk)], inc=(fc == 7))
                    ts("dve", VM[:, mc, 512 * hf:512 * hf + 512], PB[bk][:, :], RMT[:, mc:mc + 1], None, ALU.mult,
                       None, [P(bk), "RMT"], ["VM"])

        def attention_pass(p):
            fox = p < 2
            with ExitStack() as ph:
                KT = [sbt(ph, f"KT{j}", [128, S], BF16) for j in range(2)]
                VP = sbt(ph, "VP", [128, 64, 4, 65], BF16)
                QT = [sbt(ph, f"QT{j}", [128, NOWN], BF16) for j in range(2)]
                WQ = sbt(ph, "WQ", [128, 8, 256], BF16)
                WK = sbt(ph, "WK", [128, 8, 256], BF16)
                WV = sbt(ph, "WV", [128, 8, 256], BF16)
                WF = sbt(ph, "WF", [128, 8, 8], BF16)
                WFs = sbt(ph, "WFs", [128, 8, 8], F32)
                QAUG = sbt(ph, "QAUG", [12, NOWN], BF16) if fox else None
                XS = [sbt(ph, f"XS{i}", [128, 8, 260], F32) for i in range(2)]
                XB = [sbt(ph, f"XB{i}", [128, 8, 260], BF16) for i in range(2)]
                SQ = [sbt(ph, f"SQ{i}", [128, 8, 260], BF16) for i in range(2)]
                RBC = [sbt(ph, f"RBC{i}", [128, 260], F32) for i in range(2)]
                RTK = [sbt(ph, f"RTK{i}", [128, 2], F32) for i in range(2)]
                TMPRs = [sbt(ph, f"TMPR{i}", [128, 260], F32) for i in range(2)]
                TMPTs = [sbt(ph, f"TMPT{i}", [128, 2], F32) for i in range(2)]
                WST = [sbt(ph, f"WST{i}", [128, 768], F32) for i in range(2)]
                ZF = sbt(ph, "ZF", [128, 8], F32)
                EF = sbt(ph, "EF", [128, 8], F32)
                SPF = [sbt(ph, f"SPF{i}", [128, 8], F32) for i in range(4)]
                ACCF = sbt(ph, "ACCF", [128, 8], F32)
                if fox:
                    ET = [sbt(ph, f"ET{i}", [128, 512], F32) for i in range(2)]
                    PT = [sbt(ph, f"PT{i}", [128, 512], BF16) for i in range(3)]
                    ACB = [sbt(ph, f"ACB{i}", [128, 512], BF16) for i in range(1)]
                    LR = sbt(ph, "LR", [128, 512], F32)
                    BC = sbt(ph, "BC", [128, 512], F32)
                    AUGT = [sbt(ph, f"AUGT{i}", [128, 512], F32) for i in range(2)]
                else:
                    ET = SPB = PT = ACB = None
                    ACC = LR = BC = AUGT = None
                if not fox:
                    ETp = [sbt(ph, f"ETp{i}", [128, 2, 512], F32) for i in range(2)]
                    SPBp = [sbt(ph, f"SPBp{i}", [128, 2, 512], BF16) for i in range(2)]
                    PTp = [sbt(ph, f"PTp{i}", [128, 2, 512], BF16) for i in range(3)]
                    ACCp = sbt(ph, "ACCp", [128, 2, 512], F32)
                    ACBp = [sbt(ph, f"ACBp{i}", [128, 2, 512], BF16) for i in range(2)]

                for fc in range(8):
                    s = fc % 2
                    trk.dma(f"wst{s}", WST[s][:], wqkv[p, :, fc, :], (), [f"WST{s}"])
                    act(WQ[:, fc, :], WST[s][:, 0:256], AF.Copy, [f"WST{s}", "G8"], ["WQ"], scale=G8[:, fc:fc + 1])
                    act(WK[:, fc, :], WST[s][:, 256:512], AF.Copy, [f"WST{s}", "GALL"], ["WK"],
                        scale=GALL[:, 0, fc:fc + 1])
                    act(WV[:, fc, :], WST[s][:, 512:768], AF.Copy, [f"WST{s}", "GALL"], ["WV"],
                        scale=GALL[:, 0, fc:fc + 1])
                if p == 0:
                    trk.dma("wfs", WFs[:], wf[:, :, :], (), ["WFs"])
                    for fc in range(8):
                        ts("pool", WF[:, fc, :], WFs[:, fc, :], GALL[:, 0, fc:fc + 1], None, ALU.mult, None,
                           ["WFs", "GALL"], ["WF"])
                    mset("pool", ACCF[:], 0.0, ["ACCF"])
                mset("pool", VP[:, :, :, 64:65], 1.0, ["VP"])

                def prefetch(src, c0, n, s):
                    trk.dma(f"xs{s}", XS[s][:, :, 0:n], src[:, :, c0:c0 + n], (), [f"XS{s}"])
                    cp("act", XB[s][:, :, 0:n], XS[s][:, :, 0:n], [f"XS{s}"], [f"XB{s}"])
                    tt("dve", SQ[s][:, :, 0:n], XS[s][:, :, 0:n], XS[s][:, :, 0:n], ALU.mult, [f"XS{s}"], [f"SQ{s}"])

                def load_chunk(src, c0, n, s):
                    for fc in range(8):
                        mm(PB[6][:, 0:n], ONEB[:], SQ[s][:, fc, 0:n], fc == 0, fc == 7, ["ONEB", f"SQ{s}"], [P(6)],
                           inc=(fc == 7))
                    TMPR = TMPRs[s]
                    act(TMPR[:, 0:n], PB[6][:, 0:n], AF.Ln, [P(6), "EPST"], [f"TMPR{s}"], bias=EPST[:], scale=1.0 / 1024)
                    act(RBC[s][:, 0:n], TMPR[:, 0:n], AF.Exp, [f"TMPR{s}"], [f"RBC{s}"], scale=-0.5)

                stream = [(xoT, 260 * c_, 260) for c_ in range(8)] + [(xfT, 256 * c_, 256) for c_ in range(32)]
                prefetch(stream[0][0], stream[0][1], stream[0][2], 0)

                def prefetch_next(i_):
                    if i_ + 1 < len(stream):
                        prefetch(stream[i_ + 1][0], stream[i_ + 1][1], stream[i_ + 1][2], (i_ + 1) % 2)

                for c in range(8):
                    s = c % 2
                    prefetch_next(c)
                    load_chunk(xoT, 260 * c, 260, s)
                    for j in range(2):
                        bk = j
                        for fc in range(8):
                            mm(PB[bk][:, 0:260], WQ[:, fc, 128 * j:128 * j + 128], XB[s][:, fc, 0:260], fc == 0,
                               fc == 7, ["WQ", f"XB{s}"], [P(bk)], inc=(fc == 7))
                        tt("dve", QT[j][:, 260 * c:260 * c + 260], PB[bk][:, 0:260], RBC[s][:, 0:260], ALU.mult,
                           [P(bk), f"RBC{s}"], [f"QT{j}"])

                def fl_part(ci, s):
                    for blk in range(2):
                        kb = 2 * ci + blk
                        sp_ = kb % 4
                        for fc in range(8):
                            mm(PB[7][:, 0:8], XB[s][:, fc, 128 * blk:128 * blk + 128], WF[:, fc, :], fc == 0,
                               fc == 7, [f"XB{s}", "WF"], [P(7)], inc=(fc == 7))
                        stt("dve", ZF[:], PB[7][:, 0:8], RTK[s][:, blk:blk + 1], BFG[:], ALU.mult, ALU.add,
                            [P(7), f"RTK{s}", "BFG"], ["ZF"])
                        act(EF[:], ZF[:], AF.Exp, ["ZF"], ["EF"], scale=-1.0)
                        act(SPF[sp_][:], EF[:], AF.Ln, ["EF"], [f"SPF{sp_}"], bias=1.0)

                def cs_part(ci):
                    for blk in range(2):
                        kb = 2 * ci + blk
                        sp_ = kb % 4
                        mm(PB[7][:, 8:16], UF[:], SPF[sp_][:], True, False, ["UF", f"SPF{sp_}"], [P(7)], inc=False)
                        mm(PB[7][:, 8:16], ONEF[:], ACCF[:], False, True, ["ONEF", "ACCF"], [P(7)])
                        cp("dve", CNEG[:, kb, :], PB[7][:, 8:16], [P(7)], ["CNEG"])
                        tt("dve", ACCF[:], ACCF[:], SPF[sp_][:], ALU.add, ["ACCF", f"SPF{sp_}"], ["ACCF"])

                for ci in range(32):
                    s = ci % 2
                    prefetch_next(8 + ci)
                    load_chunk(xfT, 256 * ci, 256, s)
                    for blk in range(2):
                        for fc in range(8):
                            mm(PB[7][:, 16 + blk:17 + blk], SQ[s][:, fc, 128 * blk:128 * blk + 128], ONEB[:, 0:1],
                               fc == 0, fc == 7, [f"SQ{s}", "ONEB"], [P(7)], inc=(fc == 7))
                    TMPT = TMPTs[s]
                    act(TMPT[:], PB[7][:, 16:18], AF.Ln, [P(7), "EPST"], [f"TMPT{s}"], bias=EPST[:], scale=1.0 / 1024)
                    act(RTK[s][:], TMPT[:], AF.Exp, [f"TMPT{s}"], [f"RTK{s}"], scale=-0.5)
                    for j in range(2):
                        bk = j
                        for fc in range(8):
                            mm(PB[bk][:, 0:256], WK[:, fc, 128 * j:128 * j + 128], XB[s][:, fc, 0:256], fc == 0,
                               fc == 7, ["WK", f"XB{s}"], [P(bk)], inc=(fc == 7))
                        tt("dve", KT[j][:, 256 * ci:256 * ci + 256], PB[bk][:, 0:256], RBC[s][:, 0:256], ALU.mult,
                           [P(bk), f"RBC{s}"], [f"KT{j}"])
                    for blk in range(2):
                        kb = 2 * ci + blk
                        bk = 2 + blk
                        for fc in range(8):
                            mm(PB[bk][:, 0:256], XB[s][:, fc, 128 * blk:128 * blk + 128], WV[:, fc, :], fc == 0,
                               fc == 7, [f"XB{s}", "WV"], [P(bk)], inc=(fc == 7))
                        ts("dve", VP[:, kb, :, 0:64], PB[bk][:, 0:256].rearrange("p (h d) -> p h d", h=4),
                           RTK[s][:, blk:blk + 1], None, ALU.mult, None, [P(bk), f"RTK{s}"], ["VP"])
                    if p == 0:
                        fl_part(ci, s)
                        if ci >= 1:
                            cs_part(ci - 1)

                if p == 0:
                    cs_part(31)

                if p == 0:
                    for hf in range(2):
                        src = CNEG[:, 32 * hf:32 * hf + 32, :]
                        d0 = SPL3[:, 32 * hf:32 * hf + 32, :, 0]
                        d1 = SPL3[:, 32 * hf:32 * hf + 32, :, 1]
                        d2 = SPL3[:, 32 * hf:32 * hf + 32, :, 2]
                        ta = ET[0][:, 0:256].rearrange("p (k h) -> p k h", h=8)
                        tb = ET[1][:, 0:256].rearrange("p (k h) -> p k h", h=8)
                        cp("dve", d0, src, ["CNEG"], ["SPL3"])
                        tt("dve", ta, src, d0, ALU.subtract, ["CNEG", "SPL3"], ["ET0"])
                        cp("dve", d1, ta, ["ET0"], ["SPL3"])
                        tt("dve", tb, ta, d1, ALU.subtract, ["ET0", "SPL3"], ["ET1"])
                        cp("dve", d2, tb, ["ET1"], ["SPL3"])
                if fox:
                    for g4 in range(4):
                        for ib in range(4):
                            i = 4 * g4 + ib
                            srcs = [(a, 4 * i + a - 1) for a in range(4) if 4 * i + a - 1 >= 0]
                            for n_, (a, kbs) in enumerate(srcs):
                                mm(PB[6][0:12, 128 * ib:128 * ib + 128],
                                   SPL3[:, kbs, 4 * p:4 * p + 4, :].rearrange("p h j -> p (h j)"), ESEL[:, a, :],
                                   n_ == 0, n_ == len(srcs) - 1, ["SPL3", "ESEL"], [P(6)],
                                   inc=(n_ == len(srcs) - 1), skip=True)
                        cp("dve", QAUG[:, 512 * g4:512 * g4 + 512], PB[6][0:12, :], [P(6)], ["QAUG"])
                        cp("dve", QAUG[:, 2048 + 8 * g4:2048 + 8 * g4 + 8].rearrange("p (b c) -> p b c", c=2),
                           PB[6][0:12, :].rearrange("p (b c) -> p b c", c=128)[:, :, 0:2], [P(6)], ["QAUG"])

                mk = 0 if fox else 1
                tiles = []
                for hl in range(4):
                    for g in range(5):
                        halo = g == 4
                        kbs = list(range(64)) if halo else list(range(16 * g + 16))
                        if not fox:
                            kbs = kbs[::-1]
                        for n_, kb in enumerate(kbs):
                            j = kb // 4
                            if halo:
                                c0, n = 2048 + 2 * j, 32 - 2 * j
                                diag = True
                            else:
                                a = max(0, j - 4 * g)
                                c0, n = 512 * g + 128 * a, 512 - 128 * a
                                diag = j >= 4 * g
                            tiles.append(dict(hl=hl, g=g, kb=kb, c0=c0, n=n, diag=diag, first=(n_ == 0),
                                              last=(n_ == len(kbs) - 1), halo=halo,
                                              cbase=(2048 if halo else 512 * g)))
                for t_, tl in enumerate(tiles):
                    tl["idx"] = t_
                chain_no = -1
                for tl in tiles:
                    if tl["first"]:
                        chain_no += 1
                    tl["chain"] = chain_no

                def kq_mm(tl, bank, extra_last, extra_reads=()):
                    hl, kb, c0, n = tl["hl"], tl["kb"], tl["c0"], tl["n"]
                    j2, r0 = hl // 2, 64 * (hl % 2)
                    o = c0 - tl["cbase"]
                    steps = [(KT[j2][r0:r0 + 64, 128 * kb:128 * kb + 128], QT[j2][r0:r0 + 64, c0:c0 + n],
                              PB[bank][:, o:o + n], [f"KT{j2}", f"QT{j2}"])]
                    if tl["diag"]:
                        if tl["halo"]:
                            w = min(4, n)
                            steps.append((IDB[:], MSK[:, mk, kb % 4, 128:128 + w], PB[bank][:, o:o + w], ["IDB", "MSK"]))
                        else:
                            steps.append((IDB[:], MSK[:, mk, kb % 4, 0:128], PB[bank][:, o:o + 128], ["IDB", "MSK"]))
                    for n_, (l_, r_, o_, rd) in enumerate(steps):
                        lastst = (n_ == len(steps) - 1)
                        mm(o_, l_, r_, (n_ == 0) and not fox, lastst and extra_last,
                           rd + (list(extra_reads) if n_ == 0 else []), [P(bank)],
                           inc=(lastst and extra_last), skip=True)

                def abank(tl):
                    return tl["idx"] % (4 if fox else 3)

                def fox_seed(tl):
                    if True:
                        c2 = tl["chain"] % 2
                        b = abank(tl)
                        if tl["first"]:
                            nfull = 32 if tl["halo"] else 512
                            cb_ = tl["cbase"]
                            mm(PB[6][:, 0:nfull], SELB[:, tl["hl"], :], QAUG[:, cb_:cb_ + nfull], True, True,
                               ["SELB", "QAUG"], [P(6)])
                            cp("dve", AUGT[c2][:, 0:nfull], PB[6][:, 0:nfull], [P(6)], [f"AUGT{c2}"])
                        o = tl["c0"] - tl["cbase"]
                        n = tl["n"]
                        cp("dve", PB[b][:, o:o + n], AUGT[c2][:, o:o + n], [f"AUGT{c2}"], [P(b)])

                def st_A(tl):
                    if fox:
                        fox_seed(tl)
                    kq_mm(tl, abank(tl), True)

                def st_fox_exp(tl):
                    hl, kb, n = tl["hl"], tl["kb"], tl["n"]
                    o = tl["c0"] - tl["cbase"]
                    b, s3 = abank(tl), tl["idx"] % 3
                    act(PT[s3][:, 0:n], PB[b][:, o:o + n], AF.Exp, [P(b), "CNEG"], [f"PT{s3}"],
                        bias=CNEG[:, kb, 4 * p + hl:4 * p + hl + 1])

                def st_O(tl, extra_reads=(), inc=True):
                    hl, kb, n = tl["hl"], tl["kb"], tl["n"]
                    o = tl["c0"] - tl["cbase"]
                    s3 = tl["idx"] % 3
                    ob = 4 + tl["chain"] % 2
                    mm(PB[ob][0:65, o:o + n], VP[:, kb, hl, :], PT[s3][:, 0:n], tl["first"], tl["last"],
                       ["VP", f"PT{s3}"] + list(extra_reads), [P(ob)], inc=(inc or tl["last"]), skip=True)
                    if tl["last"]:
                        finalize(tl)

                def finalize(tl):
                    hl = tl["hl"]
                    ob = 4 + tl["chain"] % 2
                    n = 32 if tl["halo"] else 512
                    cb = tl["cbase"]
                    gh = 4 * p + hl
                    pair, r0 = gh // 2, 64 * (gh % 2)
                    dst = OT[r0:r0 + 64, pair, cb:cb + n]
                    if fox:
                        ts("dve", LR[64:65, 0:n], PB[ob][64:65, 0:n], 1e-30, None, ALU.max, None, [P(ob)], ["LR"])
                        trk.op("dve", lambda: nc.vector.reciprocal(out=LR[64:65, 0:n], in_=LR[64:65, 0:n]),
                               ["LR"], ["LR"])
                        mm(PB[6][0:64, 0:n], ONEF[64:65, 0:64], LR[64:65, 0:n], True, True, ["ONEF", "LR"], [P(6)])
                        cp("dve", BC[0:64, 0:n], PB[6][0:64, 0:n], [P(6)], ["BC"])
                        tt("dve", dst, PB[ob][0:64, 0:n], BC[0:64, 0:n], ALU.mult, [P(ob), "BC"], ["OT"])
                    else:
                        cp("dve", dst, PB[ob][0:64, 0:n], [P(ob)], ["OT"])

                def st_sb_esp(tl):
                    n = tl["n"]
                    o = tl["c0"] - tl["cbase"]
                    b, s2 = abank(tl), tl["idx"] % 2
                    act(ET[s2][:, 0:n], PB[b][:, o:o + n], AF.Exp, [P(b)], [f"ET{s2}"])
                    act(SPB[s2][:, 0:n], ET[s2][:, 0:n], AF.Ln, [f"ET{s2}"], [f"SPB{s2}"], bias=1.0)

                def st_sb_B(tl):
                    n = tl["n"]
                    o = tl["c0"] - tl["cbase"]
                    s2 = tl["idx"] % 2
                    bank = abank(tl)
                    use_acc = not tl["first"]
                    mm(PB[bank][:, o:o + n], NEGU[:], SPB[s2][:, 0:n], False, not use_acc, ["NEGU", f"SPB{s2}"],
                       [P(bank)], inc=(not use_acc), skip=True)
                    if use_acc:
                        ab = tl["idx"] % 2
                        mm(PB[bank][:, o:o + n], NONEB[:], ACB[ab][:, o:o + n], False, True, ["NONEB", f"ACB{ab}"],
                           [P(bank)], skip=True)

                def st_sb_acc(tl):
                    n = tl["n"]
                    o = tl["c0"] - tl["cbase"]
                    s2 = tl["idx"] % 2
                    if tl["last"]:
                        return
                    nb = (tl["idx"] + 1) % 2
                    if tl["first"]:
                        mset("dve", ACC[:], 0.0, ["ACC"])
                    tt("dve", ACC[:, o:o + n], ACC[:, o:o + n], SPB[s2][:, 0:n], ALU.add, ["ACC", f"SPB{s2}"], ["ACC"])
                    nx = tiles[tl["idx"] + 1]
                    o2 = nx["c0"] - nx["cbase"]
                    cp("dve", ACB[nb][:, o2:o2 + nx["n"]], ACC[:, o2:o2 + nx["n"]], ["ACC"], [f"ACB{nb}"])

                def st_sb_exp2(tl):
                    n = tl["n"]
                    o = tl["c0"] - tl["cbase"]
                    bank = abank(tl)
                    s3 = tl["idx"] % 3
                    act(PT[s3][:, 0:n], PB[bank][:, o:o + n], AF.Exp, [P(bank)], [f"PT{s3}"])

                if fox:
                    mset("dve", ACB[0][:], 0.0, ["ACB0"])
                    for b_ in range(4):
                        mm(PB[b_][:, :], IDB[:], ACB[0][:, :], True, True, ["IDB", "ACB0"], [P(b_)])
                    sched = [(st_A, 0), (st_fox_exp, 2), (st_O, 2)]
                else:
                    sched = [(st_A, 0), (st_sb_esp, 1), (st_sb_B, 1), (st_sb_acc, 1), (st_sb_exp2, 2), (st_O, 2)]
                if not fox:
                    ptiles = []
                    for j2 in range(2):
                        for g in range(5):
                            halo = g == 4
                            kbs = (list(range(64)) if halo else list(range(16 * g + 16)))[::-1]
                            for n_, kb in enumerate(kbs):
                                j = kb // 4
                                if halo:
                                    c0, n = 2048 + 2 * j, 32 - 2 * j
                                    diag = True
                                else:
                                    a = max(0, j - 4 * g)
                                    c0, n = 512 * g + 128 * a, 512 - 128 * a
                                    diag = j >= 4 * g
                                ptiles.append(dict(j2=j2, g=g, kb=kb, c0=c0, n=n, diag=diag, first=(n_ == 0),
                                                   last=(n_ == len(kbs) - 1), halo=halo,
                                                   cbase=(2048 if halo else 512 * g)))
                    ch = -1
                    for t_, tl in enumerate(ptiles):
                        tl["idx"] = t_
                        if tl["first"]:
                            ch += 1
                        tl["chain"] = ch

                    def pA(tl):
                        j2, kb, c0, n = tl["j2"], tl["kb"], tl["c0"], tl["n"]
                        o = c0 - tl["cbase"]
                        bA = 2 * (tl["idx"] % 3)
                        steps = []
                        for h in range(2):
                            steps.append((KT[j2][64 * h:64 * h + 64, 128 * kb:128 * kb + 128],
                                          QT[j2][64 * h:64 * h + 64, c0:c0 + n], PB[bA + h][:, o:o + n],
                                          [f"KT{j2}", f"QT{j2}"], bA + h, True))
                        if tl["diag"]:
                            for h in range(2):
                                if tl["halo"]:
                                    w = min(4, n)
                                    steps.append((IDB[:], MSK[:, 1, kb % 4, 128:128 + w], PB[bA + h][:, o:o + w],
                                                  ["IDB", "MSK"], bA + h, False))
                                else:
                                    steps.append((IDB[:], MSK[:, 1, kb % 4, 0:128], PB[bA + h][:, o:o + 128],
                                                  ["IDB", "MSK"], bA + h, False))
                        for n_, (l_, r_, o_, rd, bk_, st_) in enumerate(steps):
                            lastst = n_ == len(steps) - 1
                            mm(o_, l_, r_, st_, lastst, rd, [P(bk_)], inc=lastst, skip=True)

                    def pESP(tl):
                        n = tl["n"]
                        o = tl["c0"] - tl["cbase"]
                        bA = 2 * (tl["idx"] % 3)
                        s2 = tl["idx"] % 2
                        act(ETp[s2][:, :, 0:n], PALL[:, bA:bA + 2, o:o + n], AF.Exp, [P(bA), P(bA + 1)], [f"ETp{s2}"])
                        act(SPBp[s2][:, :, 0:n], ETp[s2][:, :, 0:n], AF.Ln, [f"ETp{s2}"], [f"SPBp{s2}"], bias=1.0)

                    def pB(tl):
                        n = tl["n"]
                        o = tl["c0"] - tl["cbase"]
                        bA = 2 * (tl["idx"] % 3)
                        s2 = tl["idx"] % 2
                        use_acc = not tl["first"]
                        for h in range(2):
                            lastm = (h == 1) and not use_acc
                            mm(PB[bA + h][:, o:o + n], NEGU[:], SPBp[s2][:, h, 0:n], False, lastm,
                               ["NEGU", f"SPBp{s2}"], [P(bA + h)], inc=lastm, skip=True)
                        if use_acc:
                            ab = tl["idx"] % 2
                            for h in range(2):
                                mm(PB[bA + h][:, o:o + n], NONEB[:], ACBp[ab][:, h, o:o + n], False, h == 1,
                                   ["NONEB", f"ACBp{ab}"], [P(bA + h)], inc=(h == 1), skip=True)

                    def pACC(tl):
                        n = tl["n"]
                        o = tl["c0"] - tl["cbase"]
                        s2 = tl["idx"] % 2
                        if tl["last"]:
                            return
                        nb = (tl["idx"] + 1) % 2
                        if tl["first"]:
                            mset("dve", ACCp[:], 0.0, ["ACCp"])
                        tt("dve", ACCp[:, :, o:o + n], ACCp[:, :, o:o + n], SPBp[s2][:, :, 0:n], ALU.add,
                           ["ACCp", f"SPBp{s2}"], ["ACCp"])
                        nx = ptiles[tl["idx"] + 1]
                        o2 = nx["c0"] - nx["cbase"]
                        cp("dve", ACBp[nb][:, :, o2:o2 + nx["n"]], ACCp[:, :, o2:o2 + nx["n"]], ["ACCp"],
                           [f"ACBp{nb}"])

                    def pEXP2(tl):
                        n = tl["n"]
                        o = tl["c0"] - tl["cbase"]
                        bA = 2 * (tl["idx"] % 3)
                        s3 = tl["idx"] % 3
                        act(PTp[s3][:, :, 0:n], PALL[:, bA:bA + 2, o:o + n], AF.Exp, [P(bA), P(bA + 1)], [f"PTp{s3}"])

                    def pO(tl):
                        j2, kb, n = tl["j2"], tl["kb"], tl["n"]
                        o = tl["c0"] - tl["cbase"]
                        s3 = tl["idx"] % 3
                        ob = 6 + tl["chain"] % 2
                        mm(PALL[0:64, ob, o:o + n], VP[:, kb, 2 * j2, 0:64], PTp[s3][:, 0, 0:n], tl["first"], tl["last"],
                           ["VP", f"PTp{s3}"], [P(ob)], inc=False, skip=True)
                        mm(PALL[64:128, ob, o:o + n], VP[:, kb, 2 * j2 + 1, 0:64], PTp[s3][:, 1, 0:n], tl["first"],
                           tl["last"], ["VP", f"PTp{s3}"], [P(ob)], inc=True, skip=True, tpos=(0, 64))
                        if tl["last"]:
                            nf = 32 if tl["halo"] else 512
                            cb_ = tl["cbase"]
                            cp("dve", OT[:, 2 * p + j2, cb_:cb_ + nf], PB[ob][:, 0:nf], [P(ob)], ["OT"])

                    psched = [(pA, 0), (pESP, 1), (pB, 1), (pACC, 1), (pEXP2, 2), (pO, 2)]
                    npt = len(ptiles)
                    for T in range(npt + 3):
                        for fn, skew in psched:
                            k = T - skew
                            if 0 <= k < npt:
                                fn(ptiles[k])
                else:
                    nt = len(tiles)
                    assert nt % 2 == 0
                    npairs = nt // 2
                    for Q in range(npairs + 1):
                        if Q < npairs:
                            t0, t1 = tiles[2 * Q], tiles[2 * Q + 1]
                            fox_seed(t0)
                            fox_seed(t1)
                            kq_mm(t0, abank(t0), False, extra_reads=[P(abank(t1))])
                            kq_mm(t1, abank(t1), True)
                        if Q >= 1:
                            t0, t1 = tiles[2 * Q - 2], tiles[2 * Q - 1]
                            st_fox_exp(t0)
                            st_fox_exp(t1)
                            st_O(t0, extra_reads=[f"PT{t1['idx'] % 3}"], inc=False)
                            st_O(t1)
                trk.barrier()

        with ExitStack() as satt:
            CNEG = sbt(satt, "CNEG", [128, 64, 8], F32)
            SPL3 = sbt(satt, "SPL3", [128, 64, 8, 3], BF16)
            for p in range(4):
                attention_pass(p)

        chunks = [(256 * c, 256) for c in range(8)] + [(2048, 32)]
        with ExitStack() as pha:
            XT = sbt(pha, "XT", [128, 8, NOWN], F32)
            trk.dma("xt", XT[:, 0:4, :], xoT[:, 0:4, :], (), ["XT"])
            trk.dma("xt2", XT[:, 4:8, :], xoT[:, 4:8, :], (), ["XT"])
            with ExitStack() as ph:
                KMT = sbt(ph, "KMT", [128, 4, 2, 256], BF16)
                VM = sbt(ph, "VM", [128, 2, 1024], BF16)
                with ExitStack() as phm:
                    mem_prep(phm, KMT, VM)
                    trk.barrier()
                WO = sbt(ph, "WO", [128, 8, 1024], BF16)
                WMQ = sbt(ph, "WMQ", [128, 8, 1024], BF16)
                WMO = sbt(ph, "WMO", [128, 8, 1024], BF16)
                WS = [sbt(ph, f"WS{i}", [128, 512], F32) for i in range(2)]
                SQO = sbt(ph, "SQO", [128, 8, 256], BF16)
                OTS = sbt(ph, "OTS", [128, 8, 256], BF16)
                RF = sbt(ph, "RF", [128, 256], F32)
                RS = sbt(ph, "RS", [128, 256], F32)
                R2 = sbt(ph, "R2", [128, 256], F32)
                TMP = sbt(ph, "TMP", [128, 256], F32)
                T1 = [sbt(ph, f"T1{i}", [128, 256], F32) for i in range(2)]
                SQX = sbt(ph, "SQX", [128, 8, 256], BF16)
                H2 = sbt(ph, "H2", [128, 8, 256], BF16)
                QM = sbt(ph, "QM", [128, 8, 256], BF16)
                PM = sbt(ph, "PM", [128, 8, 256], BF16)
                RL = sbt(ph, "RL", [128, 256], F32)
                OM = sbt(ph, "OM", [128, 8, 256], BF16)
                k = 0
                for (wsrc, wdst, nm) in ((wout, WO, "WO"), (wmq, WMQ, "WMQ"), (wmo, WMO, "WMO")):
                    for fc in range(8):
                        for hf in range(2):
                            s = k % 2
                            k += 1
                            trk.dma(f"ws{s}", WS[s][:], wsrc[:, fc, 512 * hf:512 * hf + 512], (), [f"WS{s}"])
                            dst_ = wdst[:, fc, 512 * hf:512 * hf + 512]
                            if nm == "WO":
                                act(dst_, WS[s][:], AF.Copy, [f"WS{s}", "GOUT"], [nm], scale=GOUT[:, fc:fc + 1])
                            elif nm == "WMQ":
                                act(dst_, WS[s][:], AF.Copy, [f"WS{s}", "G16"], [nm], scale=G16[:, fc:fc + 1])
                            else:
                                cp("act", dst_, WS[s][:], [f"WS{s}"], [nm])

                for (c0, n) in chunks:
                    act(SQO[:, :, 0:n], OT[:, :, c0:c0 + n], AF.Square, ["OT"], ["SQO"])
                    for grp, dst, nm in ((0, RF, "RF"), (1, RS, "RS")):
                        for q_ in range(4):
                            mm(PB[6][:, 0:n], ONEB[:], SQO[:, 4 * grp + q_, 0:n], q_ == 0, q_ == 3, ["ONEB", "SQO"],
                               [P(6)], inc=(q_ == 3))
                        act(TMP[:, 0:n], PB[6][:, 0:n], AF.Ln, [P(6), "EPST"], ["TMP"], bias=EPST[:], scale=1.0 / 512)
                        act(dst[:, 0:n], TMP[:, 0:n], AF.Exp, ["TMP"], [nm], scale=-0.5)
                    for q_ in range(8):
                        rr, nm = (RF, "RF") if q_ < 4 else (RS, "RS")
                        tt("dve", OTS[:, q_, 0:n], OT[:, q_, c0:c0 + n], rr[:, 0:n], ALU.mult, ["OT", nm], ["OTS"])
                    for fc in range(8):
                        bk = fc % 2
                        for q_ in range(8):
                            mm(PB[bk][:, 0:n], WO[:, q_, 128 * fc:128 * fc + 128], OTS[:, q_, 0:n], q_ == 0, q_ == 7,
                               ["WO", "OTS"], [P(bk)], inc=(q_ == 7))
                        tt("dve", XT[:, fc, c0:c0 + n], PB[bk][:, 0:n], XT[:, fc, c0:c0 + n], ALU.add,
                           [P(bk), "XT"], ["XT"])
                    act(SQX[:, :, 0:n], XT[:, :, c0:c0 + n], AF.Square, ["XT"], ["SQX"])
                    for fc in range(8):
                        mm(PB[6][:, 0:n], ONEB[:], SQX[:, fc, 0:n], fc == 0, fc == 7, ["ONEB", "SQX"], [P(6)],
                           inc=(fc == 7))
                    act(TMP[:, 0:n], PB[6][:, 0:n], AF.Ln, [P(6), "EPST"], ["TMP"], bias=EPST[:], scale=1.0 / 1024)
                    act(R2[:, 0:n], TMP[:, 0:n], AF.Exp, ["TMP"], ["R2"], scale=-0.5)
                    for fc in range(8):
                        tt("dve", H2[:, fc, 0:n], XT[:, fc, c0:c0 + n], R2[:, 0:n], ALU.mult, ["XT", "R2"], ["H2"])
                    for cc in range(8):
                        bk = cc % 2
                        for fc in range(8):
                            mm(PB[bk][:, 0:n], WMQ[:, fc, 128 * cc:128 * cc + 128], H2[:, fc, 0:n], fc == 0, fc == 7,
                               ["WMQ", "H2"], [P(bk)], inc=(fc == 7))
                        cp("act", QM[:, cc, 0:n], PB[bk][:, 0:n], [P(bk)], ["QM"])
                    for h in range(4):
                        for mc in range(2):
                            bk = 2 + mc
                            for dc in range(2):
                                mm(PB[bk][:, 0:n], KMT[:, h, dc, 128 * mc:128 * mc + 128], QM[:, 2 * h + dc, 0:n],
                                   dc == 0, dc == 1, ["KMT", "QM"], [P(bk)], inc=(dc == 1))
                            act(PM[:, 2 * h + mc, 0:n], PB[bk][:, 0:n], AF.Exp, [P(bk)], ["PM"])
                        for mc in range(2):
                            mm(PB[6][:, 0:n], ONEB[:], PM[:, 2 * h + mc, 0:n], mc == 0, mc == 1, ["ONEB", "PM"],
                               [P(6)], inc=(mc == 1))
                        trk.op("dve", lambda: nc.vector.reciprocal(out=RL[:, 0:n], in_=PB[6][:, 0:n]), [P(6)], ["RL"])
                        for dc in range(2):
                            bk = dc
                            for mc in range(2):
                                mm(PB[bk][:, 0:n], VM[:, mc, 256 * h + 128 * dc:256 * h + 128 * dc + 128],
                                   PM[:, 2 * h + mc, 0:n], mc == 0, mc == 1, ["VM", "PM"], [P(bk)], inc=(mc == 1))
                            tt("dve", OM[:, 2 * h + dc, 0:n], PB[bk][:, 0:n], RL[:, 0:n], ALU.mult, [P(bk), "RL"],
                               ["OM"])
                    for fc in range(8):
                        bk = 4 + fc % 2
                        for cc in range(8):
                            mm(PB[bk][:, 0:n], WMO[:, cc, 128 * fc:128 * fc + 128], OM[:, cc, 0:n], cc == 0, cc == 7,
                               ["WMO", "OM"], [P(bk)], inc=(cc == 7))
                        tt("dve", XT[:, fc, c0:c0 + n], PB[bk][:, 0:n], XT[:, fc, c0:c0 + n], ALU.add,
                           [P(bk), "XT"], ["XT"])
                trk.barrier()

            with ExitStack() as ph:
                H3 = sbt(ph, "H3", [128, 8, NOWN], BF16)
                WU = sbt(ph, "WU", [128, 8, 2, 1408], BF16)
                WD = OT[:, :, :].rearrange("p a b -> p (a b)")[:, 0:11 * 1024].rearrange("p (c n) -> p c n", n=1024)
                WUs = [sbt(ph, f"WUs{i}", [128, 704], F32) for i in range(2)]
                WDs = [sbt(ph, f"WDs{i}", [128, 512], F32) for i in range(2)]
                UH = sbt(ph, "UH", [128, 22, 32], F32)
                FIX = sbt(ph, "FIX", [128, 22, 16, 2], F32)
                YG = [sbt(ph, f"YG{i}", [128, 256], F32) for i in range(2)]
                YV = [sbt(ph, f"YV{i}", [128, 256], F32) for i in range(2)]
                AT_ = [sbt(ph, f"ATt{i}", [128, 256], BF16) for i in range(3)]
                TMP4 = sbt(ph, "TMP4", [128, 256], F32)
                R4 = sbt(ph, "R4", [128, 256], F32)
                SQ4 = OT[:, :, :].rearrange("p a b -> p (a b)")[:, 11264:11264 + 2048].rearrange("p (c n) -> p c n", n=256)
                OS = [sbt(ph, f"OS{i}", [128, 256], F32) for i in range(2)]
                for (c0, n) in chunks:
                    act(SQ4[:, :, 0:n], XT[:, :, c0:c0 + n], AF.Square, ["XT"], ["SQ4"])
                    for fc in range(8):
                        mm(PB[6][:, 0:n], ONEB[:], SQ4[:, fc, 0:n], fc == 0, fc == 7, ["ONEB", "SQ4"], [P(6)],
                           inc=(fc == 7))
                    act(TMP4[:, 0:n], PB[6][:, 0:n], AF.Ln, [P(6), "EPST"], ["TMP4"], bias=EPST[:], scale=1.0 / 1024)
                    act(R4[:, 0:n], TMP4[:, 0:n], AF.Exp, ["TMP4"], ["R4"], scale=-0.5)
                    for fc in range(8):
                        tt("dve", H3[:, fc, c0:c0 + n], XT[:, fc, c0:c0 + n], R4[:, 0:n], ALU.mult, ["XT", "R4"],
                           ["H3"])
                for sw in range(2):
                    k = 0
                    for fc in range(8):
                        for gv in range(2):
                            for hf in range(2):
                                s = k % 2
                                k += 1
                                col = 2816 * gv + 1408 * sw + 704 * hf
                                trk.dma(f"wus{s}", WUs[s][:], wup[:, fc, col:col + 704], (), [f"WUs{s}"])
                                act(WU[:, fc, gv, 704 * hf:704 * hf + 704], WUs[s][:], AF.Copy, [f"WUs{s}", "GALL"],
                                    ["WU"], scale=GALL[:, 3, fc:fc + 1])
                    k = 0
                    for cl in range(11):
                        for hf in range(2):
                            s = k % 2
                            k += 1
                            trk.dma(f"wds{s}", WDs[s][:], wdown[:, 11 * sw + cl, 512 * hf:512 * hf + 512], (),
                                    [f"WDs{s}"])
                            cp("act", WD[:, cl, 512 * hf:512 * hf + 512], WDs[s][:], [f"WDs{s}"], ["WD"])
                    for q_ in range(22):
                        gv, cl = q_ // 11, q_ % 11
                        bk = q_ % 2
                        for fc in range(8):
                            mm(PB[bk][:, 0:32], WU[:, fc, gv, 128 * cl:128 * cl + 128], H3[:, fc, 2048:2080], fc == 0,
                               fc == 7, ["WU", "H3"], [P(bk)], inc=(fc == 7))
                        tt("dve", UH[:, q_, :], PB[bk][:, 0:32], HVAL[:], ALU.mult, [P(bk), "HVAL"], ["UH"])
                    for q_ in range(22):
                        gv, cl = q_ // 11, q_ % 11
                        cc = 22 * gv + 11 * sw + cl
                        uh = UH[:, q_, :].rearrange("p (b c) -> p b c", c=2)
                        ts("pool", FIX[:, q_, :, 1], uh[:, :, 1], CW[:, cc, 0:1], None, ALU.mult, None, ["UH", "CW"],
                           ["FIX"])
                        ts("pool", FIX[:, q_, :, 0], uh[:, :, 0], CW[:, cc, 0:1], None, ALU.mult, None, ["UH", "CW"],
                           ["FIX"])
                        stt("dve", FIX[:, q_, :, 0], uh[:, :, 1], CW[:, cc, 1:2], FIX[:, q_, :, 0], ALU.mult, ALU.add,
                            ["UH", "CW", "FIX"], ["FIX"])
                    for tc in range(8):
                        c0 = 256 * tc
                        units = list(range(11))

                        def u_stage(cl, tc=tc, c0=c0, sw=sw):
                            for gv in range(2):
                                bk = 2 * (cl % 2) + gv
                                for fc in range(8):
                                    mm(PB[bk][:, 0:256], WU[:, fc, gv, 128 * cl:128 * cl + 128], H3[:, fc, c0:c0 + 256],
                                       fc == 0, fc == 7, ["WU", "H3"], [P(bk)], inc=(fc == 7))

                        def conv_stage(cl, tc=tc, c0=c0, sw=sw):
                            s = cl % 2
                            for gv, Y in ((0, YG[s]), (1, YV[s])):
                                bk = 2 * (cl % 2) + gv
                                cc = 22 * gv + 11 * sw + cl
                                q_ = 11 * gv + cl
                                nm = ("YG" if gv == 0 else "YV") + str(s)
                                act(Y[:, :], PB[bk][:, 0:256], AF.Identity, [P(bk), "CW", "CB"], [nm],
                                    bias=CB[:, cc:cc + 1], scale=CW[:, cc, 2:3])
                                y3 = Y[:, :].rearrange("p (b c) -> p b c", c=128)
                                u3 = PB[bk][:, 0:256].rearrange("p (b c) -> p b c", c=128)
                                stt("dve", y3[:, :, 1:128], u3[:, :, 0:127], CW[:, cc, 1:2], y3[:, :, 1:128], ALU.mult,
                                    ALU.add, [P(bk), "CW", nm], [nm])
                                stt("dve", y3[:, :, 2:128], u3[:, :, 0:126], CW[:, cc, 0:1], y3[:, :, 2:128], ALU.mult,
                                    ALU.add, [P(bk), "CW", nm], [nm])
                                tt("pool", y3[:, :, 0:2], y3[:, :, 0:2], FIX[:, q_, 2 * tc:2 * tc + 2, :], ALU.add,
                                   [nm, "FIX"], [nm])
                            act(YG[s][:, :], YG[s][:, :], AF.Silu, [f"YG{s}"], [f"YG{s}"])
                            a3 = cl % 3
                            tt("dve", AT_[a3][:, :], YG[s][:, :], YV[s][:, :], ALU.mult, [f"YG{s}", f"YV{s}"],
                               [f"AT{a3}"])

                        def down_stage(cl, tc=tc, c0=c0, sw=sw):
                            a3 = cl % 3
                            for fc in range(8):
                                bk = 4 + fc // 2
                                o = 256 * (fc % 2)
                                mm(PB[bk][:, o:o + 256], WD[:, cl, 128 * fc:128 * fc + 128], AT_[a3][:, :],
                                   (cl == 0 and fc % 2 == 0), cl == 10, ["WD", f"AT{a3}"], [P(bk)],
                                   inc=(fc == 7 or cl == 10), skip=True)

                        for T in range(11 + 2):
                            if T < 11:
                                u_stage(T)
                            if 0 <= T - 1 < 11:
                                conv_stage(T - 1)
                            if 0 <= T - 2 < 11:
                                down_stage(T - 2)
                        for fc in range(8):
                            bk = 4 + fc // 2
                            o = 256 * (fc % 2)
                            tt("dve", XT[:, fc, c0:c0 + 256], PB[bk][:, o:o + 256], XT[:, fc, c0:c0 + 256], ALU.add,
                               [P(bk), "XT"], ["XT"])
                for tc in range(8):
                    c0 = 256 * tc
                    act(SQ4[:, :, :], XT[:, :, c0:c0 + 256], AF.Square, ["XT"], ["SQ4"])
                    for fc in range(8):
                        mm(PB[0][:, 0:256], ONEB[:], SQ4[:, fc, :], fc == 0, fc == 7, ["ONEB", "SQ4"], [P(0)],
                           inc=(fc == 7))
                    act(TMP4[:, :], PB[0][:, 0:256], AF.Ln, [P(0), "EPST"], ["TMP4"], bias=EPST[:], scale=1.0 / 1024)
                    act(R4[:, :], TMP4[:, :], AF.Exp, ["TMP4"], ["R4"], scale=-0.5)
                    for fc in range(8):
                        s = fc % 2
                        stt("dve", OS[s][:, :], XT[:, fc, c0:c0 + 256], GALL[:, 4, fc:fc + 1], R4[:, :], ALU.mult,
                            ALU.mult, ["XT", "GALL", "R4"], [f"OS{s}"])
                        trk.dma(f"os{s}", outT[:, fc, c0:c0 + 256], OS[s][:, :], [f"OS{s}"], ())
                trk.barrier()
    return nc


_CACHE = {}


def _consts():
    ident = np.eye(128, dtype=np.float32)
    j = np.arange(128)[:, None]
    s = np.arange(128)[None, :]
    negu = np.where(j >= s, -1.0, 0.0).astype(np.float32)
    u = np.where(j <= s, 1.0, 0.0).astype(np.float32)
    cmat = np.stack([ident, negu, u], axis=1)
    selc = np.zeros((12, 4, 128), np.float32)
    for h in range(4):
        selc[3 * h:3 * h + 3, h, :] = 1.0
    return np.ascontiguousarray(cmat), selc


def _masks(r):
    m = np.zeros((128, 2, 4, 132), np.float32)
    s = np.arange(128)[:, None]
    t = np.arange(128)[None, :]
    for kind in range(2):
        for a in range(4):
            if r > a:
                mm_ = np.zeros((128, 128), np.float32)
            elif r == a:
                ok = (s <= t) if kind == 0 else (s < t)
                mm_ = np.where(ok, 0.0, NEG).astype(np.float32)
            else:
                mm_ = np.full((128, 128), NEG, np.float32)
            m[:, kind, a, 0:128] = mm_
            for ii in range(2):
                hb = 4 * ii + r - 1
                for c in range(2):
                    pos = 126 + c
                    if hb > a:
                        col = np.zeros(128, np.float32)
                    elif hb == a:
                        ok = (np.arange(128) <= pos) if kind == 0 else (np.arange(128) < pos)
                        col = np.where(ok, 0.0, NEG).astype(np.float32)
                    else:
                        col = np.full(128, NEG, np.float32)
                    m[:, kind, a, 128 + 2 * ii + c] = col
    return m


def kernel(x, mem, attn_norm_g, w_in, b_forget, fox_out_g, sb_out_g, w_out, xattn_norm_g, mem_norm_g,
           w_mq, w_mkv, w_mo, ffn_norm_g, w_up, conv_w, conv_b, w_down, final_norm_g):
    f32 = np.float32
    x = np.asarray(x, f32)
    mem = np.asarray(mem, f32)

    def fm(w):
        w = np.asarray(w, f32)
        k, n = w.shape
        return np.ascontiguousarray(w.reshape(k // 128, 128, n).transpose(1, 0, 2))

    w_in = np.asarray(w_in, f32)[0]
    wq = []
    for p in range(4):
        base = 0 if p < 2 else 1536
        h0 = 4 * (p % 2)
        cols = np.concatenate([np.arange(base + 512 * t + 64 * h0, base + 512 * t + 64 * h0 + 256) for t in range(3)])
        wq.append(fm(w_in[:, cols]))
    wqkv = np.ascontiguousarray(np.stack(wq, 0))
    wf = fm(w_in[:, 3072:3080])

    def gv(g):
        return np.asarray(g, f32).reshape(8, 128).T

    gall = np.ascontiguousarray(np.stack([gv(attn_norm_g[0]), gv(xattn_norm_g[0]), gv(mem_norm_g[0]),
                                          gv(ffn_norm_g[0]), gv(final_norm_g)], axis=1))
    bfg = np.ascontiguousarray(np.broadcast_to(np.asarray(b_forget, f32)[0][None, :], (128, 8)))
    gout = np.ascontiguousarray(gv(np.concatenate([np.asarray(fox_out_g, f32)[0], np.asarray(sb_out_g, f32)[0]])))
    common = dict(
        wqkv=wqkv, wf=wf, gall=gall, bfg=bfg, gout=gout,
        wout=fm(w_out[0]), wmq=fm(w_mq[0]), wmkv=fm(w_mkv[0]), wmo=fm(w_mo[0]), wup=fm(w_up[0]),
        convw=np.ascontiguousarray(np.asarray(conv_w, f32)[0].reshape(3, 44, 128).transpose(2, 1, 0)),
        convb=np.ascontiguousarray(np.asarray(conv_b, f32)[0].reshape(44, 128).T),
        wdown=fm(w_down[0]),
    )
    cmat, selc = _consts()
    common["cmat"] = cmat
    common["selc"] = selc

    xT = [fm(x[b]) for b in range(2)]
    xT = [fm(np.ascontiguousarray(x[b].T)) for b in range(2)]
    mT = [fm(np.ascontiguousarray(mem[b].T)) for b in range(2)]
    in_maps = []
    for c in range(8):
        b, r = c // 4, c % 4
        xo = np.zeros((128, 8, NOWN), f32)
        for i in range(16):
            t0 = 128 * (4 * i + r)
            xo[:, :, 128 * i:128 * i + 128] = xT[b][:, :, t0:t0 + 128]
            if t0 >= 2:
                xo[:, :, 2048 + 2 * i:2048 + 2 * i + 2] = xT[b][:, :, t0 - 2:t0]
        es_ = np.zeros((128, 4, 128), f32)
        es_[127, r, :] = -1.0
        hv = np.ones((128, 32), f32)
        if r == 0:
            hv[:, 0:2] = 0.0
        d = dict(common)
        d.update(xfT=xT[b], xoT=xo, memT=mT[b], masks=_masks(r), esel=es_, hvalid=hv)
        in_maps.append(d)

    if "nc" not in _CACHE:
        _CACHE["nc"] = build_program()
    res = run_bass_kernel_spmd(_CACHE["nc"], in_maps, core_ids=list(range(8)))
    out = np.zeros((2, S, 1024), f32)
    for c in range(8):
        b, r = c // 4, c % 4
        o = res.results[c]["outT"]
        o = o.transpose(2, 1, 0).reshape(2048, 1024)
        for i in range(16):
            t0 = 128 * (4 * i + r)
            out[b, t0:t0 + 128, :] = o[128 * i:128 * i + 128]
    return out
```

```python
import numpy as np
from contextlib import ExitStack
import concourse.bass as bass
import concourse.mybir as mybir
from concourse.bass_utils import run_bass_kernel_spmd

F32 = mybir.dt.float32
BF16 = mybir.dt.bfloat16
AF = mybir.ActivationFunctionType
ALU = mybir.AluOpType

S = 8192
NOWN = 2080
EPS = 1e-6
NEG = -30000.0


class Buf:
    __slots__ = ("w", "r")

    def __init__(self):
        self.w = None
        self.r = {}


class Trk:
    EPOCH = 30000

    def __init__(self, nc, es):
        self.nc = nc
        self.es = es
        self.eng = {"pe": nc.tensor, "act": nc.scalar, "dve": nc.vector, "pool": nc.gpsimd, "sp": nc.sync}
        self.sems = {e: [] for e in ("pe", "act", "dve", "pool")}
        self.cnt = {e: 0 for e in ("pe", "act", "dve", "pool")}
        self.pending = {e: False for e in ("pe", "act", "dve", "pool")}
        self.dsem = {}
        self.dcnt = {}
        self.know = {e: {} for e in self.eng}
        self.snap = {}
        self.bufs = {}
        self.nwait = 0
        self.nins = 0

    def _newsem(self, name):
        return self.es.enter_context(self.nc.semaphore(name))

    def _next_tok(self, e):
        c = self.cnt[e]
        ep, v = c // self.EPOCH, c % self.EPOCH + 1
        while len(self.sems[e]) <= ep:
            self.sems[e].append(self._newsem(f"s_{e}{len(self.sems[e])}"))
        return (e, ep, v)

    def _sem_of(self, tok):
        p, ep, v = tok
        if p.startswith("d:"):
            return self.dsem[p]
        return self.sems[p][ep]

    def _need(self, e, tok):
        return self.know[e].get(tok[0], (-1, 0)) < (tok[1], tok[2])

    def _learn(self, e, tok):
        k = self.know[e]
        sn = self.snap.get(tok)
        if sn:
            for p, val in sn.items():
                if k.get(p, (-1, 0)) < val:
                    k[p] = val
        if k.get(tok[0], (-1, 0)) < (tok[1], tok[2]):
            k[tok[0]] = (tok[1], tok[2])

    def _collect(self, e, reads, writes):
        cand = []
        for k in reads:
            b = self.bufs.get(k)
            if b is not None and b.w is not None:
                cand.append(b.w)
        for k in writes:
            b = self.bufs.get(k)
            if b is None:
                continue
            if b.w is not None and (b.w[0] != e or e != "pe"):
                cand.append(b.w)
            for p, t in b.r.items():
                if p != e or e != "pe":
                    cand.append(t)
        cand.sort(key=lambda t: (t[1], t[2]), reverse=True)
        needed = []
        for tok in cand:
            if self._need(e, tok):
                needed.append(tok)
                self._learn(e, tok)
        return needed

    def _emit_waits(self, e, needed, ins_fn):
        for tok in needed[:-1]:
            self.eng[e].wait_ge(self._sem_of(tok), tok[2])
            self.nwait += 1
        ins = ins_fn()
        if needed:
            tok = needed[-1]
            ins._wait_ge(self._sem_of(tok), tok[2])
        return ins

    def _record(self, tok, reads, writes):
        for k in reads:
            self.bufs.setdefault(k, Buf()).r[tok[0]] = tok
        for k in writes:
            b = self.bufs.setdefault(k, Buf())
            b.w = tok
            b.r = {}

    def op(self, e, fn, reads=(), writes=(), inc=True):
        needed = self._collect(e, reads, writes)
        ins = self._emit_waits(e, needed, fn)
        tok = self._next_tok(e)
        if inc:
            ins.then_inc(self.sems[e][tok[1]], 1)
            self.cnt[e] += 1
            self.pending[e] = False
            sn = dict(self.know[e])
            sn[e] = (tok[1], tok[2])
            self.snap[tok] = sn
        else:
            self.pending[e] = True
        self._record(tok, reads, writes)
        self.nins += 1
        return ins

    def dma(self, lane, out, in_, reads=(), writes=()):
        lane = "d:" + lane
        if lane not in self.dsem:
            self.dsem[lane] = self._newsem("s_" + lane.replace(":", "_"))
            self.dcnt[lane] = 0
        needed = self._collect("sp", reads, writes)
        ins = self._emit_waits("sp", needed, lambda: self.nc.sync.dma_start(out=out, in_=in_))
        self.dcnt[lane] += 1
        tok = (lane, 0, 16 * self.dcnt[lane])
        ins.then_inc(self.dsem[lane], 16)
        sn = dict(self.know["sp"])
        sn[lane] = (0, tok[2])
        self.snap[tok] = sn
        self._record(tok, reads, writes)
        self.nins += 1

    def barrier(self):
        for e in self.eng:
            toks = []
            for p in self.cnt:
                assert not self.pending[p]
                if p != e and self.cnt[p] > 0:
                    c = self.cnt[p] - 1
                    toks.append((p, c // self.EPOCH, c % self.EPOCH + 1))
            for lane, n in self.dcnt.items():
                if n > 0:
                    toks.append((lane, 0, 16 * n))
            for tok in toks:
                if self._need(e, tok):
                    self.eng[e].wait_ge(self._sem_of(tok), tok[2])
                    self.nwait += 1
                    self._learn(e, tok)
        self.bufs = {}
        self.snap = {}


def build_program():
    nc = bass.Bass("TRN2", target_bir_lowering=False)

    def din(name, shape):
        return nc.dram_tensor(name, list(shape), F32, kind="ExternalInput").ap()

    xfT = din("xfT", [128, 8, S])
    xoT = din("xoT", [128, 8, NOWN])
    memT = din("memT", [128, 8, 256])
    wqkv = din("wqkv", [4, 128, 8, 768])
    wf = din("wf", [128, 8, 8])
    gall = din("gall", [128, 5, 8])
    bfg = din("bfg", [128, 8])
    gout = din("gout", [128, 8])
    wout = din("wout", [128, 8, 1024])
    wmq = din("wmq", [128, 8, 1024])
    wmkv = din("wmkv", [128, 8, 2048])
    wmo = din("wmo", [128, 8, 1024])
    wup = din("wup", [128, 8, 5632])
    convw = din("convw", [128, 44, 3])
    convb = din("convb", [128, 44])
    wdown = din("wdown", [128, 22, 1024])
    masks = din("masks", [128, 2, 4, 132])
    esel = din("esel", [128, 4, 128])
    selc = din("selc", [12, 4, 128])
    cmat = din("cmat", [128, 3, 128])
    hvalid = din("hvalid", [128, 32])
    outT = nc.dram_tensor("outT", [128, 8, 2048], F32, kind="ExternalOutput").ap()

    with ExitStack() as es:
        trk = Trk(nc, es)

        uid = [0]

        def sbt(st, name, shape, dt):
            uid[0] += 1
            return st.enter_context(nc.sbuf_tensor(f"{name}_{uid[0]}", list(shape), dt))

        PALL = es.enter_context(nc.psum_tensor("pall", [128, 8, 512], F32))
        PB = [PALL[:, i, :] for i in range(8)]

        def P(i):
            return ("ps", i)

        def mm(out, lhsT, rhs, start, stop, reads, writes, inc=True, skip=False, tpos=None):
            if tpos is None:
                trk.op("pe", lambda: nc.tensor.matmul(out, lhsT, rhs, start=start, stop=stop,
                                                      skip_group_check=skip), reads, writes, inc=inc)
            else:
                trk.op("pe", lambda: nc.tensor.matmul(out, lhsT, rhs, start=start, stop=stop,
                                                      skip_group_check=skip, tile_position=tpos),
                       reads, writes, inc=inc)

        def act(out, in_, func, reads, writes, bias=None, scale=None):
            kw = {}
            if bias is not None:
                kw["bias"] = bias
            if scale is not None:
                kw["scale"] = scale
            trk.op("act", lambda: nc.scalar.activation(out=out, in_=in_, func=func, **kw), reads, writes)

        def veng(e):
            return nc.vector if e == "dve" else nc.gpsimd

        def tt(e, out, in0, in1, op, reads, writes):
            trk.op(e, lambda: veng(e).tensor_tensor(out=out, in0=in0, in1=in1, op=op), reads, writes)

        def ts(e, out, in0, s1, s2, op0, op1, reads, writes):
            if s2 is None:
                trk.op(e, lambda: veng(e).tensor_scalar(out=out, in0=in0, scalar1=s1, scalar2=None, op0=op0),
                       reads, writes)
            else:
                trk.op(e, lambda: veng(e).tensor_scalar(out=out, in0=in0, scalar1=s1, scalar2=s2, op0=op0,
                                                        op1=op1), reads, writes)

        def stt(e, out, in0, scalar, in1, op0, op1, reads, writes):
            trk.op(e, lambda: veng(e).scalar_tensor_tensor(out=out, in0=in0, scalar=scalar, in1=in1,
                                                           op0=op0, op1=op1), reads, writes)

        def cp(e, out, in_, reads, writes):
            if e == "act":
                act(out, in_, AF.Copy, reads, writes)
            else:
                trk.op(e, lambda: veng(e).tensor_copy(out=out, in_=in_), reads, writes)

        def mset(e, ap, val, writes):
            trk.op(e, lambda: veng(e).memset(ap, val), (), writes)

        OT = sbt(es, "OT", [128, 8, NOWN], BF16)
        IDB = sbt(es, "IDB", [128, 128], BF16)
        NEGU = sbt(es, "NEGU", [128, 128], BF16)
        UF = sbt(es, "UF", [128, 128], F32)
        ONEB = sbt(es, "ONEB", [128, 128], BF16)
        NONEB = sbt(es, "NONEB", [128, 128], BF16)
        ONEF = sbt(es, "ONEF", [128, 128], F32)
        MSK = sbt(es, "MSK", [128, 2, 4, 132], BF16)
        ESEL = sbt(es, "ESEL", [128, 4, 128], BF16)
        SELB = sbt(es, "SELB", [12, 4, 128], BF16)
        GALL = sbt(es, "GALL", [128, 5, 8], F32)
        GOUT = sbt(es, "GOUT", [128, 8], F32)
        BFG = sbt(es, "BFG", [128, 8], F32)
        HVAL = sbt(es, "HVAL", [128, 32], F32)
        EPST = sbt(es, "EPST", [128, 1], F32)
        G8 = sbt(es, "G8", [128, 8], F32)
        G16 = sbt(es, "G16", [128, 8], F32)
        CW = sbt(es, "CW", [128, 44, 3], F32)
        CB = sbt(es, "CB", [128, 44], F32)

        with ExitStack() as ph:
            STG = sbt(ph, "STG0", [128, 2, 4, 132], F32)
            CM = sbt(ph, "CM0", [128, 3, 128], F32)
            ES0 = sbt(ph, "ES0", [128, 4, 128], F32)
            SL0 = sbt(ph, "SL0", [12, 4, 128], F32)
            trk.dma("c0", STG[:], masks[:, :, :, :], (), ["STG"])
            trk.dma("c1", CM[:], cmat[:, :, :], (), ["CM"])
            trk.dma("c2", ES0[:], esel[:, :, :], (), ["ES0"])
            trk.dma("c3", SL0[:], selc[:, :, :], (), ["SL0"])
            trk.dma("c4", GALL[:], gall[:, :, :], (), ["GALL"])
            trk.dma("c5", GOUT[:], gout[:, :], (), ["GOUT"])
            trk.dma("c6", BFG[:], bfg[:, :], (), ["BFG"])
            trk.dma("c7", HVAL[:], hvalid[:, :], (), ["HVAL"])
            trk.dma("c8", CW[:], convw[:, :, :], (), ["CW"])
            trk.dma("c9", CB[:], convb[:, :], (), ["CB"])
            cp("dve", MSK[:], STG[:], ["STG"], ["MSK"])
            cp("dve", IDB[:], CM[:, 0, :], ["CM"], ["IDB"])
            cp("dve", NEGU[:], CM[:, 1, :], ["CM"], ["NEGU"])
            cp("dve", UF[:], CM[:, 2, :], ["CM"], ["UF"])
            cp("dve", ESEL[:], ES0[:], ["ES0"], ["ESEL"])
            cp("dve", SELB[:], SL0[:], ["SL0"], ["SELB"])
            mset("pool", ONEB[:], 1.0, ["ONEB"])
            mset("pool", NONEB[:], -1.0, ["NONEB"])
            mset("pool", ONEF[:], 1.0, ["ONEF"])
            mset("pool", EPST[:], EPS, ["EPST"])
            ts("dve", G8[:], GALL[:, 0, :], 0.125, None, ALU.mult, None, ["GALL"], ["G8"])
            ts("dve", G16[:], GALL[:, 1, :], 1.0 / 16, None, ALU.mult, None, ["GALL"], ["G16"])

            trk.barrier()

        def mem_prep(ph, KMT, VM):
            MS = sbt(ph, "MS", [128, 8, 256], F32)
            MBm = sbt(ph, "MBm", [128, 8, 256], BF16)
            MSQ = sbt(ph, "MSQ", [128, 8, 256], BF16)
            RMB = sbt(ph, "RMB", [128, 256], F32)
            RMT = sbt(ph, "RMT", [128, 2], F32)
            TMPm = sbt(ph, "TMPm", [128, 256], F32)
            WKVs = [sbt(ph, f"WKVs{i}", [128, 2048], F32) for i in range(2)]
            WKV = sbt(ph, "WKV", [128, 8, 2048], BF16)
            trk.dma("ms", MS[:], memT[:, :, :], (), ["MS"])
            cp("act", MBm[:], MS[:], ["MS"], ["MBm"])
            tt("dve", MSQ[:], MS[:], MS[:], ALU.mult, ["MS"], ["MSQ"])
            for fc in range(8):
                s = fc % 2
                trk.dma(f"wkvs{s}", WKVs[s][:], wmkv[:, fc, :], (), [f"WKVs{s}"])
                act(WKV[:, fc, :], WKVs[s][:], AF.Copy, [f"WKVs{s}", "GALL"], ["WKV"], scale=GALL[:, 2, fc:fc + 1])
            for fc in range(8):
                mm(PB[6][:, 0:256], ONEB[:], MSQ[:, fc, :], fc == 0, fc == 7, ["ONEB", "MSQ"], [P(6)], inc=(fc == 7))
            act(TMPm[:], PB[6][:, 0:256], AF.Ln, [P(6), "EPST"], ["TMPm"], bias=EPST[:], scale=1.0 / 1024)
            act(RMB[:], TMPm[:], AF.Exp, ["TMPm"], ["RMB"], scale=-0.5)
            for mc in range(2):
                for fc in range(8):
                    mm(PB[7][:, mc:mc + 1], MSQ[:, fc, 128 * mc:128 * mc + 128], ONEB[:, 0:1], fc == 0, fc == 7,
                       ["MSQ", "ONEB"], [P(7)], inc=(fc == 7))
            act(TMPm[:, 0:2], PB[7][:, 0:2], AF.Ln, [P(7), "EPST"], ["TMPm"], bias=EPST[:], scale=1.0 / 1024)
            act(RMT[:], TMPm[:, 0:2], AF.Exp, ["TMPm"], ["RMT"], scale=-0.5)
            for cc in range(8):
                h, dc = cc // 2, cc % 2
                bk = cc % 2
                for fc in range(8):
                    mm(PB[bk][:, 0:256], WKV[:, fc, 128 * cc:128 * cc + 128], MBm[:, fc, :], fc == 0, fc == 7,
                       ["WKV", "MBm"], [P(bk)], inc=(fc == 7))
                tt("dve", KMT[:, h, dc, :], PB[bk][:, 0:256], RMB[:], ALU.mult, [P(bk), "RMB"], ["KMT"])
            for mc in range(2):
                for hf in range(2):
                    bk = 2 + (2 * mc + hf) % 2
                    for fc in range(8):
                        mm(PB[bk][:, :], MBm[:, fc, 128 * mc:128 * mc + 128],
                           WKV[:, fc, 1024 + 512 * hf:1024 + 512 * hf + 512], fc == 0, fc == 7,
                           ["MBm", "WKV"], [P(bk)], inc=(fc == 7))
                    ts("dve", VM[:, mc, 512 * hf:512 * hf + 512], PB[bk][:, :], RMT[:, mc:mc + 1], None, ALU.mult,
                       None, [P(bk), "RMT"], ["VM"])

        def attention_pass(p):
            fox = p < 2
            with ExitStack() as ph:
                KT = [sbt(ph, f"KT{j}", [128, S], BF16) for j in range(2)]
                VP = sbt(ph, "VP", [128, 64, 4, 65], BF16)
                QT = [sbt(ph, f"QT{j}", [128, NOWN], BF16) for j in range(2)]
                WQ = sbt(ph, "WQ", [128, 8, 256], BF16)
                WK = sbt(ph, "WK", [128, 8, 256], BF16)
                WV = sbt(ph, "WV", [128, 8, 256], BF16)
                WF = sbt(ph, "WF", [128, 8, 8], BF16)
                WFs = sbt(ph, "WFs", [128, 8, 8], F32)
                QAUG = sbt(ph, "QAUG", [12, NOWN], BF16) if fox else None
                XS = [sbt(ph, f"XS{i}", [128, 8, 260], F32) for i in range(2)]
                XB = [sbt(ph, f"XB{i}", [128, 8, 260], BF16) for i in range(2)]
                SQ = [sbt(ph, f"SQ{i}", [128, 8, 260], BF16) for i in range(2)]
                RBC = [sbt(ph, f"RBC{i}", [128, 260], F32) for i in range(2)]
                RTK = [sbt(ph, f"RTK{i}", [128, 2], F32) for i in range(2)]
                TMPRs = [sbt(ph, f"TMPR{i}", [128, 260], F32) for i in range(2)]
                TMPTs = [sbt(ph, f"TMPT{i}", [128, 2], F32) for i in range(2)]
                WST = [sbt(ph, f"WST{i}", [128, 768], F32) for i in range(2)]
                ZF = sbt(ph, "ZF", [128, 8], F32)
                EF = sbt(ph, "EF", [128, 8], F32)
                SPF = [sbt(ph, f"SPF{i}", [128, 8], F32) for i in range(4)]
                ACCF = sbt(ph, "ACCF", [128, 8], F32)
                if fox:
                    ET = [sbt(ph, f"ET{i}", [128, 512], F32) for i in range(2)]
                    PT = [sbt(ph, f"PT{i}", [128, 512], BF16) for i in range(3)]
                    ACB = [sbt(ph, f"ACB{i}", [128, 512], BF16) for i in range(1)]
                    LR = sbt(ph, "LR", [128, 512], F32)
                    BC = sbt(ph, "BC", [128, 512], F32)
                    AUGT = [sbt(ph, f"AUGT{i}", [128, 2, 512], F32) for i in range(2)]
                    PTp = [sbt(ph, f"PTp{i}", [128, 2, 512], BF16) for i in range(3)]
                else:
                    ET = SPB = PT = ACB = None
                    ACC = LR = BC = AUGT = None
                if not fox:
                    ETp = [sbt(ph, f"ETp{i}", [128, 2, 512], F32) for i in range(2)]
                    SPBp = [sbt(ph, f"SPBp{i}", [128, 2, 512], BF16) for i in range(2)]
                    PTp = [sbt(ph, f"PTp{i}", [128, 2, 512], BF16) for i in range(3)]
                    ACCp = sbt(ph, "ACCp", [128, 2, 512], F32)
                    ACBp = [sbt(ph, f"ACBp{i}", [128, 2, 512], BF16) for i in range(2)]

                for fc in range(8):
                    s = fc % 2
                    trk.dma(f"wst{s}", WST[s][:], wqkv[p, :, fc, :], (), [f"WST{s}"])
                    act(WQ[:, fc, :], WST[s][:, 0:256], AF.Copy, [f"WST{s}", "G8"], ["WQ"], scale=G8[:, fc:fc + 1])
                    act(WK[:, fc, :], WST[s][:, 256:512], AF.Copy, [f"WST{s}", "GALL"], ["WK"],
                        scale=GALL[:, 0, fc:fc + 1])
                    act(WV[:, fc, :], WST[s][:, 512:768], AF.Copy, [f"WST{s}", "GALL"], ["WV"],
                        scale=GALL[:, 0, fc:fc + 1])
                if p == 0:
                    trk.dma("wfs", WFs[:], wf[:, :, :], (), ["WFs"])
                    for fc in range(8):
                        ts("pool", WF[:, fc, :], WFs[:, fc, :], GALL[:, 0, fc:fc + 1], None, ALU.mult, None,
                           ["WFs", "GALL"], ["WF"])
                    mset("pool", ACCF[:], 0.0, ["ACCF"])
                mset("pool", VP[:, :, :, 64:65], 1.0, ["VP"])

                def prefetch(src, c0, n, s):
                    trk.dma(f"xs{s}", XS[s][:, :, 0:n], src[:, :, c0:c0 + n], (), [f"XS{s}"])
                    cp("act", XB[s][:, :, 0:n], XS[s][:, :, 0:n], [f"XS{s}"], [f"XB{s}"])
                    tt("dve", SQ[s][:, :, 0:n], XS[s][:, :, 0:n], XS[s][:, :, 0:n], ALU.mult, [f"XS{s}"], [f"SQ{s}"])

                def load_chunk(src, c0, n, s):
                    for fc in range(8):
                        mm(PB[6][:, 0:n], ONEB[:], SQ[s][:, fc, 0:n], fc == 0, fc == 7, ["ONEB", f"SQ{s}"], [P(6)],
                           inc=(fc == 7))
                    TMPR = TMPRs[s]
                    act(TMPR[:, 0:n], PB[6][:, 0:n], AF.Ln, [P(6), "EPST"], [f"TMPR{s}"], bias=EPST[:], scale=1.0 / 1024)
                    act(RBC[s][:, 0:n], TMPR[:, 0:n], AF.Exp, [f"TMPR{s}"], [f"RBC{s}"], scale=-0.5)

                stream = [(xoT, 260 * c_, 260) for c_ in range(8)] + [(xfT, 256 * c_, 256) for c_ in range(32)]
                prefetch(stream[0][0], stream[0][1], stream[0][2], 0)

                def prefetch_next(i_):
                    if i_ + 1 < len(stream):
                        prefetch(stream[i_ + 1][0], stream[i_ + 1][1], stream[i_ + 1][2], (i_ + 1) % 2)

                for c in range(8):
                    s = c % 2
                    prefetch_next(c)
                    load_chunk(xoT, 260 * c, 260, s)
                    for j in range(2):
                        bk = j
                        for fc in range(8):
                            mm(PB[bk][:, 0:260], WQ[:, fc, 128 * j:128 * j + 128], XB[s][:, fc, 0:260], fc == 0,
                               fc == 7, ["WQ", f"XB{s}"], [P(bk)], inc=(fc == 7))
                        tt("dve", QT[j][:, 260 * c:260 * c + 260], PB[bk][:, 0:260], RBC[s][:, 0:260], ALU.mult,
                           [P(bk), f"RBC{s}"], [f"QT{j}"])

                def fl_part(ci, s):
                    for blk in range(2):
                        kb = 2 * ci + blk
                        sp_ = kb % 4
                        for fc in range(8):
                            mm(PB[7][:, 0:8], XB[s][:, fc, 128 * blk:128 * blk + 128], WF[:, fc, :], fc == 0,
                               fc == 7, [f"XB{s}", "WF"], [P(7)], inc=(fc == 7))
                        stt("dve", ZF[:], PB[7][:, 0:8], RTK[s][:, blk:blk + 1], BFG[:], ALU.mult, ALU.add,
                            [P(7), f"RTK{s}", "BFG"], ["ZF"])
                        act(EF[:], ZF[:], AF.Exp, ["ZF"], ["EF"], scale=-1.0)
                        act(SPF[sp_][:], EF[:], AF.Ln, ["EF"], [f"SPF{sp_}"], bias=1.0)

                def cs_part(ci):
                    for blk in range(2):
                        kb = 2 * ci + blk
                        sp_ = kb % 4
                        mm(PB[7][:, 8:16], UF[:], SPF[sp_][:], True, False, ["UF", f"SPF{sp_}"], [P(7)], inc=False)
                        mm(PB[7][:, 8:16], ONEF[:], ACCF[:], False, True, ["ONEF", "ACCF"], [P(7)])
                        cp("dve", CNEG[:, kb, :], PB[7][:, 8:16], [P(7)], ["CNEG"])
                        tt("dve", ACCF[:], ACCF[:], SPF[sp_][:], ALU.add, ["ACCF", f"SPF{sp_}"], ["ACCF"])

                for ci in range(32):
                    s = ci % 2
                    prefetch_next(8 + ci)
                    load_chunk(xfT, 256 * ci, 256, s)
                    for blk in range(2):
                        for fc in range(8):
                            mm(PB[7][:, 16 + blk:17 + blk], SQ[s][:, fc, 128 * blk:128 * blk + 128], ONEB[:, 0:1],
                               fc == 0, fc == 7, [f"SQ{s}", "ONEB"], [P(7)], inc=(fc == 7))
                    TMPT = TMPTs[s]
                    act(TMPT[:], PB[7][:, 16:18], AF.Ln, [P(7), "EPST"], [f"TMPT{s}"], bias=EPST[:], scale=1.0 / 1024)
                    act(RTK[s][:], TMPT[:], AF.Exp, [f"TMPT{s}"], [f"RTK{s}"], scale=-0.5)
                    for j in range(2):
                        bk = j
                        for fc in range(8):
                            mm(PB[bk][:, 0:256], WK[:, fc, 128 * j:128 * j + 128], XB[s][:, fc, 0:256], fc == 0,
                               fc == 7, ["WK", f"XB{s}"], [P(bk)], inc=(fc == 7))
                        tt("dve", KT[j][:, 256 * ci:256 * ci + 256], PB[bk][:, 0:256], RBC[s][:, 0:256], ALU.mult,
                           [P(bk), f"RBC{s}"], [f"KT{j}"])
                    for blk in range(2):
                        kb = 2 * ci + blk
                        bk = 2 + blk
                        for fc in range(8):
                            mm(PB[bk][:, 0:256], XB[s][:, fc, 128 * blk:128 * blk + 128], WV[:, fc, :], fc == 0,
                               fc == 7, [f"XB{s}", "WV"], [P(bk)], inc=(fc == 7))
                        ts("dve", VP[:, kb, :, 0:64], PB[bk][:, 0:256].rearrange("p (h d) -> p h d", h=4),
                           RTK[s][:, blk:blk + 1], None, ALU.mult, None, [P(bk), f"RTK{s}"], ["VP"])
                    if p == 0:
                        fl_part(ci, s)
                        if ci >= 1:
                            cs_part(ci - 1)

                if p == 0:
                    cs_part(31)

                if p == 0:
                    for hf in range(2):
                        src = CNEG[:, 32 * hf:32 * hf + 32, :]
                        d0 = SPL3[:, 32 * hf:32 * hf + 32, :, 0]
                        d1 = SPL3[:, 32 * hf:32 * hf + 32, :, 1]
                        d2 = SPL3[:, 32 * hf:32 * hf + 32, :, 2]
                        ta = ET[0][:, 0:256].rearrange("p (k h) -> p k h", h=8)
                        tb = ET[1][:, 0:256].rearrange("p (k h) -> p k h", h=8)
                        cp("dve", d0, src, ["CNEG"], ["SPL3"])
                        tt("dve", ta, src, d0, ALU.subtract, ["CNEG", "SPL3"], ["ET0"])
                        cp("dve", d1, ta, ["ET0"], ["SPL3"])
                        tt("dve", tb, ta, d1, ALU.subtract, ["ET0", "SPL3"], ["ET1"])
                        cp("dve", d2, tb, ["ET1"], ["SPL3"])
                if fox:
                    for g4 in range(4):
                        for ib in range(4):
                            i = 4 * g4 + ib
                            srcs = [(a, 4 * i + a - 1) for a in range(4) if 4 * i + a - 1 >= 0]
                            for n_, (a, kbs) in enumerate(srcs):
                                mm(PB[6][0:12, 128 * ib:128 * ib + 128],
                                   SPL3[:, kbs, 4 * p:4 * p + 4, :].rearrange("p h j -> p (h j)"), ESEL[:, a, :],
                                   n_ == 0, n_ == len(srcs) - 1, ["SPL3", "ESEL"], [P(6)],
                                   inc=(n_ == len(srcs) - 1), skip=True)
                        cp("dve", QAUG[:, 512 * g4:512 * g4 + 512], PB[6][0:12, :], [P(6)], ["QAUG"])
                        cp("dve", QAUG[:, 2048 + 8 * g4:2048 + 8 * g4 + 8].rearrange("p (b c) -> p b c", c=2),
                           PB[6][0:12, :].rearrange("p (b c) -> p b c", c=128)[:, :, 0:2], [P(6)], ["QAUG"])

                mk = 0 if fox else 1
                tiles = []
                for hl in range(4):
                    for g in range(5):
                        halo = g == 4
                        kbs = list(range(64)) if halo else list(range(16 * g + 16))
                        if not fox:
                            kbs = kbs[::-1]
                        for n_, kb in enumerate(kbs):
                            j = kb // 4
                            if halo:
                                c0, n = 2048 + 2 * j, 32 - 2 * j
                                diag = True
                            else:
                                a = max(0, j - 4 * g)
                                c0, n = 512 * g + 128 * a, 512 - 128 * a
                                diag = j >= 4 * g
                            tiles.append(dict(hl=hl, g=g, kb=kb, c0=c0, n=n, diag=diag, first=(n_ == 0),
                                              last=(n_ == len(kbs) - 1), halo=halo,
                                              cbase=(2048 if halo else 512 * g)))
                for t_, tl in enumerate(tiles):
                    tl["idx"] = t_
                chain_no = -1
                for tl in tiles:
                    if tl["first"]:
                        chain_no += 1
                    tl["chain"] = chain_no

                def kq_mm(tl, bank, extra_last):
                    hl, kb, c0, n = tl["hl"], tl["kb"], tl["c0"], tl["n"]
                    j2, r0 = hl // 2, 64 * (hl % 2)
                    o = c0 - tl["cbase"]
                    steps = [(KT[j2][r0:r0 + 64, 128 * kb:128 * kb + 128], QT[j2][r0:r0 + 64, c0:c0 + n],
                              PB[bank][:, o:o + n], [f"KT{j2}", f"QT{j2}"])]
                    if tl["diag"]:
                        if tl["halo"]:
                            w = min(4, n)
                            steps.append((IDB[:], MSK[:, mk, kb % 4, 128:128 + w], PB[bank][:, o:o + w], ["IDB", "MSK"]))
                        else:
                            steps.append((IDB[:], MSK[:, mk, kb % 4, 0:128], PB[bank][:, o:o + 128], ["IDB", "MSK"]))
                    for n_, (l_, r_, o_, rd) in enumerate(steps):
                        lastst = (n_ == len(steps) - 1)
                        mm(o_, l_, r_, (n_ == 0) and not fox, lastst and extra_last, rd, [P(bank)],
                           inc=(lastst and extra_last), skip=True)

                def abank(tl):
                    return tl["idx"] % (4 if fox else 3)

                def st_A(tl):
                    if fox:
                        c2 = tl["chain"] % 2
                        b = abank(tl)
                        if tl["first"]:
                            nfull = 32 if tl["halo"] else 512
                            cb_ = tl["cbase"]
                            mm(PB[6][:, 0:nfull], SELB[:, tl["hl"], :], QAUG[:, cb_:cb_ + nfull], True, True,
                               ["SELB", "QAUG"], [P(6)])
                            cp("dve", AUGT[c2][:, 0:nfull], PB[6][:, 0:nfull], [P(6)], [f"AUGT{c2}"])
                        o = tl["c0"] - tl["cbase"]
                        n = tl["n"]
                        cp("dve", PB[b][:, o:o + n], AUGT[c2][:, o:o + n], [f"AUGT{c2}"], [P(b)])
                    kq_mm(tl, abank(tl), True)

                def st_fox_exp(tl):
                    hl, kb, n = tl["hl"], tl["kb"], tl["n"]
                    o = tl["c0"] - tl["cbase"]
                    b, s3 = abank(tl), tl["idx"] % 3
                    act(PT[s3][:, 0:n], PB[b][:, o:o + n], AF.Exp, [P(b), "CNEG"], [f"PT{s3}"],
                        bias=CNEG[:, kb, 4 * p + hl:4 * p + hl + 1])

                def st_O(tl):
                    hl, kb, n = tl["hl"], tl["kb"], tl["n"]
                    o = tl["c0"] - tl["cbase"]
                    s3 = tl["idx"] % 3
                    ob = 4 + tl["chain"] % 2
                    mm(PB[ob][0:65, o:o + n], VP[:, kb, hl, :], PT[s3][:, 0:n], tl["first"], tl["last"],
                       ["VP", f"PT{s3}"], [P(ob)], skip=True)
                    if tl["last"]:
                        finalize(tl)

                def finalize(tl):
                    hl = tl["hl"]
                    ob = 4 + tl["chain"] % 2
                    n = 32 if tl["halo"] else 512
                    cb = tl["cbase"]
                    gh = 4 * p + hl
                    pair, r0 = gh // 2, 64 * (gh % 2)
                    dst = OT[r0:r0 + 64, pair, cb:cb + n]
                    if fox:
                        ts("dve", LR[64:65, 0:n], PB[ob][64:65, 0:n], 1e-30, None, ALU.max, None, [P(ob)], ["LR"])
                        trk.op("dve", lambda: nc.vector.reciprocal(out=LR[64:65, 0:n], in_=LR[64:65, 0:n]),
                               ["LR"], ["LR"])
                        mm(PB[6][0:64, 0:n], ONEF[64:65, 0:64], LR[64:65, 0:n], True, True, ["ONEF", "LR"], [P(6)])
                        cp("dve", BC[0:64, 0:n], PB[6][0:64, 0:n], [P(6)], ["BC"])
                        tt("dve", dst, PB[ob][0:64, 0:n], BC[0:64, 0:n], ALU.mult, [P(ob), "BC"], ["OT"])
                    else:
                        cp("dve", dst, PB[ob][0:64, 0:n], [P(ob)], ["OT"])

                def st_sb_esp(tl):
                    n = tl["n"]
                    o = tl["c0"] - tl["cbase"]
                    b, s2 = abank(tl), tl["idx"] % 2
                    act(ET[s2][:, 0:n], PB[b][:, o:o + n], AF.Exp, [P(b)], [f"ET{s2}"])
                    act(SPB[s2][:, 0:n], ET[s2][:, 0:n], AF.Ln, [f"ET{s2}"], [f"SPB{s2}"], bias=1.0)

                def st_sb_B(tl):
                    n = tl["n"]
                    o = tl["c0"] - tl["cbase"]
                    s2 = tl["idx"] % 2
                    bank = abank(tl)
                    use_acc = not tl["first"]
                    mm(PB[bank][:, o:o + n], NEGU[:], SPB[s2][:, 0:n], False, not use_acc, ["NEGU", f"SPB{s2}"],
                       [P(bank)], inc=(not use_acc), skip=True)
                    if use_acc:
                        ab = tl["idx"] % 2
                        mm(PB[bank][:, o:o + n], NONEB[:], ACB[ab][:, o:o + n], False, True, ["NONEB", f"ACB{ab}"],
                           [P(bank)], skip=True)

                def st_sb_acc(tl):
                    n = tl["n"]
                    o = tl["c0"] - tl["cbase"]
                    s2 = tl["idx"] % 2
                    if tl["last"]:
                        return
                    nb = (tl["idx"] + 1) % 2
                    if tl["first"]:
                        mset("dve", ACC[:], 0.0, ["ACC"])
                    tt("dve", ACC[:, o:o + n], ACC[:, o:o + n], SPB[s2][:, 0:n], ALU.add, ["ACC", f"SPB{s2}"], ["ACC"])
                    nx = tiles[tl["idx"] + 1]
                    o2 = nx["c0"] - nx["cbase"]
                    cp("dve", ACB[nb][:, o2:o2 + nx["n"]], ACC[:, o2:o2 + nx["n"]], ["ACC"], [f"ACB{nb}"])

                def st_sb_exp2(tl):
                    n = tl["n"]
                    o = tl["c0"] - tl["cbase"]
                    bank = abank(tl)
                    s3 = tl["idx"] % 3
                    act(PT[s3][:, 0:n], PB[bank][:, o:o + n], AF.Exp, [P(bank)], [f"PT{s3}"])

                if fox:
                    mset("dve", ACB[0][:], 0.0, ["ACB0"])
                    for b_ in range(4):
                        mm(PB[b_][:, :], IDB[:], ACB[0][:, :], True, True, ["IDB", "ACB0"], [P(b_)])
                    sched = [(st_A, 0), (st_fox_exp, 2), (st_O, 2)]
                else:
                    sched = [(st_A, 0), (st_sb_esp, 1), (st_sb_B, 1), (st_sb_acc, 1), (st_sb_exp2, 2), (st_O, 2)]
                if not fox:
                    ptiles = []
                    for j2 in range(2):
                        for g in range(5):
                            halo = g == 4
                            kbs = (list(range(64)) if halo else list(range(16 * g + 16)))[::-1]
                            for n_, kb in enumerate(kbs):
                                j = kb // 4
                                if halo:
                                    c0, n = 2048 + 2 * j, 32 - 2 * j
                                    diag = True
                                else:
                                    a = max(0, j - 4 * g)
                                    c0, n = 512 * g + 128 * a, 512 - 128 * a
                                    diag = j >= 4 * g
                                ptiles.append(dict(j2=j2, g=g, kb=kb, c0=c0, n=n, diag=diag, first=(n_ == 0),
                                                   last=(n_ == len(kbs) - 1), halo=halo,
                                                   cbase=(2048 if halo else 512 * g)))
                    ch = -1
                    for t_, tl in enumerate(ptiles):
                        tl["idx"] = t_
                        if tl["first"]:
                            ch += 1
                        tl["chain"] = ch

                    def pA(tl):
                        j2, kb, c0, n = tl["j2"], tl["kb"], tl["c0"], tl["n"]
                        o = c0 - tl["cbase"]
                        bA = 2 * (tl["idx"] % 3)
                        steps = []
                        for h in range(2):
                            steps.append((KT[j2][64 * h:64 * h + 64, 128 * kb:128 * kb + 128],
                                          QT[j2][64 * h:64 * h + 64, c0:c0 + n], PB[bA + h][:, o:o + n],
                                          [f"KT{j2}", f"QT{j2}"], bA + h, True))
                        if tl["diag"]:
                            for h in range(2):
                                if tl["halo"]:
                                    w = min(4, n)
                                    steps.append((IDB[:], MSK[:, 1, kb % 4, 128:128 + w], PB[bA + h][:, o:o + w],
                                                  ["IDB", "MSK"], bA + h, False))
                                else:
                                    steps.append((IDB[:], MSK[:, 1, kb % 4, 0:128], PB[bA + h][:, o:o + 128],
                                                  ["IDB", "MSK"], bA + h, False))
                        for n_, (l_, r_, o_, rd, bk_, st_) in enumerate(steps):
                            lastst = n_ == len(steps) - 1
                            mm(o_, l_, r_, st_, lastst, rd, [P(bk_)], inc=lastst, skip=True)

                    def pESP(tl):
                        n = tl["n"]
                        o = tl["c0"] - tl["cbase"]
                        bA = 2 * (tl["idx"] % 3)
                        s2 = tl["idx"] % 2
                        act(ETp[s2][:, :, 0:n], PALL[:, bA:bA + 2, o:o + n], AF.Exp, [P(bA), P(bA + 1)], [f"ETp{s2}"])
                        act(SPBp[s2][:, :, 0:n], ETp[s2][:, :, 0:n], AF.Ln, [f"ETp{s2}"], [f"SPBp{s2}"], bias=1.0)

                    def pB(tl):
                        n = tl["n"]
                        o = tl["c0"] - tl["cbase"]
                        bA = 2 * (tl["idx"] % 3)
                        s2 = tl["idx"] % 2
                        use_acc = not tl["first"]
                        for h in range(2):
                            lastm = (h == 1) and not use_acc
                            mm(PB[bA + h][:, o:o + n], NEGU[:], SPBp[s2][:, h, 0:n], False, lastm,
                               ["NEGU", f"SPBp{s2}"], [P(bA + h)], inc=lastm, skip=True)
                        if use_acc:
                            ab = tl["idx"] % 2
                            for h in range(2):
                                mm(PB[bA + h][:, o:o + n], NONEB[:], ACBp[ab][:, h, o:o + n], False, h == 1,
                                   ["NONEB", f"ACBp{ab}"], [P(bA + h)], inc=(h == 1), skip=True)

                    def pACC(tl):
                        n = tl["n"]
                        o = tl["c0"] - tl["cbase"]
                        s2 = tl["idx"] % 2
                        if tl["last"]:
                            return
                        nb = (tl["idx"] + 1) % 2
                        if tl["first"]:
                            mset("dve", ACCp[:], 0.0, ["ACCp"])
                        tt("dve", ACCp[:, :, o:o + n], ACCp[:, :, o:o + n], SPBp[s2][:, :, 0:n], ALU.add,
                           ["ACCp", f"SPBp{s2}"], ["ACCp"])
                        nx = ptiles[tl["idx"] + 1]
                        o2 = nx["c0"] - nx["cbase"]
                        cp("dve", ACBp[nb][:, :, o2:o2 + nx["n"]], ACCp[:, :, o2:o2 + nx["n"]], ["ACCp"],
                           [f"ACBp{nb}"])

                    def pEXP2(tl):
                        n = tl["n"]
                        o = tl["c0"] - tl["cbase"]
                        bA = 2 * (tl["idx"] % 3)
                        s3 = tl["idx"] % 3
                        act(PTp[s3][:, :, 0:n], PALL[:, bA:bA + 2, o:o + n], AF.Exp, [P(bA), P(bA + 1)], [f"PTp{s3}"])

                    def pO(tl):
                        j2, kb, n = tl["j2"], tl["kb"], tl["n"]
                        o = tl["c0"] - tl["cbase"]
                        s3 = tl["idx"] % 3
                        ob = 6 + tl["chain"] % 2
                        mm(PALL[0:64, ob, o:o + n], VP[:, kb, 2 * j2, 0:64], PTp[s3][:, 0, 0:n], tl["first"], tl["last"],
                           ["VP", f"PTp{s3}"], [P(ob)], inc=False, skip=True)
                        mm(PALL[64:128, ob, o:o + n], VP[:, kb, 2 * j2 + 1, 0:64], PTp[s3][:, 1, 0:n], tl["first"],
                           tl["last"], ["VP", f"PTp{s3}"], [P(ob)], inc=True, skip=True, tpos=(0, 64))
                        if tl["last"]:
                            nf = 32 if tl["halo"] else 512
                            cb_ = tl["cbase"]
                            cp("dve", OT[:, 2 * p + j2, cb_:cb_ + nf], PB[ob][:, 0:nf], [P(ob)], ["OT"])

                    psched = [(pA, 0), (pESP, 1), (pB, 1), (pACC, 1), (pEXP2, 2), (pO, 2)]
                    npt = len(ptiles)
                    for T in range(npt + 3):
                        for fn, skew in psched:
                            k = T - skew
                            if 0 <= k < npt:
                                fn(ptiles[k])
                else:
                    ptiles = []
                    for j2 in range(2):
                        for g in range(5):
                            halo = g == 4
                            kbs = list(range(64)) if halo else list(range(16 * g + 16))
                            for n_, kb in enumerate(kbs):
                                j = kb // 4
                                if halo:
                                    c0, n = 2048 + 2 * j, 32 - 2 * j
                                    diag = True
                                else:
                                    a = max(0, j - 4 * g)
                                    c0, n = 512 * g + 128 * a, 512 - 128 * a
                                    diag = j >= 4 * g
                                ptiles.append(dict(j2=j2, g=g, kb=kb, c0=c0, n=n, diag=diag, first=(n_ == 0),
                                                   last=(n_ == len(kbs) - 1), halo=halo,
                                                   cbase=(2048 if halo else 512 * g)))
                    ch = -1
                    for t_, tl in enumerate(ptiles):
                        tl["idx"] = t_
                        if tl["first"]:
                            ch += 1
                        tl["chain"] = ch

                    def fA(tl):
                        j2, kb, c0, n = tl["j2"], tl["kb"], tl["c0"], tl["n"]
                        o = c0 - tl["cbase"]
                        bA = 2 * (tl["idx"] % 2)
                        c2 = tl["chain"] % 2
                        lb = 6 + c2
                        if tl["first"]:
                            nfull = 32 if tl["halo"] else 512
                            cb_ = tl["cbase"]
                            for h in range(2):
                                mm(PB[lb][:, 0:nfull], SELB[:, 2 * j2 + h, :], QAUG[:, cb_:cb_ + nfull], True, True,
                                   ["SELB", "QAUG"], [P(lb)])
                                cp("dve", AUGT[c2][:, h, 0:nfull], PB[lb][:, 0:nfull], [P(lb)], [f"AUGT{c2}"])
                        cp("dve", PALL[:, bA:bA + 2, o:o + n], AUGT[c2][:, :, o:o + n], [f"AUGT{c2}"],
                           [P(bA), P(bA + 1)])
                        steps = []
                        for h in range(2):
                            steps.append((KT[j2][64 * h:64 * h + 64, 128 * kb:128 * kb + 128],
                                          QT[j2][64 * h:64 * h + 64, c0:c0 + n], PB[bA + h][:, o:o + n],
                                          [f"KT{j2}", f"QT{j2}"], bA + h))
                        if tl["diag"]:
                            for h in range(2):
                                if tl["halo"]:
                                    w = min(4, n)
                                    steps.append((IDB[:], MSK[:, 0, kb % 4, 128:128 + w], PB[bA + h][:, o:o + w],
                                                  ["IDB", "MSK"], bA + h))
                                else:
                                    steps.append((IDB[:], MSK[:, 0, kb % 4, 0:128], PB[bA + h][:, o:o + 128],
                                                  ["IDB", "MSK"], bA + h))
                        for n_, (l_, r_, o_, rd, bk_) in enumerate(steps):
                            lastst = n_ == len(steps) - 1
                            mm(o_, l_, r_, False, lastst, rd, [P(bk_)], inc=lastst, skip=True)

                    def fEXP(tl):
                        j2, kb, n = tl["j2"], tl["kb"], tl["n"]
                        o = tl["c0"] - tl["cbase"]
                        bA = 2 * (tl["idx"] % 2)
                        s3 = tl["idx"] % 3
                        for h in range(2):
                            gh = 4 * p + 2 * j2 + h
                            act(PTp[s3][:, h, 0:n], PB[bA + h][:, o:o + n], AF.Exp, [P(bA + h), "CNEG"], [f"PTp{s3}"],
                                bias=CNEG[:, kb, gh:gh + 1])

                    def fO(tl):
                        j2, kb, n = tl["j2"], tl["kb"], tl["n"]
                        o = tl["c0"] - tl["cbase"]
                        s3 = tl["idx"] % 3
                        c2 = tl["chain"] % 2
                        ob, lb = 4 + c2, 6 + c2
                        mm(PALL[0:64, ob, o:o + n], VP[:, kb, 2 * j2, 0:64], PTp[s3][:, 0, 0:n], tl["first"], tl["last"],
                           ["VP", f"PTp{s3}"], [P(ob)], inc=False, skip=True)
                        mm(PALL[64:128, ob, o:o + n], VP[:, kb, 2 * j2 + 1, 0:64], PTp[s3][:, 1, 0:n], tl["first"],
                           tl["last"], ["VP", f"PTp{s3}"], [P(ob)], inc=False, skip=True, tpos=(0, 64))
                        mm(PALL[0:64, lb, o:o + n], ONEB[:, 0:64], PTp[s3][:, 0, 0:n], tl["first"], tl["last"],
                           ["ONEB", f"PTp{s3}"], [P(lb)], inc=False, skip=True)
                        mm(PALL[64:128, lb, o:o + n], ONEB[:, 0:64], PTp[s3][:, 1, 0:n], tl["first"], tl["last"],
                           ["ONEB", f"PTp{s3}"], [P(lb)], inc=True, skip=True, tpos=(0, 64))
                        if tl["last"]:
                            nf = 32 if tl["halo"] else 512
                            cb_ = tl["cbase"]
                            ts("dve", BC[:, 0:nf], PB[lb][:, 0:nf], 1e-30, None, ALU.max, None, [P(lb)], ["BC"])
                            trk.op("dve", lambda: nc.vector.reciprocal(out=BC[:, 0:nf], in_=BC[:, 0:nf]), ["BC"], ["BC"])
                            tt("dve", OT[:, 2 * p + j2, cb_:cb_ + nf], PB[ob][:, 0:nf], BC[:, 0:nf], ALU.mult,
                               [P(ob), "BC"], ["OT"])

                    npt = len(ptiles)
                    for T in range(npt + 2):
                        if T < npt:
                            fA(ptiles[T])
                        if 0 <= T - 1 < npt:
                            fEXP(ptiles[T - 1])
                            fO(ptiles[T - 1])
                trk.barrier()

        with ExitStack() as satt:
            CNEG = sbt(satt, "CNEG", [128, 64, 8], F32)
            SPL3 = sbt(satt, "SPL3", [128, 64, 8, 3], BF16)
            for p in range(4):
                attention_pass(p)

        chunks = [(256 * c, 256) for c in range(8)] + [(2048, 32)]
        with ExitStack() as pha:
            XT = sbt(pha, "XT", [128, 8, NOWN], F32)
            trk.dma("xt", XT[:, 0:4, :], xoT[:, 0:4, :], (), ["XT"])
            trk.dma("xt2", XT[:, 4:8, :], xoT[:, 4:8, :], (), ["XT"])
            with ExitStack() as ph:
                KMT = sbt(ph, "KMT", [128, 4, 2, 256], BF16)
                VM = sbt(ph, "VM", [128, 2, 1024], BF16)
                with ExitStack() as phm:
                    mem_prep(phm, KMT, VM)
                    trk.barrier()
                WO = sbt(ph, "WO", [128, 8, 1024], BF16)
                WMQ = sbt(ph, "WMQ", [128, 8, 1024], BF16)
                WMO = sbt(ph, "WMO", [128, 8, 1024], BF16)
                WS = [sbt(ph, f"WS{i}", [128, 512], F32) for i in range(2)]
                SQO = sbt(ph, "SQO", [128, 8, 256], BF16)
                OTS = sbt(ph, "OTS", [128, 8, 256], BF16)
                RF = sbt(ph, "RF", [128, 256], F32)
                RS = sbt(ph, "RS", [128, 256], F32)
                R2 = sbt(ph, "R2", [128, 256], F32)
                TMP = sbt(ph, "TMP", [128, 256], F32)
                T1 = [sbt(ph, f"T1{i}", [128, 256], F32) for i in range(2)]
                SQX = sbt(ph, "SQX", [128, 8, 256], BF16)
                H2 = sbt(ph, "H2", [128, 8, 256], BF16)
                QM = sbt(ph, "QM", [128, 8, 256], BF16)
                PM = sbt(ph, "PM", [128, 8, 256], BF16)
                RL = sbt(ph, "RL", [128, 256], F32)
                OM = sbt(ph, "OM", [128, 8, 256], BF16)
                k = 0
                for (wsrc, wdst, nm) in ((wout, WO, "WO"), (wmq, WMQ, "WMQ"), (wmo, WMO, "WMO")):
                    for fc in range(8):
                        for hf in range(2):
                            s = k % 2
                            k += 1
                            trk.dma(f"ws{s}", WS[s][:], wsrc[:, fc, 512 * hf:512 * hf + 512], (), [f"WS{s}"])
                            dst_ = wdst[:, fc, 512 * hf:512 * hf + 512]
                            if nm == "WO":
                                act(dst_, WS[s][:], AF.Copy, [f"WS{s}", "GOUT"], [nm], scale=GOUT[:, fc:fc + 1])
                            elif nm == "WMQ":
                                act(dst_, WS[s][:], AF.Copy, [f"WS{s}", "G16"], [nm], scale=G16[:, fc:fc + 1])
                            else:
                                cp("act", dst_, WS[s][:], [f"WS{s}"], [nm])

                for (c0, n) in chunks:
                    act(SQO[:, :, 0:n], OT[:, :, c0:c0 + n], AF.Square, ["OT"], ["SQO"])
                    for grp, dst, nm in ((0, RF, "RF"), (1, RS, "RS")):
                        for q_ in range(4):
                            mm(PB[6][:, 0:n], ONEB[:], SQO[:, 4 * grp + q_, 0:n], q_ == 0, q_ == 3, ["ONEB", "SQO"],
                               [P(6)], inc=(q_ == 3))
                        act(TMP[:, 0:n], PB[6][:, 0:n], AF.Ln, [P(6), "EPST"], ["TMP"], bias=EPST[:], scale=1.0 / 512)
                        act(dst[:, 0:n], TMP[:, 0:n], AF.Exp, ["TMP"], [nm], scale=-0.5)
                    for q_ in range(8):
                        rr, nm = (RF, "RF") if q_ < 4 else (RS, "RS")
                        tt("dve", OTS[:, q_, 0:n], OT[:, q_, c0:c0 + n], rr[:, 0:n], ALU.mult, ["OT", nm], ["OTS"])
                    for fc in range(8):
                        bk = fc % 2
                        for q_ in range(8):
                            mm(PB[bk][:, 0:n], WO[:, q_, 128 * fc:128 * fc + 128], OTS[:, q_, 0:n], q_ == 0, q_ == 7,
                               ["WO", "OTS"], [P(bk)], inc=(q_ == 7))
                        tt("dve", XT[:, fc, c0:c0 + n], PB[bk][:, 0:n], XT[:, fc, c0:c0 + n], ALU.add,
                           [P(bk), "XT"], ["XT"])
                    act(SQX[:, :, 0:n], XT[:, :, c0:c0 + n], AF.Square, ["XT"], ["SQX"])
                    for fc in range(8):
                        mm(PB[6][:, 0:n], ONEB[:], SQX[:, fc, 0:n], fc == 0, fc == 7, ["ONEB", "SQX"], [P(6)],
                           inc=(fc == 7))
                    act(TMP[:, 0:n], PB[6][:, 0:n], AF.Ln, [P(6), "EPST"], ["TMP"], bias=EPST[:], scale=1.0 / 1024)
                    act(R2[:, 0:n], TMP[:, 0:n], AF.Exp, ["TMP"], ["R2"], scale=-0.5)
                    for fc in range(8):
                        tt("dve", H2[:, fc, 0:n], XT[:, fc, c0:c0 + n], R2[:, 0:n], ALU.mult, ["XT", "R2"], ["H2"])
                    for cc in range(8):
                        bk = cc % 2
                        for fc in range(8):
                            mm(PB[bk][:, 0:n], WMQ[:, fc, 128 * cc:128 * cc + 128], H2[:, fc, 0:n], fc == 0, fc == 7,
                               ["WMQ", "H2"], [P(bk)], inc=(fc == 7))
                        cp("act", QM[:, cc, 0:n], PB[bk][:, 0:n], [P(bk)], ["QM"])
                    for h in range(4):
                        for mc in range(2):
                            bk = 2 + mc
                            for dc in range(2):
                                mm(PB[bk][:, 0:n], KMT[:, h, dc, 128 * mc:128 * mc + 128], QM[:, 2 * h + dc, 0:n],
                                   dc == 0, dc == 1, ["KMT", "QM"], [P(bk)], inc=(dc == 1))
                            act(PM[:, 2 * h + mc, 0:n], PB[bk][:, 0:n], AF.Exp, [P(bk)], ["PM"])
                        for mc in range(2):
                            mm(PB[6][:, 0:n], ONEB[:], PM[:, 2 * h + mc, 0:n], mc == 0, mc == 1, ["ONEB", "PM"],
                               [P(6)], inc=(mc == 1))
                        trk.op("dve", lambda: nc.vector.reciprocal(out=RL[:, 0:n], in_=PB[6][:, 0:n]), [P(6)], ["RL"])
                        for dc in range(2):
                            bk = dc
                            for mc in range(2):
                                mm(PB[bk][:, 0:n], VM[:, mc, 256 * h + 128 * dc:256 * h + 128 * dc + 128],
                                   PM[:, 2 * h + mc, 0:n], mc == 0, mc == 1, ["VM", "PM"], [P(bk)], inc=(mc == 1))
                            tt("dve", OM[:, 2 * h + dc, 0:n], PB[bk][:, 0:n], RL[:, 0:n], ALU.mult, [P(bk), "RL"],
                               ["OM"])
                    for fc in range(8):
                        bk = 4 + fc % 2
                        for cc in range(8):
                            mm(PB[bk][:, 0:n], WMO[:, cc, 128 * fc:128 * fc + 128], OM[:, cc, 0:n], cc == 0, cc == 7,
                               ["WMO", "OM"], [P(bk)], inc=(cc == 7))
                        tt("dve", XT[:, fc, c0:c0 + n], PB[bk][:, 0:n], XT[:, fc, c0:c0 + n], ALU.add,
                           [P(bk), "XT"], ["XT"])
                trk.barrier()

            with ExitStack() as ph:
                H3 = sbt(ph, "H3", [128, 8, NOWN], BF16)
                WU = sbt(ph, "WU", [128, 8, 2, 1408], BF16)
                WD = OT[:, :, :].rearrange("p a b -> p (a b)")[:, 0:11 * 1024].rearrange("p (c n) -> p c n", n=1024)
                WUs = [sbt(ph, f"WUs{i}", [128, 704], F32) for i in range(2)]
                WDs = [sbt(ph, f"WDs{i}", [128, 512], F32) for i in range(2)]
                UH = sbt(ph, "UH", [128, 22, 32], F32)
                FIX = sbt(ph, "FIX", [128, 22, 16, 2], F32)
                YG = [sbt(ph, f"YG{i}", [128, 256], F32) for i in range(2)]
                YV = [sbt(ph, f"YV{i}", [128, 256], F32) for i in range(2)]
                AT_ = [sbt(ph, f"ATt{i}", [128, 256], BF16) for i in range(3)]
                TMP4 = sbt(ph, "TMP4", [128, 256], F32)
                R4 = sbt(ph, "R4", [128, 256], F32)
                SQ4 = OT[:, :, :].rearrange("p a b -> p (a b)")[:, 11264:11264 + 2048].rearrange("p (c n) -> p c n", n=256)
                OS = [sbt(ph, f"OS{i}", [128, 256], F32) for i in range(2)]
                for (c0, n) in chunks:
                    act(SQ4[:, :, 0:n], XT[:, :, c0:c0 + n], AF.Square, ["XT"], ["SQ4"])
                    for fc in range(8):
                        mm(PB[6][:, 0:n], ONEB[:], SQ4[:, fc, 0:n], fc == 0, fc == 7, ["ONEB", "SQ4"], [P(6)],
                           inc=(fc == 7))
                    act(TMP4[:, 0:n], PB[6][:, 0:n], AF.Ln, [P(6), "EPST"], ["TMP4"], bias=EPST[:], scale=1.0 / 1024)
                    act(R4[:, 0:n], TMP4[:, 0:n], AF.Exp, ["TMP4"], ["R4"], scale=-0.5)
                    for fc in range(8):
                        tt("dve", H3[:, fc, c0:c0 + n], XT[:, fc, c0:c0 + n], R4[:, 0:n], ALU.mult, ["XT", "R4"],
                           ["H3"])
                for sw in range(2):
                    k = 0
                    for fc in range(8):
                        for gv in range(2):
                            for hf in range(2):
                                s = k % 2
                                k += 1
                                col = 2816 * gv + 1408 * sw + 704 * hf
                                trk.dma(f"wus{s}", WUs[s][:], wup[:, fc, col:col + 704], (), [f"WUs{s}"])
                                act(WU[:, fc, gv, 704 * hf:704 * hf + 704], WUs[s][:], AF.Copy, [f"WUs{s}", "GALL"],
                                    ["WU"], scale=GALL[:, 3, fc:fc + 1])
                    k = 0
                    for cl in range(11):
                        for hf in range(2):
                            s = k % 2
                            k += 1
                            trk.dma(f"wds{s}", WDs[s][:], wdown[:, 11 * sw + cl, 512 * hf:512 * hf + 512], (),
                                    [f"WDs{s}"])
                            cp("act", WD[:, cl, 512 * hf:512 * hf + 512], WDs[s][:], [f"WDs{s}"], ["WD"])
                    for q_ in range(22):
                        gv, cl = q_ // 11, q_ % 11
                        bk = q_ % 2
                        for fc in range(8):
                            mm(PB[bk][:, 0:32], WU[:, fc, gv, 128 * cl:128 * cl + 128], H3[:, fc, 2048:2080], fc == 0,
                               fc == 7, ["WU", "H3"], [P(bk)], inc=(fc == 7))
                        tt("dve", UH[:, q_, :], PB[bk][:, 0:32], HVAL[:], ALU.mult, [P(bk), "HVAL"], ["UH"])
                    for q_ in range(22):
                        gv, cl = q_ // 11, q_ % 11
                        cc = 22 * gv + 11 * sw + cl
                        uh = UH[:, q_, :].rearrange("p (b c) -> p b c", c=2)
                        ts("pool", FIX[:, q_, :, 1], uh[:, :, 1], CW[:, cc, 0:1], None, ALU.mult, None, ["UH", "CW"],
                           ["FIX"])
                        ts("pool", FIX[:, q_, :, 0], uh[:, :, 0], CW[:, cc, 0:1], None, ALU.mult, None, ["UH", "CW"],
                           ["FIX"])
                        stt("dve", FIX[:, q_, :, 0], uh[:, :, 1], CW[:, cc, 1:2], FIX[:, q_, :, 0], ALU.mult, ALU.add,
                            ["UH", "CW", "FIX"], ["FIX"])
                    for tc in range(8):
                        c0 = 256 * tc
                        units = list(range(11))

                        def u_stage(cl, tc=tc, c0=c0, sw=sw):
                            for gv in range(2):
                                bk = 2 * (cl % 2) + gv
                                for fc in range(8):
                                    mm(PB[bk][:, 0:256], WU[:, fc, gv, 128 * cl:128 * cl + 128], H3[:, fc, c0:c0 + 256],
                                       fc == 0, fc == 7, ["WU", "H3"], [P(bk)], inc=(fc == 7))

                        def conv_stage(cl, tc=tc, c0=c0, sw=sw):
                            s = cl % 2
                            for gv, Y in ((0, YG[s]), (1, YV[s])):
                                bk = 2 * (cl % 2) + gv
                                cc = 22 * gv + 11 * sw + cl
                                q_ = 11 * gv + cl
                                nm = ("YG" if gv == 0 else "YV") + str(s)
                                act(Y[:, :], PB[bk][:, 0:256], AF.Identity, [P(bk), "CW", "CB"], [nm],
                                    bias=CB[:, cc:cc + 1], scale=CW[:, cc, 2:3])
                                y3 = Y[:, :].rearrange("p (b c) -> p b c", c=128)
                                u3 = PB[bk][:, 0:256].rearrange("p (b c) -> p b c", c=128)
                                stt("dve", y3[:, :, 1:128], u3[:, :, 0:127], CW[:, cc, 1:2], y3[:, :, 1:128], ALU.mult,
                                    ALU.add, [P(bk), "CW", nm], [nm])
                                stt("dve", y3[:, :, 2:128], u3[:, :, 0:126], CW[:, cc, 0:1], y3[:, :, 2:128], ALU.mult,
                                    ALU.add, [P(bk), "CW", nm], [nm])
                                tt("pool", y3[:, :, 0:2], y3[:, :, 0:2], FIX[:, q_, 2 * tc:2 * tc + 2, :], ALU.add,
                                   [nm, "FIX"], [nm])
                            act(YG[s][:, :], YG[s][:, :], AF.Silu, [f"YG{s}"], [f"YG{s}"])
                            a3 = cl % 3
                            tt("dve", AT_[a3][:, :], YG[s][:, :], YV[s][:, :], ALU.mult, [f"YG{s}", f"YV{s}"],
                               [f"AT{a3}"])

                        def down_stage(cl, tc=tc, c0=c0, sw=sw):
                            a3 = cl % 3
                            for fc in range(8):
                                bk = 4 + fc // 2
                                o = 256 * (fc % 2)
                                mm(PB[bk][:, o:o + 256], WD[:, cl, 128 * fc:128 * fc + 128], AT_[a3][:, :],
                                   (cl == 0 and fc % 2 == 0), cl == 10, ["WD", f"AT{a3}"], [P(bk)],
                                   inc=(fc == 7 or cl == 10), skip=True)

                        for T in range(11 + 2):
                            if T < 11:
                                u_stage(T)
                            if 0 <= T - 1 < 11:
                                conv_stage(T - 1)
                            if 0 <= T - 2 < 11:
                                down_stage(T - 2)
                        for fc in range(8):
                            bk = 4 + fc // 2
                            o = 256 * (fc % 2)
                            tt("dve", XT[:, fc, c0:c0 + 256], PB[bk][:, o:o + 256], XT[:, fc, c0:c0 + 256], ALU.add,
                               [P(bk), "XT"], ["XT"])
                for tc in range(8):
                    c0 = 256 * tc
                    act(SQ4[:, :, :], XT[:, :, c0:c0 + 256], AF.Square, ["XT"], ["SQ4"])
                    for fc in range(8):
                        mm(PB[0][:, 0:256], ONEB[:], SQ4[:, fc, :], fc == 0, fc == 7, ["ONEB", "SQ4"], [P(0)],
                           inc=(fc == 7))
                    act(TMP4[:, :], PB[0][:, 0:256], AF.Ln, [P(0), "EPST"], ["TMP4"], bias=EPST[:], scale=1.0 / 1024)
                    act(R4[:, :], TMP4[:, :], AF.Exp, ["TMP4"], ["R4"], scale=-0.5)
                    for fc in range(8):
                        s = fc % 2
                        stt("dve", OS[s][:, :], XT[:, fc, c0:c0 + 256], GALL[:, 4, fc:fc + 1], R4[:, :], ALU.mult,
                            ALU.mult, ["XT", "GALL", "R4"], [f"OS{s}"])
                        trk.dma(f"os{s}", outT[:, fc, c0:c0 + 256], OS[s][:, :], [f"OS{s}"], ())
                trk.barrier()
    return nc


_CACHE = {}


def _consts():
    ident = np.eye(128, dtype=np.float32)
    j = np.arange(128)[:, None]
    s = np.arange(128)[None, :]
    negu = np.where(j >= s, -1.0, 0.0).astype(np.float32)
    u = np.where(j <= s, 1.0, 0.0).astype(np.float32)
    cmat = np.stack([ident, negu, u], axis=1)
    selc = np.zeros((12, 4, 128), np.float32)
    for h in range(4):
        selc[3 * h:3 * h + 3, h, :] = 1.0
    return np.ascontiguousarray(cmat), selc


def _masks(r):
    m = np.zeros((128, 2, 4, 132), np.float32)
    s = np.arange(128)[:, None]
    t = np.arange(128)[None, :]
    for kind in range(2):
        for a in range(4):
            if r > a:
                mm_ = np.zeros((128, 128), np.float32)
            elif r == a:
                ok = (s <= t) if kind == 0 else (s < t)
                mm_ = np.where(ok, 0.0, NEG).astype(np.float32)
            else:
                mm_ = np.full((128, 128), NEG, np.float32)
            m[:, kind, a, 0:128] = mm_
            for ii in range(2):
                hb = 4 * ii + r - 1
                for c in range(2):
                    pos = 126 + c
                    if hb > a:
                        col = np.zeros(128, np.float32)
                    elif hb == a:
                        ok = (np.arange(128) <= pos) if kind == 0 else (np.arange(128) < pos)
                        col = np.where(ok, 0.0, NEG).astype(np.float32)
                    else:
                        col = np.full(128, NEG, np.float32)
                    m[:, kind, a, 128 + 2 * ii + c] = col
    return m


def kernel(x, mem, attn_norm_g, w_in, b_forget, fox_out_g, sb_out_g, w_out, xattn_norm_g, mem_norm_g,
           w_mq, w_mkv, w_mo, ffn_norm_g, w_up, conv_w, conv_b, w_down, final_norm_g):
    f32 = np.float32
    x = np.asarray(x, f32)
    mem = np.asarray(mem, f32)

    def fm(w):
        w = np.asarray(w, f32)
        k, n = w.shape
        return np.ascontiguousarray(w.reshape(k // 128, 128, n).transpose(1, 0, 2))

    w_in = np.asarray(w_in, f32)[0]
    wq = []
    for p in range(4):
        base = 0 if p < 2 else 1536
        h0 = 4 * (p % 2)
        cols = np.concatenate([np.arange(base + 512 * t + 64 * h0, base + 512 * t + 64 * h0 + 256) for t in range(3)])
        wq.append(fm(w_in[:, cols]))
    wqkv = np.ascontiguousarray(np.stack(wq, 0))
    wf = fm(w_in[:, 3072:3080])

    def gv(g):
        return np.asarray(g, f32).reshape(8, 128).T

    gall = np.ascontiguousarray(np.stack([gv(attn_norm_g[0]), gv(xattn_norm_g[0]), gv(mem_norm_g[0]),
                                          gv(ffn_norm_g[0]), gv(final_norm_g)], axis=1))
    bfg = np.ascontiguousarray(np.broadcast_to(np.asarray(b_forget, f32)[0][None, :], (128, 8)))
    gout = np.ascontiguousarray(gv(np.concatenate([np.asarray(fox_out_g, f32)[0], np.asarray(sb_out_g, f32)[0]])))
    common = dict(
        wqkv=wqkv, wf=wf, gall=gall, bfg=bfg, gout=gout,
        wout=fm(w_out[0]), wmq=fm(w_mq[0]), wmkv=fm(w_mkv[0]), wmo=fm(w_mo[0]), wup=fm(w_up[0]),
        convw=np.ascontiguousarray(np.asarray(conv_w, f32)[0].reshape(3, 44, 128).transpose(2, 1, 0)),
        convb=np.ascontiguousarray(np.asarray(conv_b, f32)[0].reshape(44, 128).T),
        wdown=fm(w_down[0]),
    )
    cmat, selc = _consts()
    common["cmat"] = cmat
    common["selc"] = selc

    xT = [fm(x[b]) for b in range(2)]
    xT = [fm(np.ascontiguousarray(x[b].T)) for b in range(2)]
    mT = [fm(np.ascontiguousarray(mem[b].T)) for b in range(2)]
    in_maps = []
    for c in range(8):
        b, r = c // 4, c % 4
        xo = np.zeros((128, 8, NOWN), f32)
        for i in range(16):
            t0 = 128 * (4 * i + r)
            xo[:, :, 128 * i:128 * i + 128] = xT[b][:, :, t0:t0 + 128]
            if t0 >= 2:
                xo[:, :, 2048 + 2 * i:2048 + 2 * i + 2] = xT[b][:, :, t0 - 2:t0]
        es_ = np.zeros((128, 4, 128), f32)
        es_[127, r, :] = -1.0
        hv = np.ones((128, 32), f32)
        if r == 0:
            hv[:, 0:2] = 0.0
        d = dict(common)
        d.update(xfT=xT[b], xoT=xo, memT=mT[b], masks=_masks(r), esel=es_, hvalid=hv)
        in_maps.append(d)

    if "nc" not in _CACHE:
        _CACHE["nc"] = build_program()
    res = run_bass_kernel_spmd(_CACHE["nc"], in_maps, core_ids=list(range(8)))
    out = np.zeros((2, S, 1024), f32)
    for c in range(8):
        b, r = c // 4, c % 4
        o = res.results[c]["outT"]
        o = o.transpose(2, 1, 0).reshape(2048, 1024)
        for i in range(16):
            t0 = 128 * (4 * i + r)
            out[b, t0:t0 + 128, :] = o[128 * i:128 * i + 128]
    return out
```
